# Optimizing a Trainium2 kernel written in Bass

```python
import jax
import jax.numpy as jnp
from jax import lax

D_MODEL = 1024
BATCH = 2
SEQ = 8192
DEPTH = 1

HEAD_DIM = 64
NSA_HEADS = 8
NSA_KV = 2
SWA_HEADS = 8
SWA_KV = 1
CMP_STRIDE = 16
CMP_LEN = 2 * CMP_STRIDE
CMP_HIDDEN = 256
SEL_LEN = 64
SEL_TOPN = 16
SEL_LOCAL = 2
NSA_WINDOW = 512
SWA_WINDOW = 128
Q_BLOCK = 128
D_FF = 2816
ROPE_THETA = 10000.0
RMS_EPS = 1e-6
FFN_HALF = 0.5
NEG_INF = -1e30
FORCE = 1e9

NSA_Q_W = NSA_HEADS * HEAD_DIM
NSA_KV_W = NSA_KV * HEAD_DIM
NSA_GATE_W = 3 * NSA_HEADS
SWA_Q_W = SWA_HEADS * HEAD_DIM
SWA_KV_W = SWA_KV * HEAD_DIM
IN_WIDTHS = (NSA_Q_W, NSA_KV_W, NSA_KV_W, NSA_KV_W, NSA_KV_W, NSA_KV_W, NSA_KV_W, NSA_GATE_W,
             SWA_Q_W, SWA_KV_W, SWA_KV_W, D_MODEL, D_MODEL)

kernel_name = 'hybrid_nsa_swa_sink_macaron'


def _split_points():
    pts, acc = [], 0
    for w in IN_WIDTHS[:-1]:
        acc += w
        pts.append(acc)
    return pts


def _rmsnorm(x, g):
    xf = x.astype(jnp.float32)
    y = xf * lax.rsqrt(jnp.mean(xf * xf, axis=-1, keepdims=True) + RMS_EPS)
    return (y * g.astype(jnp.float32)).astype(x.dtype)


def _swiglu(x, w_gate, w_up, w_down):
    return (jax.nn.silu(x @ w_gate) * (x @ w_up)) @ w_down


def _rope(x, pos):
    half = HEAD_DIM // 2
    inv_freq = ROPE_THETA ** (-jnp.arange(half, dtype=jnp.float32) / half)
    ang = pos.astype(jnp.float32)[..., None] * inv_freq
    cos = jnp.cos(ang)[:, :, None, :]
    sin = jnp.sin(ang)[:, :, None, :]
    xf = x.astype(jnp.float32)
    x1, x2 = xf[..., :half], xf[..., half:]
    return jnp.concatenate([x1 * cos - x2 * sin, x2 * cos + x1 * sin], axis=-1).astype(x.dtype)


def _banded_attention(q, k, v, window, sink):
    B, S, H, D = q.shape
    G = k.shape[2]
    R = H // G
    nb = S // Q_BLOCK
    pad = -(-window // Q_BLOCK) * Q_BLOCK
    L = pad + Q_BLOCK
    kp = jnp.pad(k, ((0, 0), (pad, 0), (0, 0), (0, 0)))
    vp = jnp.pad(v, ((0, 0), (pad, 0), (0, 0), (0, 0)))
    idx = jnp.arange(nb)[:, None] * Q_BLOCK + jnp.arange(L)[None, :]
    kb = kp[:, idx]
    vb = vp[:, idx]
    qb = q.reshape(B, nb, Q_BLOCK, G, R, D)
    s = jnp.einsum('bnqgrd,bnkgd->bngrqk', qb, kb).astype(jnp.float32) * (D ** -0.5)
    qpos = jnp.arange(nb)[:, None] * Q_BLOCK + jnp.arange(Q_BLOCK)[None, :]
    kpos = idx - pad
    diff = qpos[:, :, None] - kpos[:, None, :]
    mask = (diff >= 0) & (diff < window) & (kpos[:, None, :] >= 0)
    s = jnp.where(mask[None, :, None, None], s, NEG_INF)
    if sink is None:
        p = jax.nn.softmax(s, axis=-1)
    else:
        sk = sink.astype(jnp.float32).reshape(G, R)[None, None, :, :, None, None]
        m = jnp.maximum(jnp.max(s, axis=-1, keepdims=True), sk)
        e = jnp.exp(s - m)
        p = e / (jnp.sum(e, axis=-1, keepdims=True) + jnp.exp(sk - m))
    o = jnp.einsum('bngrqk,bnkgd->bnqgrd', p.astype(vb.dtype), vb)
    return o.reshape(B, S, H, D)


def _compress(kv, pe, w1, w2):
    B, S, G, D = kv.shape
    c = kv.reshape(B, S // CMP_STRIDE, CMP_STRIDE, G, D)
    blocks = jnp.concatenate([c[:, :-1], c[:, 1:]], axis=2)
    blocks = blocks + pe[None, None, :, None, :]
    n_cmp = blocks.shape[1]
    flat = blocks.transpose(0, 1, 3, 2, 4).reshape(B, n_cmp, G, CMP_LEN * D)
    return jax.nn.gelu(flat @ w1) @ w2


def _nsa(q, k_cmp, v_cmp, k_slc, v_slc, k_win, v_win, gates, pos, pe_k, wk1, wk2, pe_v, wv1, wv2):
    B, S, H, D = q.shape
    G = NSA_KV
    R = H // G
    scale = D ** -0.5
    n_sel = S // SEL_LEN
    n_top = min(SEL_TOPN, n_sel)
    nb = S // Q_BLOCK
    qg = q.reshape(B, S, G, R, D)
    t = jnp.arange(S)

    kc = _compress(k_cmp, pe_k, wk1, wk2)
    vc = _compress(v_cmp, pe_v, wv1, wv2)
    n_cmp = kc.shape[1]
    kc = _rope(kc, pos[:, CMP_LEN - 1::CMP_STRIDE])
    end = jnp.arange(n_cmp) * CMP_STRIDE + CMP_LEN - 1
    vis = end[None, :] <= t[:, None]
    s = jnp.einsum('bsgrd,bcgd->bgrsc', qg, kc).astype(jnp.float32) * scale
    s = jnp.where(vis, s, NEG_INF)
    p_cmp = jnp.where(vis, jax.nn.softmax(s, axis=-1), 0.0)
    o_cmp = jnp.einsum('bgrsc,bcgd->bsgrd', p_cmp.astype(vc.dtype), vc)

    cs = jnp.arange(n_cmp) * CMP_STRIDE
    ss = jnp.arange(n_sel) * SEL_LEN
    overlap = jnp.clip(jnp.minimum(cs[:, None] + CMP_LEN, ss[None, :] + SEL_LEN)
                       - jnp.maximum(cs[:, None], ss[None, :]), 0, None).astype(jnp.float32) / CMP_LEN
    imp = jnp.einsum('bgrsc,cj->bgsj', p_cmp, overlap)
    blk = jnp.arange(n_sel)[None, :]
    tb = (t // SEL_LEN)[:, None]
    forced = (blk == 0) | ((tb - blk >= 0) & (tb - blk < SEL_LOCAL))
    imp = jnp.where(forced, FORCE, jnp.where(blk > tb, -FORCE, imp))
    _, sel = lax.top_k(imp, n_top)

    kT = k_slc.transpose(0, 2, 1, 3)
    vT = v_slc.transpose(0, 2, 1, 3)
    offs = jnp.arange(SEL_LEN)
    gather = jax.vmap(jax.vmap(lambda a, i: a[i]))

    def one_block(args):
        qb, selb, tq = args
        idx = (selb[..., None] * SEL_LEN + offs).reshape(B, G, Q_BLOCK, n_top * SEL_LEN)
        kb = gather(kT, idx)
        vb = gather(vT, idx)
        sc = jnp.einsum('bqgrd,bgqtd->bgrqt', qb, kb).astype(jnp.float32) * scale
        m = (idx <= tq[None, None, :, None])[:, :, None]
        pr = jax.nn.softmax(jnp.where(m, sc, NEG_INF), axis=-1)
        return jnp.einsum('bgrqt,bgqtd->bqgrd', pr.astype(vb.dtype), vb)

    qs = qg.reshape(B, nb, Q_BLOCK, G, R, D).transpose(1, 0, 2, 3, 4, 5)
    sels = sel.reshape(B, G, nb, Q_BLOCK, n_top).transpose(2, 0, 1, 3, 4)
    tqs = t.reshape(nb, Q_BLOCK)
    o_slc = lax.map(one_block, (qs, sels, tqs))
    o_slc = o_slc.transpose(1, 0, 2, 3, 4, 5).reshape(B, S, G, R, D)

    o_win = _banded_attention(q, k_win, v_win, NSA_WINDOW, None).reshape(B, S, G, R, D)

    g = jax.nn.sigmoid(gates.astype(jnp.float32)).reshape(B, S, 3, G, R, 1)
    o = (g[:, :, 0] * o_cmp.astype(jnp.float32) + g[:, :, 1] * o_slc.astype(jnp.float32)
         + g[:, :, 2] * o_win.astype(jnp.float32))
    return o.astype(q.dtype).reshape(B, S, H * D)


def setup_inputs(seed: int = 0) -> dict:
    key = jax.random.key(seed)
    ks = jax.random.split(key, 24)
    f32 = jnp.float32

    def w(k, shape, fan_in):
        return jax.random.normal(k, shape, f32) * fan_in ** -0.5

    def gain(k, shape):
        return 1.0 + 0.02 * jax.random.normal(k, shape, f32)

    w_in_width = sum(IN_WIDTHS)
    return {
        'x': jax.random.normal(ks[0], (BATCH, SEQ, D_MODEL), f32),
        'positions': jnp.broadcast_to(jnp.arange(SEQ, dtype=jnp.int32), (BATCH, SEQ)),
        'norm_ffn1': gain(ks[1], (DEPTH, D_MODEL)),
        'ffn1_gate': w(ks[2], (DEPTH, D_MODEL, D_FF), D_MODEL),
        'ffn1_up': w(ks[3], (DEPTH, D_MODEL, D_FF), D_MODEL),
        'ffn1_down': w(ks[4], (DEPTH, D_FF, D_MODEL), D_FF),
        'norm_mix': gain(ks[5], (DEPTH, D_MODEL)),
        'w_in': w(ks[6], (DEPTH, D_MODEL, w_in_width), D_MODEL),
        'cmp_pe_k': 0.1 * jax.random.normal(ks[7], (DEPTH, CMP_LEN, HEAD_DIM), f32),
        'cmp_k_w1': w(ks[8], (DEPTH, CMP_LEN * HEAD_DIM, CMP_HIDDEN), CMP_LEN * HEAD_DIM),
        'cmp_k_w2': w(ks[9], (DEPTH, CMP_HIDDEN, HEAD_DIM), CMP_HIDDEN),
        'cmp_pe_v': 0.1 * jax.random.normal(ks[10], (DEPTH, CMP_LEN, HEAD_DIM), f32),
        'cmp_v_w1': w(ks[11], (DEPTH, CMP_LEN * HEAD_DIM, CMP_HIDDEN), CMP_LEN * HEAD_DIM),
        'cmp_v_w2': w(ks[12], (DEPTH, CMP_HIDDEN, HEAD_DIM), CMP_HIDDEN),
        'swa_sinks': jax.random.normal(ks[13], (DEPTH, SWA_HEADS), f32),
        'w_branch_a': w(ks[14], (DEPTH, NSA_Q_W, D_MODEL), NSA_Q_W),
        'w_branch_b': w(ks[15], (DEPTH, SWA_Q_W, D_MODEL), SWA_Q_W),
        'w_out': w(ks[16], (DEPTH, D_MODEL, D_MODEL), D_MODEL),
        'norm_ffn2': gain(ks[17], (DEPTH, D_MODEL)),
        'ffn2_gate': w(ks[18], (DEPTH, D_MODEL, D_FF), D_MODEL),
        'ffn2_up': w(ks[19], (DEPTH, D_MODEL, D_FF), D_MODEL),
        'ffn2_down': w(ks[20], (DEPTH, D_FF, D_MODEL), D_FF),
        'norm_final': gain(ks[21], (D_MODEL,)),
    }


def reference(x, positions, norm_ffn1, ffn1_gate, ffn1_up, ffn1_down, norm_mix, w_in,
              cmp_pe_k, cmp_k_w1, cmp_k_w2, cmp_pe_v, cmp_v_w1, cmp_v_w2, swa_sinks,
              w_branch_a, w_branch_b, w_out, norm_ffn2, ffn2_gate, ffn2_up, ffn2_down, norm_final):
    B, S, _ = x.shape
    pts = _split_points()
    h = x
    for i in range(DEPTH):
        h = h + FFN_HALF * _swiglu(_rmsnorm(h, norm_ffn1[i]), ffn1_gate[i], ffn1_up[i], ffn1_down[i])

        u = _rmsnorm(h, norm_mix[i])
        (nq, kc, vc, ksl, vsl, kwn, vwn, ng, sq, sk, sv, ga, gb) = jnp.split(u @ w_in[i], pts, axis=-1)
        nq = _rope(nq.reshape(B, S, NSA_HEADS, HEAD_DIM), positions)
        kc = kc.reshape(B, S, NSA_KV, HEAD_DIM)
        vc = vc.reshape(B, S, NSA_KV, HEAD_DIM)
        ksl = _rope(ksl.reshape(B, S, NSA_KV, HEAD_DIM), positions)
        vsl = vsl.reshape(B, S, NSA_KV, HEAD_DIM)
        kwn = _rope(kwn.reshape(B, S, NSA_KV, HEAD_DIM), positions)
        vwn = vwn.reshape(B, S, NSA_KV, HEAD_DIM)
        o_a = _nsa(nq, kc, vc, ksl, vsl, kwn, vwn, ng, positions,
                   cmp_pe_k[i], cmp_k_w1[i], cmp_k_w2[i], cmp_pe_v[i], cmp_v_w1[i], cmp_v_w2[i])

        sq = _rope(sq.reshape(B, S, SWA_HEADS, HEAD_DIM), positions)
        sk = _rope(sk.reshape(B, S, SWA_KV, HEAD_DIM), positions)
        sv = sv.reshape(B, S, SWA_KV, HEAD_DIM)
        o_b = _banded_attention(sq, sk, sv, SWA_WINDOW, swa_sinks[i]).reshape(B, S, SWA_Q_W)

        merged = jax.nn.sigmoid(ga) * (o_a @ w_branch_a[i]) + jax.nn.sigmoid(gb) * (o_b @ w_branch_b[i])
        h = h + merged @ w_out[i]

        h = h + FFN_HALF * _swiglu(_rmsnorm(h, norm_ffn2[i]), ffn2_gate[i], ffn2_up[i], ffn2_down[i])
    return _rmsnorm(h, norm_final)
```

```python
import contextlib
import numpy as np
import concourse.bass as bass
import concourse.mybir as mybir
from concourse.bass_utils import run_bass_kernel_spmd

F32 = mybir.dt.float32
BF16 = mybir.dt.bfloat16
I32 = mybir.dt.int32
AF = mybir.ActivationFunctionType
ALU = mybir.AluOpType
AX = mybir.AxisListType

D = 1024
DFF = 2816
NF = DFF // 128
NT = 64
NOWN = 16
WIN_W = 3992
EPS = 1e-6
FORCE = 1e9
NEGB = -30000.0
SEG = 8000
ARENA_WORDS = 53200


class Buf:
    __slots__ = ("name", "writers", "readers", "dsem", "dcount", "shadow", "excl")

    def __init__(self, name="", excl=False):
        self.name = name
        self.excl = excl
        self.writers = []
        self.readers = []
        self.dsem = None
        self.dcount = 0
        self.shadow = None


class Op:
    __slots__ = ("eng", "fn", "waits", "idx", "dma", "dbuf", "dval")

    def __init__(self, eng, fn, dma):
        self.eng = eng
        self.fn = fn
        self.dma = dma
        self.waits = []
        self.idx = -1
        self.dbuf = None
        self.dval = 0


class Sched:
    ENGS = ("pe", "act", "dve", "pool", "sp")

    def __init__(self, nc):
        self.nc = nc
        self.ops = {e: [] for e in self.ENGS}
        self.seen = {e: {} for e in self.ENGS}
        self.dbufs = []
        self.cnt = {e: 0 for e in self.ENGS}
        self.last = {e: None for e in self.ENGS}

    def _need(self, op, dep):
        if dep.dma:
            b = dep.dbuf
            return (("d", id(b)), b, b.dcount * 16)
        if dep.eng == op.eng and dep.eng == "pe":
            return None
        seg = dep.idx // SEG
        return (("e", dep.eng, seg), None, dep.idx % SEG + 1)

    def op(self, eng, fn, reads=(), writes=(), pwrites=(), dma=False, dbuf=None, extra=()):
        o = Op(eng, fn, dma)
        deps = list(extra)
        for b in reads:
            deps.extend(b.writers)
            if b.excl:
                deps.extend(r for r in b.readers if r.eng != eng)
        for b in pwrites:
            deps.extend(b.readers)
            if b.writers:
                deps.append(b.writers[0])
        for b in writes:
            deps.extend(b.writers)
            deps.extend(b.readers)
        if dma:
            if eng == "pool":
                if dbuf.shadow is None:
                    dbuf.shadow = Buf(dbuf.name + "_sw")
                dbuf = dbuf.shadow
            if dbuf.dsem is None:
                dbuf.dsem = True
                self.dbufs.append(dbuf)
            dbuf.dcount += 1
            o.dbuf = dbuf
            o.dval = dbuf.dcount * 16
        else:
            o.idx = self.cnt[eng]
            self.cnt[eng] += 1
            self.last[eng] = o
        seen = self.seen[eng]
        need = {}
        for d in deps:
            if d is o:
                continue
            r = self._need(o, d)
            if r is None:
                continue
            key, b, val = r
            if dma and d.dma and d.dbuf is dbuf and val == o.dval:
                val -= 16
                if val <= 0:
                    continue
            if key not in need or need[key][1] < val:
                need[key] = (b, val)
        for key, (b, val) in need.items():
            if seen.get(key, 0) >= val:
                continue
            seen[key] = val
            o.waits.append((key, b, val))
        self.ops[eng].append(o)
        for b in reads:
            b.readers.append(o)
        for b in pwrites:
            b.writers.append(o)
        for b in writes:
            b.writers = [o]
            b.readers = []
        return o

    def barrier(self):
        lasts = [o for o in self.last.values() if o is not None]
        dmas = []
        for b in self.dbufs:
            f = Op("sp", None, True)
            f.dbuf = b
            dmas.append(f)
        for e in self.ENGS:
            self.op(e, lambda eng: eng.nop(), extra=lasts + dmas)

    def emit(self):
        nc = self.nc
        with contextlib.ExitStack() as st:
            esems = {}
            for e in self.ENGS:
                n = self.cnt[e]
                for s in range((n + SEG - 1) // SEG):
                    esems[(e, s)] = st.enter_context(nc.semaphore(f"se_{e}_{s}"))
            for i, b in enumerate(self.dbufs):
                b.dsem = st.enter_context(nc.semaphore(f"sd_{i}"))
            self.nsem = len(esems) + len(self.dbufs)
            block = st.enter_context(nc.Block())

            def run(engname, eng):
                cnt = 0
                for o in self.ops[engname]:
                    for key, b, val in o.waits:
                        if key[0] == "d":
                            eng.wait_ge(b.dsem, val)
                        else:
                            eng.wait_ge(esems[(key[1], key[2])], val)
                    ins = o.fn(eng)
                    if o.dma:
                        ins.then_inc(o.dbuf.dsem, 16)
                    else:
                        ins.then_inc(esems[(engname, cnt // SEG)], 1)
                        cnt += 1

            @block.tensor
            def _(eng):
                run("pe", eng)

            @block.scalar
            def _(eng):
                run("act", eng)

            @block.vector
            def _(eng):
                run("dve", eng)

            @block.gpsimd
            def _(eng):
                run("pool", eng)

            @block.sync
            def _(eng):
                run("sp", eng)


class Arena:
    def __init__(self, base, nwords):
        self.base = base
        self.n = nwords
        self.off = 0
        self.peak = 0

    def alloc(self, shape, dtype):
        free = int(np.prod(shape))
        words = free if dtype in (F32, I32) else (free + 1) // 2
        words = (words + 7) // 8 * 8
        assert self.off + words <= self.n, f"arena overflow {self.off}+{words}>{self.n}"
        v = self.base[:, self.off:self.off + words]
        self.off += words
        self.peak = max(self.peak, self.off)
        if dtype == BF16:
            v = v.bitcast(BF16)
        elif dtype == I32:
            v = v.bitcast(I32)
        v = v[:, 0:free]
        if len(shape) > 1:
            names = [f"a{i}" for i in range(len(shape))]
            kw = {n: int(s) for n, s in zip(names, shape)}
            v = v.rearrange(f"p ({' '.join(names)}) -> p {' '.join(names)}", **kw)
        return v


def build_program(stop_after=None, dbg=False, skip1a=False, nt1b=NT, nown=NOWN):
    nc = bass.Bass("TRN2", target_bir_lowering=False)

    def din(name, shape, dt=F32):
        return nc.dram_tensor(name, list(shape), dt, kind="ExternalInput").ap()

    xrel = din("xrel", [NT * 128, D])
    posall = din("posall", [128, NT + 4], I32)
    g_ffn1 = din("g_ffn1", [1, D])
    g_mix = din("g_mix", [1, D])
    g_ffn2 = din("g_ffn2", [1, D])
    g_fin = din("g_fin", [1, D])
    w1g = din("w1g", [D, DFF])
    w1u = din("w1u", [D, DFF])
    w1d = din("w1d", [DFF, D])
    w2g = din("w2g", [D, DFF])
    w2u = din("w2u", [D, DFF])
    w2d = din("w2d", [DFF, D])
    w_in = din("w_in", [D, WIN_W])
    pe_k = din("pe_k", [64, 32])
    pe_v = din("pe_v", [64, 32])
    ck_w1 = din("ck_w1", [2048, 256])
    ck_w2 = din("ck_w2", [256, 64])
    cv_w1 = din("cv_w1", [2048, 256])
    cv_w2 = din("cv_w2", [256, 64])
    sinks = din("sinks", [1, 8])
    w_a = din("w_a", [512, D])
    w_b = din("w_b", [512, D])
    w_o = din("w_o", [D, D])
    ident_d = din("ident", [128, 128])
    invf2_d = din("invf2", [1, 64])
    phase_d = din("phase", [1, 64])
    eall_d = din("eall", [64, NT * 128])
    cmask_d = din("cmask", [128, 2, 128])
    kvalid_d = din("kvalid", [128, NT])
    cvalid_d = din("cvalid", [128, 4])
    cmpmT_d = din("cmpmT", [NOWN, 128, 128])
    cmpm_d = din("cmpm", [NOWN, 128, 512])
    keep_d = din("keepm", [NOWN, 128, 128])
    addm_d = din("addm", [NOWN, 128, 128])

    out_d = nc.dram_tensor("out", [NOWN * 128, D], F32, kind="ExternalOutput").ap()
    hscr = nc.dram_tensor("hscr", [NT * 128, D], F32, kind="Internal").ap()
    h2scr = nc.dram_tensor("h2scr", [NOWN * 128, D], F32, kind="Internal").ap()
    qscr = nc.dram_tensor("qscr", [NOWN, 128, 2048], BF16, kind="Internal").ap()
    gabscr = nc.dram_tensor("gabscr", [NOWN, 128, 2048], BF16, kind="Internal").ap()
    ngscr = nc.dram_tensor("ngscr", [NOWN, 128, 24], F32, kind="Internal").ap()
    dbg_d = None
    if dbg:
        dbg_d = nc.dram_tensor("dbg", [128, 16384], F32, kind="ExternalOutput").ap()

    st = contextlib.ExitStack()
    with st:
        arena_t = st.enter_context(nc.sbuf_tensor("arena", [128, ARENA_WORDS], F32))
        A = Arena(arena_t[:], ARENA_WORDS)
        psum_all = st.enter_context(nc.psum_tensor("psum_all", [128, 4096], F32))[:]
        PB = [psum_all[:, i * 512:(i + 1) * 512] for i in range(8)]
        PBb = [Buf(f"bank{i}", excl=True) for i in range(8)]
        S = Sched(nc)

        def MM(out, lhsT, rhs, start, stop, reads, wb):
            if start:
                S.op("pe", lambda e: e.matmul(out, lhsT=lhsT, rhs=rhs, start=True, stop=stop),
                     reads=reads, writes=[wb])
            else:
                S.op("pe", lambda e: e.matmul(out, lhsT=lhsT, rhs=rhs, start=False, stop=stop),
                     reads=reads, pwrites=[wb])

        def MMP(out, lhsT, rhs, start, stop, reads, wb, first):
            if first:
                S.op("pe", lambda e: e.matmul(out, lhsT=lhsT, rhs=rhs, start=start, stop=stop),
                     reads=reads, writes=[wb])
            else:
                S.op("pe", lambda e: e.matmul(out, lhsT=lhsT, rhs=rhs, start=start, stop=stop),
                     reads=reads, pwrites=[wb])

        def TR(out, in_, ident, reads, wb, first):
            if first:
                S.op("pe", lambda e: e.transpose(out, in_, ident), reads=reads, writes=[wb])
            else:
                S.op("pe", lambda e: e.transpose(out, in_, ident), reads=reads, pwrites=[wb])

        def ACT(out, in_, func, reads, writes, bias=None, scale=None, accum=None, pwrites=()):
            kw = {}
            if bias is not None:
                kw["bias"] = bias
            if scale is not None:
                kw["scale"] = scale
            if accum is not None:
                kw["accum_out"] = accum
            S.op("act", lambda e: e.activation(out, in_, func, **kw), reads=reads, writes=writes,
                 pwrites=pwrites)

        def ENG(eng, name, reads, writes, *args, pwrites=(), **kw):
            S.op(eng, lambda e: getattr(e, name)(*args, **kw), reads=reads, writes=writes,
                 pwrites=pwrites)

        def DVE(name, reads, writes, *args, pwrites=(), **kw):
            ENG("dve", name, reads, writes, *args, pwrites=pwrites, **kw)

        def POOL(name, reads, writes, *args, pwrites=(), **kw):
            ENG("pool", name, reads, writes, *args, pwrites=pwrites, **kw)

        def DMA(q, out, in_, reads=(), writes=(), pwrites=(), dbuf=None):
            S.op(q, lambda e: e.dma_start(out=out, in_=in_), reads=reads, writes=writes,
                 pwrites=pwrites, dma=True, dbuf=dbuf)

        dbg_off = [0]
        dbg_buf = Buf("dbg")

        def DBG(ap_sb, rbuf, ncols, parts=128):
            if not dbg:
                return
            o = dbg_off[0]
            DMA("sp", dbg_d[0:parts, o:o + ncols], ap_sb, reads=[rbuf], pwrites=[dbg_buf], dbuf=dbg_buf)
            dbg_off[0] = o + ncols

        def early(name, tile_ap=None, rbuf=None):
            if stop_after != name:
                return False
            b_o = Buf("out")
            if tile_ap is not None:
                DMA("sp", out_d[0:128, 0:tile_ap.shape[1]], tile_ap, reads=[rbuf], pwrites=[b_o], dbuf=b_o)
                S.op("sp", lambda e: e.nop(), reads=[b_o])
            S.barrier()
            S.emit()
            return True

        ident_f = A.alloc([128], F32)
        ident_b = A.alloc([128], BF16)
        b_const = Buf("const")
        DMA("sp", ident_f, ident_d[:, :], pwrites=[b_const], dbuf=b_const)
        DMA("pool", ident_b, ident_d[:, :], pwrites=[b_const], dbuf=b_const)
        ones_f = A.alloc([128], F32)
        DVE("memset", [], [], ones_f, 1.0, pwrites=[b_const])
        stats = A.alloc([64], F32)
        base_mark = A.off

        def rmsnorm_to_bf16(x_sb, xbuf, gam, gbuf, out_bf, obuf, junk, jbuf, sc, scbuf):
            ACT(junk, x_sb, AF.Square, [xbuf], [jbuf, scbuf], accum=sc[:, 0:1])
            DVE("tensor_scalar", [scbuf], [scbuf], sc[:, 1:2], sc[:, 0:1], 1.0 / D, EPS, ALU.mult, ALU.add)
            ACT(sc[:, 2:3], sc[:, 1:2], AF.Sqrt, [scbuf], [scbuf])
            DVE("reciprocal", [scbuf], [scbuf], sc[:, 3:4], sc[:, 2:3])
            DVE("scalar_tensor_tensor", [xbuf, scbuf, gbuf], [obuf], out_bf, x_sb, sc[:, 3:4], gam,
                ALU.mult, ALU.mult)

        def transpose_to(dst3, src_bf, sbuf_, nblk, bank, dstbuf, evac="act"):
            pbv = PB[bank].bitcast(BF16)
            for b in range(nblk):
                TR(pbv[:, b * 128:(b + 1) * 128], src_bf[:, b * 128:(b + 1) * 128], ident_b,
                   [sbuf_, b_const], PBb[bank], b == 0)
            src = pbv[:, 0:nblk * 128].rearrange("p (b t) -> p b t", b=nblk)
            if evac == "act":
                ACT(dst3, src, AF.Copy, [PBb[bank]], [dstbuf])
            else:
                DVE("tensor_copy", [PBb[bank]], [dstbuf], dst3, src)

        def ffn_phase(ntiles, src_d, gam_d, wg_d, wu_d, wd_d, finish):
            m0 = A.off
            Wg = A.alloc([8, DFF], BF16)
            Wu = A.alloc([8, DFF], BF16)
            Wd = A.alloc([NF, D], BF16)
            xnT = A.alloc([8, 512], BF16)
            aT = A.alloc([NF, 512], BF16)
            xs = [A.alloc([D], F32) for _ in range(4)]
            xr = [A.alloc([D], F32) for _ in range(2)]
            sg = [A.alloc([512], BF16) for _ in range(2)]
            gam = A.alloc([D], F32)
            xn = [A.alloc([D], BF16) for _ in range(4)]
            sc = [A.alloc([4], F32) for _ in range(4)]
            b_gam = Buf()
            DMA("sp", gam, gam_d[0:1, :].broadcast_to([128, D]), writes=[b_gam], dbuf=b_gam)
            fparts = [(0, 2), (2, 6), (6, 14), (14, 22)]
            b_wg = [Buf() for _ in fparts]
            b_wu = [Buf() for _ in fparts]
            wgs = wg_d.rearrange("(kc p) n -> p kc n", p=128)
            wus = wu_d.rearrange("(kc p) n -> p kc n", p=128)
            for i, (f0, f1) in enumerate(fparts):
                DMA("pool", Wg[:, :, f0 * 128:f1 * 128], wgs[:, :, f0 * 128:f1 * 128], writes=[b_wg[i]], dbuf=b_wg[i])
                DMA("pool", Wu[:, :, f0 * 128:f1 * 128], wus[:, :, f0 * 128:f1 * 128], writes=[b_wu[i]], dbuf=b_wu[i])
            wds = wd_d.rearrange("(f p) n -> p f n", p=128)
            dparts = [(0, 6), (6, 14), (14, 22)]
            b_wd = [Buf() for _ in dparts]
            for i, (f0, f1) in enumerate(dparts):
                DMA("pool", Wd[:, f0:f1, :], wds[:, f0:f1, :], writes=[b_wd[i]], dbuf=b_wd[i])

            def fpart(f, parts):
                for i, (f0, f1) in enumerate(parts):
                    if f0 <= f < f1:
                        return i
                raise ValueError

            b_xs = [Buf() for _ in range(4)]
            b_xr = [Buf() for _ in range(2)]
            b_xn = [Buf() for _ in range(4)]
            b_sc = [Buf() for _ in range(4)]
            b_xnT = Buf()
            b_aT = [Buf() for _ in range(NF)]
            b_sg = [Buf() for _ in range(2)]
            ngroups = ntiles // 4

            def norm_part(G):
                for j in range(4):
                    t = 4 * G + j
                    DMA("sp", xs[j], src_d[t * 128:(t + 1) * 128, :], writes=[b_xs[j]], dbuf=b_xs[j])
                    rmsnorm_to_bf16(xs[j], b_xs[j], gam, b_gam, xn[j], b_xn[j], xn[j], b_xn[j], sc[j], b_sc[j])

            def tr_part(G):
                for j in range(4):
                    transpose_to(xnT[:, :, j * 128:(j + 1) * 128], xn[j], b_xn[j], 8, 6, b_xnT, evac="act")

            cnt = [0]

            def gateup(G):
                for f in range(NF):
                    if f == 6 and G + 1 < ngroups:
                        norm_part(G + 1)
                    pg, pu = (0, 1) if f % 2 == 0 else (2, 3)
                    ig = fpart(f, fparts)
                    for kc in range(8):
                        MM(PB[pg], Wg[:, kc, f * 128:(f + 1) * 128], xnT[:, kc, :], kc == 0, kc == 7,
                           [b_wg[ig], b_xnT], PBb[pg])
                    for kc in range(8):
                        MM(PB[pu], Wu[:, kc, f * 128:(f + 1) * 128], xnT[:, kc, :], kc == 0, kc == 7,
                           [b_wu[ig], b_xnT], PBb[pu])
                    s = f % 2
                    ACT(sg[s], PB[pg], AF.Silu, [PBb[pg]], [b_sg[s]])
                    DVE("tensor_tensor", [b_sg[s], PBb[pu]], [b_aT[f]], aT[:, f, :], sg[s], PB[pu], ALU.mult)

            def down(G):
                for j in range(4):
                    t = 4 * G + j
                    r = t % 2
                    DMA("sp", xr[r], src_d[t * 128:(t + 1) * 128, :], writes=[b_xr[r]], dbuf=b_xr[r])
                    bk0 = 4 if j % 2 == 0 else 6
                    for f in range(NF):
                        idp = fpart(f, dparts)
                        for hf in range(2):
                            bk = bk0 + hf
                            MM(PB[bk], aT[:, f, j * 128:(j + 1) * 128], Wd[:, f, hf * 512:(hf + 1) * 512],
                               f == 0, f == NF - 1, [b_aT[f], b_wd[idp]], PBb[bk])
                    for hf in range(2):
                        bk = bk0 + hf
                        DVE("scalar_tensor_tensor", [PBb[bk], b_xr[r]], [], xr[r][:, hf * 512:(hf + 1) * 512],
                            PB[bk], 0.5, xr[r][:, hf * 512:(hf + 1) * 512], ALU.mult, ALU.add,
                            pwrites=[b_xr[r]])
                    finish(t, xr[r], b_xr[r])

            norm_part(0)
            tr_part(0)
            for G in range(ngroups):
                gateup(G)
                if G + 1 < ngroups:
                    tr_part(G + 1)
                down(G)
            S.barrier()
            A.off = m0

        b_hscr = Buf("hscr")

        def fin1(t, h_sb, hbuf):
            DMA("pool", hscr[t * 128:(t + 1) * 128, :], h_sb, reads=[hbuf], pwrites=[b_hscr], dbuf=b_hscr)

        if skip1a:
            hscr = xrel
        else:
            ffn_phase(nt1b, xrel, g_ffn1, w1g, w1u, w1d, fin1)

        if stop_after == "1a":
            tmp = A.alloc([D], F32)
            b_tmp = Buf()
            b_out = Buf("out")
            for t in range(NOWN):
                DMA("sp", tmp, hscr[t * 128:(t + 1) * 128, :], reads=[b_hscr], writes=[b_tmp], dbuf=b_tmp)
                DMA("sp", out_d[t * 128:(t + 1) * 128, :], tmp, reads=[b_tmp], pwrites=[b_out], dbuf=b_out)
            S.op("sp", lambda e: e.nop(), reads=[b_out, dbg_buf])
            S.emit()
            return nc, S, A


        KT_all = A.alloc([3, NT * 128], BF16)
        KT_slc, KT_win, KT_swa = KT_all[:, 0, :], KT_all[:, 1, :], KT_all[:, 2, :]
        V1_slc = A.alloc([NT, 2, 65], BF16)
        V1_win = A.alloc([NT, 2, 65], BF16)
        V1_swa = A.alloc([NT, 65], BF16)
        KcT = A.alloc([512], BF16)
        V1c = A.alloc([4, 2, 65], BF16)
        kvalid = A.alloc([NT], F32)
        cvalid = A.alloc([4], F32)
        b_KV = Buf("kv")
        b_tab = Buf("tab")
        DMA("sp", kvalid, kvalid_d[:, :], pwrites=[b_tab], dbuf=b_tab)
        DMA("sp", cvalid, cvalid_d[:, :], pwrites=[b_tab], dbuf=b_tab)
        POOL("memset", [], [], V1_slc[:, :, :, 64:65], 1.0, pwrites=[b_KV])
        POOL("memset", [], [], V1_win[:, :, :, 64:65], 1.0, pwrites=[b_KV])
        POOL("memset", [], [], V1_swa[:, :, 64:65], 1.0, pwrites=[b_KV])
        POOL("memset", [], [], V1c[:, :, :, 64:65], 1.0, pwrites=[b_KV])
        if early("res", kvalid, b_tab):
            return nc, S, A
        m_res = A.off
        sincos_all = A.alloc([NT + 4, 64], F32)
        sincos = sincos_all[:, 0:NT, :]
        sincos_c = sincos_all[:, NT:NT + 4, :]
        b_sc_tab = Buf("sincos")
        m_sincos = A.off

        TWO_PI = float(2 * np.pi)
        C1 = 6.28125
        C2 = float(2 * np.pi - 6.28125)

        def make_sincos(dst, pos_d, n):
            m0 = A.off
            posi = A.alloc([n], I32)
            posf = A.alloc([n], F32)
            invf2 = A.alloc([64], F32)
            ph = A.alloc([64], F32)
            ang = A.alloc([n, 64], F32)
            ki = A.alloc([n, 64], I32)
            kf = A.alloc([n, 64], F32)
            bt = Buf()
            DMA("sp", posi, pos_d[:, :], pwrites=[bt], dbuf=bt)
            DMA("sp", invf2, invf2_d[0:1, :].broadcast_to([128, 64]), pwrites=[bt], dbuf=bt)
            DMA("sp", ph, phase_d[0:1, :].broadcast_to([128, 64]), pwrites=[bt], dbuf=bt)
            bw = Buf()
            DVE("tensor_copy", [bt], [bw], posf, posi)
            DVE("tensor_tensor", [bt, bw], [bw], ang, invf2.unsqueeze(1).broadcast_to([128, n, 64]),
                posf.unsqueeze(2).broadcast_to([128, n, 64]), ALU.mult)
            DVE("tensor_tensor", [bt, bw], [bw], ang, ang, ph.unsqueeze(1).broadcast_to([128, n, 64]), ALU.add)
            DVE("tensor_scalar", [bw], [bw], ki, ang, 1.0 / TWO_PI, None, ALU.mult)
            DVE("tensor_copy", [bw], [bw], kf, ki)
            DVE("scalar_tensor_tensor", [bw], [bw], ang, kf, -C1, ang, ALU.mult, ALU.add)
            DVE("scalar_tensor_tensor", [bw], [bw], ang, kf, -C2, ang, ALU.mult, ALU.add)
            DVE("tensor_scalar", [bw], [bw], kf, ang, float(np.pi), -TWO_PI, ALU.is_gt, ALU.mult)
            DVE("tensor_tensor", [bw], [bw], ang, ang, kf, ALU.add)
            DVE("tensor_scalar", [bw], [bw], kf, ang, float(-np.pi), TWO_PI, ALU.is_lt, ALU.mult)
            DVE("tensor_tensor", [bw], [bw], ang, ang, kf, ALU.add)
            DVE("tensor_scalar", [bw], [bw], ang, ang, float(np.pi), float(-np.pi), ALU.min, ALU.max)
            ACT(dst, ang, AF.Sin, [bw], [], pwrites=[b_sc_tab])
            S.barrier()
            A.off = m0

        m_pre_wkv = A.off
        Wkv = A.alloc([8, 896], BF16)
        b_wkv = Buf()
        wins = w_in.rearrange("(kc p) n -> p kc n", p=128)
        for d0, s0, wd in ((0, 768, 128), (128, 1024, 128), (256, 1816, 64), (320, 896, 128),
                           (448, 1152, 128), (576, 1880, 64), (640, 512, 256)):
            DMA("pool", Wkv[:, :, d0:d0 + wd], wins[:, :, s0:s0 + wd], pwrites=[b_wkv], dbuf=b_wkv)
        make_sincos(sincos_all, posall, NT + 4)
        if early("sincos", sincos[:, 3, :], b_sc_tab):
            return nc, S, A

        def rope(dst, src, sc_ap, nh, rbufs, wbuf, tmp, tbuf, dst_views=None):
            sin_b = sc_ap[:, 0:32].unsqueeze(1).broadcast_to([128, nh, 32])
            cos_b = sc_ap[:, 32:64].unsqueeze(1).broadcast_to([128, nh, 32])
            x1 = src[:, :, 0:32]
            x2 = src[:, :, 32:64]
            t1 = tmp[:, 0, 0:nh, :]
            t2 = tmp[:, 1, 0:nh, :]
            d1, d2 = (dst[:, :, 0:32], dst[:, :, 32:64]) if dst_views is None else dst_views
            DVE("tensor_tensor", rbufs, [tbuf], t1, x1, cos_b, ALU.mult)
            DVE("tensor_tensor", rbufs + [tbuf], [tbuf], t2, x2, sin_b, ALU.mult)
            DVE("tensor_tensor", [tbuf], [], d1, t1, t2, ALU.subtract, pwrites=[wbuf])
            DVE("tensor_tensor", rbufs + [tbuf], [tbuf], t1, x2, cos_b, ALU.mult)
            DVE("tensor_tensor", rbufs + [tbuf], [tbuf], t2, x1, sin_b, ALU.mult)
            DVE("tensor_tensor", [tbuf], [], d2, t1, t2, ALU.add, pwrites=[wbuf])

        m1b = A.off
        kvT_raw = A.alloc([2, NT * 128 + 16], BF16)
        b_kvT = Buf("kvT_raw")
        if nt1b < NT:
            POOL("memset", [], [b_kvT], kvT_raw, 0.0)
            POOL("memset", [], [b_KV], KT_all, 0.0)
            for t_ in (V1_slc, V1_win):
                POOL("memset", [], [b_KV], t_[:, :, :, 0:64], 0.0)
            POOL("memset", [], [b_KV], V1_swa[:, :, 0:64], 0.0)
        POOL("memset", [], [b_kvT], kvT_raw[:, :, NT * 128:NT * 128 + 16], 0.0)
        m1b2 = A.off
        gmix = A.alloc([D], F32)
        b_gmix = Buf()
        DMA("sp", gmix, g_mix[0:1, :].broadcast_to([128, D]), writes=[b_gmix], dbuf=b_gmix)
        hb = [A.alloc([D], F32) for _ in range(3)]
        ub = [A.alloc([D], BF16) for _ in range(3)]
        uT = [A.alloc([8, 128], BF16) for _ in range(3)]
        junk = A.alloc([D], BF16)
        scs = [A.alloc([4], F32) for _ in range(3)]
        rk = [A.alloc([6, 64], BF16) for _ in range(3)]
        rtmp = A.alloc([2, 16, 32], F32)
        kvraw = [A.alloc([256], BF16) for _ in range(3)]
        b_hb = [Buf() for _ in range(3)]
        b_ub = [Buf() for _ in range(3)]
        b_uT = [Buf() for _ in range(3)]
        b_junk = Buf()
        b_scs = [Buf() for _ in range(3)]
        b_kin = [Buf() for _ in range(2)]
        b_rk = [Buf() for _ in range(3)]
        b_rtmp = Buf()
        b_kvraw = [Buf() for _ in range(3)]

        def load_norm(tau, k, gam, gbuf):
            DMA("sp", hb[k], hscr[tau * 128:(tau + 1) * 128, :], reads=[b_hscr], writes=[b_hb[k]], dbuf=b_hb[k])
            rmsnorm_to_bf16(hb[k], b_hb[k], gam, gbuf, ub[k], b_ub[k], junk, b_junk, scs[k], b_scs[k])

        def norm_T(k):
            transpose_to(uT[k], ub[k], b_ub[k], 8, 6, b_uT[k], evac="act")

        a_sb = [A.alloc([448], F32) for _ in range(2)]
        b_sb = [A.alloc([448], F32) for _ in range(2)]
        b_asb = [Buf() for _ in range(2)]
        b_bsb = [Buf() for _ in range(2)]

        def stage1(tau):
            k = tau % 2
            k4 = tau % 3
            pa, pb = (0, 1) if k == 0 else (2, 3)
            for kc in range(8):
                MM(PB[pa][:, 0:448], uT[k4][:, kc, :], Wkv[:, kc, 0:448], kc == 0, kc == 7, [b_uT[k4], b_wkv], PBb[pa])
            for kc in range(8):
                MM(PB[pb][:, 0:448], uT[k4][:, kc, :], Wkv[:, kc, 448:896], kc == 0, kc == 7, [b_uT[k4], b_wkv], PBb[pb])
            ACT(a_sb[k], PB[pa][:, 0:448], AF.Copy, [PBb[pa]], [b_asb[k]])
            ACT(b_sb[k], PB[pb][:, 0:448], AF.Copy, [PBb[pb]], [b_bsb[k]])
            POOL("tensor_copy", [b_asb[k]], [], V1_slc[:, tau, :, 0:64],
                 a_sb[k][:, 320:448].rearrange("p (g d) -> p g d", g=2), pwrites=[b_KV])
            POOL("tensor_copy", [b_bsb[k]], [], V1_win[:, tau, :, 0:64],
                 b_sb[k][:, 0:128].rearrange("p (g d) -> p g d", g=2), pwrites=[b_KV])
            POOL("tensor_copy", [b_bsb[k]], [], V1_swa[:, tau, 0:64], b_sb[k][:, 128:192], pwrites=[b_KV])
            ACT(kvraw[k4], b_sb[k][:, 192:448], AF.Copy, [b_bsb[k]], [b_kvraw[k4]])
            rope(rk[k4][:, 0:5, :], a_sb[k][:, 0:320].rearrange("p (h d) -> p h d", h=5), sincos[:, tau, :], 5,
                 [b_asb[k], b_sc_tab], b_rk[k4], rtmp, b_rtmp)
            DVE("tensor_copy", [b_rk[k4]], [], rk[k4][:, 5, :], rk[k4][:, 4, :], pwrites=[b_rk[k4]])

        def stage2(tau):
            k = tau % 3
            pbv = PB[7].bitcast(BF16)
            rkf = rk[k].rearrange("p h d -> p (h d)")
            for bl in range(3):
                TR(pbv[:, bl * 128:(bl + 1) * 128], rkf[:, bl * 128:(bl + 1) * 128], ident_b, [b_rk[k], b_const],
                   PBb[7], bl == 0)
            for bl in range(2):
                TR(pbv[:, (3 + bl) * 128:(4 + bl) * 128], kvraw[k][:, bl * 128:(bl + 1) * 128], ident_b,
                   [b_kvraw[k], b_const], PBb[7], False)
            ts = slice(tau * 128, (tau + 1) * 128)
            ACT(KT_all[:, :, ts], pbv[:, 0:384].rearrange("p (a t) -> p a t", a=3), AF.Copy, [PBb[7]], [], pwrites=[b_KV])
            DVE("tensor_copy", [PBb[7]], [], kvT_raw[:, :, ts], pbv[:, 384:640].rearrange("p (a t) -> p a t", a=2),
                pwrites=[b_kvT])

        for t_ in range(min(3, nt1b)):
            load_norm(t_, t_ % 3, gmix, b_gmix)
        for t_ in range(min(2, nt1b)):
            norm_T(t_ % 3)
        stage1(0)
        for tau in range(nt1b):
            if tau + 3 < nt1b:
                load_norm(tau + 3, (tau + 3) % 3, gmix, b_gmix)
            if tau + 2 < nt1b:
                norm_T((tau + 2) % 3)
            if tau + 1 < nt1b:
                stage1(tau + 1)
            if tau >= 1:
                stage2(tau - 1)
        stage2(nt1b - 1)
        S.barrier()
        A.off = m1b2

        if stop_after == "1b":
            b_out = Buf("out")
            tmpf = A.alloc([2048], F32)
            bt_ = Buf()
            DVE("tensor_copy", [b_KV], [bt_], tmpf[:, 0:128], KT_slc[:, 384:512])
            DVE("tensor_copy", [b_KV], [], tmpf[:, 128:256], KT_win[:, 384:512], pwrites=[bt_])
            DVE("tensor_copy", [b_KV], [], tmpf[:, 256:384], KT_swa[:, 384:512], pwrites=[bt_])
            DVE("tensor_copy", [b_KV], [], tmpf[:, 384:514], V1_slc[:, 3, :, :].rearrange("p g d -> p (g d)"), pwrites=[bt_])
            DVE("tensor_copy", [b_KV], [], tmpf[:, 514:644], V1_win[:, 3, :, :].rearrange("p g d -> p (g d)"), pwrites=[bt_])
            DVE("tensor_copy", [b_KV], [], tmpf[:, 644:709], V1_swa[:, 3, :], pwrites=[bt_])
            DVE("tensor_copy", [b_kvT], [], tmpf[:, 709:837], kvT_raw[:, 0, 384:512], pwrites=[bt_])
            DVE("tensor_copy", [b_kvT], [], tmpf[:, 837:965], kvT_raw[:, 1, 384:512], pwrites=[bt_])
            DVE("tensor_copy", [b_sc_tab], [], tmpf[:, 965:1029], sincos[:, 3, :], pwrites=[bt_])
            DMA("sp", out_d[0:128, :], tmpf[:, 0:1024], reads=[bt_], pwrites=[b_out], dbuf=b_out)
            DMA("sp", out_d[128:256, :], tmpf[:, 1024:2048], reads=[bt_], pwrites=[b_out], dbuf=b_out)
            S.op("sp", lambda e: e.nop(), reads=[b_out])
            S.emit()
            return nc, S, A

        m1c = A.off
        W1bd = A.alloc([32, 512], BF16)
        W2s = [A.alloc([2, 64], BF16) for _ in range(2)]
        peT = [A.alloc([32], BF16) for _ in range(2)]
        peb = A.alloc([512], BF16)
        ones_b = A.alloc([128], BF16)
        rtmp = A.alloc([2, 16, 32], F32)
        b_rtmp = Buf()
        b_w1r = Buf()
        b_w1bd = Buf()
        POOL("memset", [], [b_w1bd], W1bd[0:64, :, 256:512], 0.0)
        POOL("memset", [], [], W1bd[64:128, :, 0:256], 0.0, pwrites=[b_w1bd])
        for kv, (w2d_, ped_) in enumerate(((ck_w2, pe_k), (cv_w2, pe_v))):
            DMA("pool", W2s[kv], w2d_.rearrange("(c p) n -> p c n", p=128), pwrites=[b_w1r], dbuf=b_w1r)
            DMA("pool", peT[kv][0:64], ped_[:, :], pwrites=[b_w1r], dbuf=b_w1r)
        DVE("memset", [], [], ones_b, 1.0, pwrites=[b_w1r])
        b_peb = Buf()
        xg = A.alloc([512], F32)
        sq = A.alloc([512], F32)
        h1 = A.alloc([512], BF16)
        h1T = A.alloc([4, 128], BF16)
        kc_in = A.alloc([2, 64], F32)
        rkc = A.alloc([2, 64], BF16)
        b_xg, b_sq, b_h1, b_h1T, b_kcin, b_rkc = Buf(), Buf(), Buf(), Buf(), Buf(), Buf()
        for kv, w1d_ in enumerate((ck_w1, cv_w1)):
            src = w1d_.rearrange("(o d) n -> d o n", d=64)
            DMA("pool", W1bd[0:64, :, 0:256], src, pwrites=[b_w1bd], dbuf=b_w1bd)
            DMA("pool", W1bd[64:128, :, 256:512], src, pwrites=[b_w1bd], dbuf=b_w1bd)
            for o in range(32):
                MM(PB[2][0:1, 0:256], peT[kv][0:64, o:o + 1], W1bd[0:64, o, 0:256], o == 0, o == 31,
                   [b_w1r, b_w1bd], PBb[2])
            ACT(peb[0:1, 0:256], PB[2][0:1, 0:256], AF.Copy, [PBb[2]], [b_peb])
            ACT(peb[0:1, 256:512], PB[2][0:1, 0:256], AF.Copy, [PBb[2]], [], pwrites=[b_peb])
            def first_layer(ch):
                bk = ch % 2
                for o in range(32):
                    c0 = ch * 2048 + o
                    MM(PB[bk], kvT_raw[:, kv, c0:c0 + 2033:16], W1bd[:, o, :], o == 0, False, [b_kvT, b_w1bd], PBb[bk])
                MM(PB[bk], ones_b[0:1, :], peb[0:1, :], False, True, [b_peb, b_w1r], PBb[bk])

            def rest(ch):
                bk = ch % 2
                ACT(sq, PB[bk], AF.Square, [PBb[bk]], [b_sq])
                DVE("tensor_scalar", [b_sq], [b_sq], sq, sq, 0.044715, 1.0, ALU.mult, ALU.add)
                DVE("tensor_tensor", [b_sq, PBb[bk]], [b_xg], xg, sq, PB[bk], ALU.mult)
                ACT(xg, xg, AF.Sigmoid, [b_xg], [b_xg], scale=1.5957691216057308)
                DVE("tensor_tensor", [b_xg, PBb[bk]], [b_h1], h1, xg, PB[bk], ALU.mult)

            def second_layer(ch):
                transpose_to(h1T, h1, b_h1, 4, 7, b_h1T, evac="act")
                for g in range(2):
                    ob = 4 + g
                    for c2 in range(2):
                        MM(PB[ob][:, 0:64], h1T[:, g * 2 + c2, :], W2s[kv][:, c2, :], c2 == 0, c2 == 1,
                           [b_h1T, b_w1r], PBb[ob])
                    if kv == 0:
                        if g == 0:
                            DVE("tensor_copy", [PBb[ob]], [b_kcin], kc_in[:, 0, :], PB[ob][:, 0:64])
                        else:
                            DVE("tensor_copy", [PBb[ob]], [], kc_in[:, 1, :], PB[ob][:, 0:64], pwrites=[b_kcin])
                    else:
                        ACT(V1c[:, ch, g, 0:64], PB[ob][:, 0:64], AF.Copy, [PBb[ob]], [], pwrites=[b_KV])
                if kv == 0:
                    rope(rkc, kc_in, sincos_c[:, ch, :], 2, [b_kcin, b_sc_tab], b_rkc, rtmp, b_rtmp)
                    pbv = PB[6].bitcast(BF16)
                    TR(pbv[:, 0:128], rkc.rearrange("p h d -> p (h d)"), ident_b, [b_rkc, b_const], PBb[6], True)
                    ACT(KcT[:, ch * 128:(ch + 1) * 128], pbv[:, 0:128], AF.Copy, [PBb[6]], [], pwrites=[b_KV])

            first_layer(0)
            for ch in range(4):
                rest(ch)
                if ch + 1 < 4:
                    first_layer(ch + 1)
                second_layer(ch)
        S.barrier()
        A.off = m_pre_wkv

        if stop_after == "1c":
            b_out = Buf("out")
            tmpf = A.alloc([1024], F32)
            bt_ = Buf()
            DVE("tensor_copy", [b_KV], [bt_], tmpf[:, 0:512], KcT)
            DVE("tensor_copy", [b_KV], [], tmpf[:, 512:1024], V1c.rearrange("p c g d -> p (c g d)")[:, 0:512], pwrites=[bt_])
            DMA("sp", out_d[0:128, :], tmpf[:, 0:1024], reads=[bt_], pwrites=[b_out], dbuf=b_out)
            S.op("sp", lambda e: e.nop(), reads=[b_out])
            S.emit()
            return nc, S, A

        b_qscr, b_gab, b_ngs = Buf("qscr"), Buf("gabscr"), Buf("ngscr")
        Wq = A.alloc([8, 1048], BF16)
        Wgab = A.alloc([8, 2048], BF16)
        b_wq = Buf()
        DMA("pool", Wq[:, :, 0:512], wins[:, :, 0:512], pwrites=[b_wq], dbuf=b_wq)
        DMA("pool", Wq[:, :, 512:1024], wins[:, :, 1304:1816], pwrites=[b_wq], dbuf=b_wq)
        DMA("pool", Wq[:, :, 1024:1048], wins[:, :, 1280:1304], pwrites=[b_wq], dbuf=b_wq)
        DMA("pool", Wgab[:, :, 0:1024], wins[:, :, 1944:2968], pwrites=[b_wq], dbuf=b_wq)
        DMA("pool", Wgab[:, :, 1024:2048], wins[:, :, 2968:3992], pwrites=[b_wq], dbuf=b_wq)
        gmix = A.alloc([D], F32)
        b_gmix = Buf()
        DMA("sp", gmix, g_mix[0:1, :].broadcast_to([128, D]), writes=[b_gmix], dbuf=b_gmix)
        hb = [A.alloc([D], F32) for _ in range(2)]
        ub = [A.alloc([D], BF16) for _ in range(3)]
        uT = [A.alloc([8, 128], BF16) for _ in range(2)]
        gabs = [A.alloc([2048], BF16) for _ in range(2)]
        b_gabbs = [Buf() for _ in range(2)]
        ngss = [A.alloc([24], F32) for _ in range(2)]
        b_ngsbs = [Buf() for _ in range(2)]
        scs = [A.alloc([4], F32) for _ in range(3)]
        rtmp = A.alloc([2, 8, 32], F32)
        qin = A.alloc([16, 64], F32)
        rq = A.alloc([1024], BF16)
        qTz = A.alloc([4, 4, 128], BF16)
        b_qTz = Buf()
        POOL("memset", [], [b_qTz], qTz, 0.0)
        b_hb = [Buf() for _ in range(2)]
        b_ub = [Buf() for _ in range(3)]
        b_uT = [Buf() for _ in range(2)]
        b_junk, b_rtmp, b_qin, b_rq, b_qT, b_ngsb, b_gabb = Buf(), Buf(), Buf(), Buf(), Buf(), Buf(), Buf()
        b_scs = [Buf() for _ in range(3)]
        def stage_n1d(i):
            tau = 4 * i + 3
            kh, k3 = i % 2, i % 3
            DMA("sp", hb[kh], hscr[tau * 128:(tau + 1) * 128, :], reads=[b_hscr], writes=[b_hb[kh]], dbuf=b_hb[kh])
            rmsnorm_to_bf16(hb[kh], b_hb[kh], gmix, b_gmix, ub[k3], b_ub[k3], ub[k3], b_ub[k3], scs[k3], b_scs[k3])

        def stage_a1d(i):
            tau = 4 * i + 3
            k = i % 2
            qin = qins[k]
            b_qin = b_qins[k]
            gab, b_gabb, ngs, b_ngsb = gabs[k], b_gabbs[k], ngss[k], b_ngsbs[k]
            transpose_to(uT[k], ub[i % 3], b_ub[i % 3], 8, 6, b_uT[k], evac="act")
            for kc in range(8):
                MM(PB[5][:, 0:24], uT[k][:, kc, :], Wq[:, kc, 1024:1048], kc == 0, kc == 7, [b_uT[k], b_wq], PBb[5])
            ACT(ngs, PB[5][:, 0:24], AF.Sigmoid, [PBb[5]], [b_ngsb])
            DMA("pool", ngscr[i], ngs, reads=[b_ngsb], pwrites=[b_ngs], dbuf=b_ngs)
            for hq in range(2):
                for kc in range(8):
                    MM(PB[hq], uT[k][:, kc, :], Wq[:, kc, hq * 512:(hq + 1) * 512], kc == 0, kc == 7,
                       [b_uT[k], b_wq], PBb[hq])
            ACT(qin[:, 0:8, :], PB[0].rearrange("p (h d) -> p h d", h=8), AF.Copy, [PBb[0]], [b_qin])
            ACT(qin[:, 8:16, :], PB[1].rearrange("p (h d) -> p h d", h=8), AF.Copy, [PBb[1]], [], pwrites=[b_qin])
            for gq in range(4):
                bk = 2 + (gq % 2) if gq < 2 else 4 + (gq % 2)
                bk = [2, 3, 4, 5][gq]
                for kc in range(8):
                    MM(PB[bk], uT[k][:, kc, :], Wgab[:, kc, gq * 512:(gq + 1) * 512], kc == 0, kc == 7,
                       [b_uT[k], b_wq], PBb[bk])
                if gq == 0:
                    ACT(gab[:, 0:512], PB[bk], AF.Sigmoid, [PBb[bk]], [b_gabb])
                else:
                    ACT(gab[:, gq * 512:(gq + 1) * 512], PB[bk], AF.Sigmoid, [PBb[bk]], [], pwrites=[b_gabb])
            DMA("pool", gabscr[i], gab, reads=[b_gabb], pwrites=[b_gab], dbuf=b_gab)

        def stage_r1d(i):
            tau = 4 * i + 3
            k = i % 2
            rq = rqs[k]
            b_rq = b_rqs[k]
            qin = qins[k]
            b_qin = b_qins[k]
            for s_ in range(2):
                src = qin[:, s_ * 8:(s_ + 1) * 8, :].rearrange("p (g r) d -> p r g d", g=2)
                dstv = rq[:, s_ * 512:(s_ + 1) * 512].rearrange("p (r g d) -> p r g d", r=4, g=2)
                sc_ap = sincos[:, tau, :]
                sin_b = sc_ap[:, 0:32].unsqueeze(1).unsqueeze(1).broadcast_to([128, 4, 2, 32])
                cos_b = sc_ap[:, 32:64].unsqueeze(1).unsqueeze(1).broadcast_to([128, 4, 2, 32])
                x1, x2 = src[:, :, :, 0:32], src[:, :, :, 32:64]
                t1 = rtmp[:, 0, 0:8, :].rearrange("p (r g) d -> p r g d", g=2)
                t2 = rtmp[:, 1, 0:8, :].rearrange("p (r g) d -> p r g d", g=2)
                rb = [b_qin, b_sc_tab]
                DVE("tensor_tensor", rb, [b_rtmp], t1, x1, cos_b, ALU.mult)
                DVE("tensor_tensor", rb + [b_rtmp], [b_rtmp], t2, x2, sin_b, ALU.mult)
                if s_ == 0:
                    DVE("tensor_tensor", [b_rtmp], [b_rq], dstv[:, :, :, 0:32], t1, t2, ALU.subtract)
                else:
                    DVE("tensor_tensor", [b_rtmp], [], dstv[:, :, :, 0:32], t1, t2, ALU.subtract, pwrites=[b_rq])
                DVE("tensor_tensor", rb + [b_rtmp], [b_rtmp], t1, x2, cos_b, ALU.mult)
                DVE("tensor_tensor", rb + [b_rtmp], [b_rtmp], t2, x1, sin_b, ALU.mult)
                DVE("tensor_tensor", [b_rtmp], [], dstv[:, :, :, 32:64], t1, t2, ALU.add, pwrites=[b_rq])

        def stage_t1d(i):
            k = i % 2
            rq = rqs[k]
            b_rq = b_rqs[k]
            pbv7 = PB[7].bitcast(BF16)
            for bl in range(8):
                TR(pbv7[:, bl * 128:(bl + 1) * 128], rq[:, bl * 128:(bl + 1) * 128], ident_b, [b_rq, b_const],
                   PBb[7], bl == 0)
            for s_ in range(2):
                for g in range(2):
                    srcv = pbv7[g * 64:(g + 1) * 64, s_ * 512:(s_ + 1) * 512].rearrange("p (b t) -> p b t", b=4)
                    dstv = qTz[g * 64:(g + 1) * 64, 2 * s_ + g, :, :]
                    if g == 0:
                        ACT(dstv, srcv, AF.Copy, [PBb[7]], [], pwrites=[b_qTz])
                    else:
                        DVE("tensor_copy", [PBb[7]], [], dstv, srcv, pwrites=[b_qTz])
            DMA("pool", qscr[i], qTz.rearrange("p a b t -> p (a b t)"), reads=[b_qTz], pwrites=[b_qscr], dbuf=b_qscr)


        qins = [qin, A.alloc([16, 64], F32)]
        b_qins = [b_qin, Buf()]
        rqs = [rq, A.alloc([1024], BF16)]
        b_rqs = [b_rq, Buf()]
        for i_ in range(min(3, nown)):
            stage_n1d(i_)
        stage_a1d(0)
        if nown > 1:
            stage_a1d(1)
        stage_r1d(0)
        for i in range(nown):
            if i + 3 < nown:
                stage_n1d(i + 3)
            if i + 2 < nown:
                stage_a1d(i + 2)
            if i + 1 < nown:
                stage_r1d(i + 1)
            stage_t1d(i)
        S.barrier()
        A.off = m_res

        if stop_after == "1d":
            b_out = Buf("out")
            tmpb = A.alloc([1024], BF16)
            tmpf = A.alloc([1024], F32)
            bt_, bt2 = Buf(), Buf()
            DMA("sp", tmpb, qscr[0][:, 0:1024], reads=[b_qscr], writes=[bt_], dbuf=bt_)
            DVE("tensor_copy", [bt_], [bt2], tmpf, tmpb)
            DMA("sp", out_d[0:128, :], tmpf, reads=[bt2], pwrites=[b_out], dbuf=b_out)
            S.op("sp", lambda e: e.nop(), reads=[b_out])
            S.emit()
            return nc, S, A

        KE1 = A.alloc([NT * 128], BF16)
        KE = [KT_slc, KE1]
        DVE("tensor_copy", [b_KV], [], KE1[64:128, 0:NT * 64], KT_slc[64:128, 0:NT * 64], pwrites=[b_KV])
        ACT(KE1[64:128, NT * 64:NT * 128], KT_slc[64:128, NT * 64:NT * 128], AF.Copy, [b_KV], [], pwrites=[b_KV])
        Wa = A.alloc([4, D], BF16)
        Wb = A.alloc([4, D], BF16)
        Wo = A.alloc([8, D], BF16)
        b_E = Buf("E")
        causal_m = A.alloc([128], BF16)
        strict_m = A.alloc([128], BF16)
        causal_rep = causal_m.unsqueeze(1).broadcast_to([128, 4, 128])
        strict_rep = strict_m.unsqueeze(1).broadcast_to([128, 4, 128])
        sinkexp = A.alloc([8], F32)
        b_masks = Buf("masks")
        b_mrep = Buf("mrep")
        DMA("pool", causal_m, cmask_d[:, 0, :], pwrites=[b_mrep], dbuf=b_mrep)
        DMA("pool", strict_m, cmask_d[:, 1, :], pwrites=[b_mrep], dbuf=b_mrep)
        DMA("sp", sinkexp, sinks[0:1, :].broadcast_to([128, 8]), writes=[b_masks], dbuf=b_masks)
        ACT(sinkexp, sinkexp, AF.Exp, [b_masks], [], pwrites=[b_mrep])
        qT = [A.alloc([4, 512], BF16) for _ in range(2)]
        ngs = [A.alloc([24], F32) for _ in range(2)]
        gab0 = A.alloc([2048], BF16)
        hb20 = A.alloc([D], F32)
        gab = [gab0, gab0]
        hb2 = [hb20, hb20]
        cmT_m = [A.alloc([128], BF16) for _ in range(2)]
        cmT_rep = [m_.unsqueeze(1).broadcast_to([128, 4, 128]) for m_ in cmT_m]
        QS = [[[A.alloc([512], BF16) for _ in range(2)] for _ in range(2)] for _ in range(2)]
        b_QS = [[[Buf() for _ in range(2)] for _ in range(2)] for _ in range(2)]
        selb_sw = A.alloc([128], F32)
        oab_f = [A.alloc([1024], F32) for _ in range(2)]
        b_qT = [Buf() for _ in range(2)]
        b_ngsb = [Buf() for _ in range(2)]
        b_gabb0, b_hb20 = Buf(), Buf()
        b_gabb = [b_gabb0, b_gabb0]
        b_hb2 = [b_hb20, b_hb20]
        b_cmT = [Buf() for _ in range(2)]
        b_oabf = [Buf() for _ in range(2)]
        cmpm = A.alloc([512], F32)
        keepm = A.alloc([128], F32)
        addm = A.alloc([128], F32)
        e_sb = A.alloc([512], F32)
        em = [A.alloc([512], F32) for _ in range(4)]
        P4 = A.alloc([512], F32)
        imp = A.alloc([128], F32)
        imp2 = A.alloc([128], F32)
        tmpk = A.alloc([128], F32)
        m8a = A.alloc([8], F32)
        m8b = A.alloc([8], F32)
        rs = A.alloc([4], F32)
        rinv = A.alloc([4], F32)
        selb = A.alloc([128], F32)
        otmp = A.alloc([4, 64], F32)
        fac4 = A.alloc([4], F32)
        b_tile, b_e, b_P4, b_imp, b_sel, b_rs = Buf(), Buf(), Buf(), Buf(), Buf(), Buf()
        b_em = [Buf() for _ in range(4)]
        b_otmp, b_fac = Buf(), Buf()
        m1, m2 = em[0], em[1]
        b_m1, b_m2 = b_em[0], b_em[1]
        oT = em[2].bitcast(BF16).rearrange("p (b t) -> p b t", b=8)
        mT = em[3].bitcast(BF16).rearrange("p (b t) -> p b t", b=8)
        b_oT, b_mT = b_em[2], b_em[3]
        oab = e_sb.bitcast(BF16)
        merged = P4.bitcast(BF16)
        b_oab, b_mg = b_e, b_P4
        b_h2scr = Buf("h2scr")
        acc_cnt = [0]
        oT_cnt = [0]
        oTs = [A.alloc([512], F32) for _ in range(2)]
        b_oTs = [Buf() for _ in range(2)]
        pT2 = [A.alloc([1024], BF16) for _ in range(2)]
        b_pT2 = [[Buf(), Buf()] for _ in range(2)]

        def next_ob():
            ob = 4 + acc_cnt[0] % 2
            acc_cnt[0] += 1
            return ob

        def combine(par, ob, g, gate_ap, sink_ap, dst, first):
            v = PB[ob][:, 0:260].rearrange("p (r e) -> p r e", e=65)
            den = v[:, :, 64]
            num = v[:, :, 0:64]
            if sink_ap is not None:
                DVE("tensor_tensor", [PBb[ob], b_mrep], [b_fac], fac4, den, sink_ap, ALU.add)
            else:
                DVE("tensor_scalar", [PBb[ob]], [b_fac], fac4, den, 1e-30, None, ALU.max)
            DVE("reciprocal", [b_fac], [b_fac], fac4, fac4)
            if gate_ap is not None:
                DVE("tensor_tensor", [b_fac, b_ngsb[par]], [b_fac], fac4, fac4, gate_ap, ALU.mult)
            tgt = dst[:, g * 256:(g + 1) * 256].rearrange("p (r d) -> p r d", r=4)
            fbc = fac4.unsqueeze(2).broadcast_to([128, 4, 64])
            if first:
                DVE("tensor_tensor", [PBb[ob], b_fac], [], tgt, num, fbc, ALU.mult, pwrites=[b_oabf[par]])
            else:
                DVE("tensor_tensor", [PBb[ob], b_fac], [b_otmp], otmp, num, fbc, ALU.mult)
                DVE("tensor_tensor", [b_otmp, b_oabf[par]], [], tgt, tgt, otmp, ALU.add, pwrites=[b_oabf[par]])

        def run_groups(par, groups, side):
            units = []
            for (g, qblk0, items, gate_ap, sink_ap, dst, first) in groups:
                ob = next_ob()
                n = len(items)
                for idx, it in enumerate(items):
                    units.append((g, qblk0, ob, idx, n, it, (gate_ap, sink_ap, dst, first)))
            pending = []
            LOOK = 3

            def stage_a(j, u):
                g, qblk0, ob, idx, n, (KT_ap, V_ap, bias_ap, sel, mask), _ = u
                bk = j % 4
                pv = pT2[bk // 2][:, (bk % 2) * 512:(bk % 2 + 1) * 512]
                pb_ = b_pT2[bk // 2][bk % 2]
                if sel is None:
                    rhs = qT[par][:, (qblk0 // 4) * 2 + g, :]
                    MM(PB[bk], KT_ap, rhs, True, True, [b_KV, b_qT[par]], PBb[bk])
                else:
                    w_ = sel // 32
                    MM(PB[bk], KT_ap, QS[par][g][w_], True, True, [b_KV, b_QS[par][g][w_]], PBb[bk])
                if bias_ap is None:
                    ACT(pv, PB[bk], AF.Exp, [PBb[bk]], [pb_], scale=0.125)
                else:
                    ACT(pv, PB[bk], AF.Exp, [PBb[bk], b_tab], [pb_], bias=bias_ap, scale=0.125)
                if mask is not None:
                    pv3 = pv.rearrange("p (b t) -> p b t", b=4)
                    DVE("tensor_tensor", [pb_, b_mrep, b_cmT[par]], [pb_], pv3, pv3, mask, ALU.mult)

            def stage_b(j, u):
                g, qblk0, ob, idx, n, (KT_ap, V_ap, bias_ap, sel, mask), (gate_ap, sink_ap, dst, first) = u
                bk = j % 4
                pv = pT2[bk // 2][:, (bk % 2) * 512:(bk % 2 + 1) * 512]
                pb_ = b_pT2[bk // 2][bk % 2]
                MM(PB[ob][0:65, :], V_ap, pv, idx == 0, idx == n - 1, [pb_, b_KV], PBb[ob])
                if idx == n - 1:
                    k2 = oT_cnt[0] % 2
                    oT_cnt[0] += 1
                    DVE("tensor_copy", [PBb[ob]], [b_oTs[k2]], oTs[k2][0:65, :], PB[ob][0:65, :])
                    pending.append((j + 3, k2, g, gate_ap, sink_ap, dst, first))

            def flush(j, force=False):
                while pending and (force or pending[0][0] <= j):
                    _, k2, g, gate_ap, sink_ap, dst, first = pending.pop(0)
                    for r in range(4):
                        TR(PB[7][:, r * 65:(r + 1) * 65], oTs[k2][0:65, r * 128:(r + 1) * 128], ident_f[0:65, 0:65],
                           [b_oTs[k2], b_const], PBb[7], r == 0)
                    combine(par, 7, g, gate_ap, sink_ap, dst, first)

            nu = len(units)
            njobs = len(side)
            stride = max(1, (nu + LOOK) // max(1, njobs))
            burst = max(1, -(-njobs // (nu + LOOK)))
            for j in range(nu + LOOK):
                if j < nu:
                    stage_a(j, units[j])
                if j - LOOK >= 0:
                    stage_b(j - LOOK, units[j - LOOK])
                    flush(j - LOOK)
                if j % stride == stride - 1:
                    for _ in range(burst):
                        if side:
                            job = side.pop(0)
                            if job is not None:
                                job()
            flush(0, force=True)
            while side:
                job = side.pop(0)
                if job is not None:
                    job()

        def prologue_jobs(i):
            par = i % 2
            tau = 4 * i + 3
            ncol = 32 * i + 32
            nblk = ncol // 4
            jobs = []

            def loads():
                DMA("sp", qT[par].rearrange("p a t -> p (a t)"), qscr[i], reads=[b_qscr], writes=[b_qT[par]], dbuf=b_qT[par])
                DMA("sp", ngs[par], ngscr[i], reads=[b_ngs], writes=[b_ngsb[par]], dbuf=b_ngsb[par])
                DMA("pool", cmT_m[par], cmpmT_d[i], writes=[b_cmT[par]], dbuf=b_cmT[par])
                for w_ in range(2):
                    DMA("sp", QS[par][0][w_][0:64, :], qscr[i][0:64, 0:512], reads=[b_qscr], pwrites=[b_QS[par][0][w_]],
                        dbuf=b_QS[par][0][w_])
                    DMA("sp", QS[par][1][w_][64:128, :], qscr[i][64:128, 512:1024], reads=[b_qscr],
                        pwrites=[b_QS[par][1][w_]], dbuf=b_QS[par][1][w_])
                DMA("sp", cmpm, cmpm_d[i], writes=[b_tile], dbuf=b_tile)
                DMA("sp", keepm, keep_d[i], pwrites=[b_tile], dbuf=b_tile)
                DMA("sp", addm, addm_d[i], pwrites=[b_tile], dbuf=b_tile)
            jobs.append(loads)

            def score(g, r):
                def f():
                    bk = 6 if r % 2 == 0 else 7
                    MM(PB[bk][:, 0:ncol], qT[par][:, g, r * 128:(r + 1) * 128], KcT[:, 0:ncol], True, True,
                       [b_qT[par], b_KV], PBb[bk])
                    ACT(e_sb[:, 0:ncol], PB[bk][:, 0:ncol], AF.Exp, [PBb[bk]], [b_e], scale=0.125)
                    DVE("scalar_tensor_tensor", [b_e, b_tile], [b_em[r]], em[r][:, 0:ncol], e_sb[:, 0:ncol], 1.0,
                        cmpm[:, 0:ncol], ALU.mult, ALU.mult, pwrites=[b_rs], accum_out=rs[:, r:r + 1])
                return f

            def chain(g):
                P4v = P4[:, 0:ncol].rearrange("p (b f) -> p b f", f=4)

                def fa():
                    DVE("tensor_scalar", [b_rs], [b_rs], rs, rs, 1e-30, None, ALU.max)
                    DVE("reciprocal", [b_rs], [b_rs], rinv, rs)
                    DVE("tensor_scalar", [b_em[0], b_rs], [b_P4], P4[:, 0:ncol], em[0][:, 0:ncol], rinv[:, 0:1], None, ALU.mult)
                    for r in range(1, 4):
                        DVE("scalar_tensor_tensor", [b_em[r], b_rs, b_P4], [b_P4], P4[:, 0:ncol], em[r][:, 0:ncol],
                            rinv[:, r:r + 1], P4[:, 0:ncol], ALU.mult, ALU.add)

                def fb():
                    DVE("memset", [], [b_imp], imp, 0.0)
                    DVE("tensor_reduce", [b_P4, b_imp], [b_imp], imp[:, 0:nblk], P4v, AX.X, ALU.add)
                    DVE("scalar_tensor_tensor", [b_P4, b_imp], [b_imp], imp[:, 0:nblk], P4v[:, :, 3], -0.5, imp[:, 0:nblk],
                        ALU.mult, ALU.add)
                    DVE("scalar_tensor_tensor", [b_P4, b_imp], [b_imp], imp[:, 1:nblk], P4v[:, 0:nblk - 1, 3], 0.5,
                        imp[:, 1:nblk], ALU.mult, ALU.add)
                    DVE("tensor_tensor", [b_imp, b_tile], [b_imp], imp2, imp, keepm, ALU.mult)
                    DVE("tensor_tensor", [b_imp, b_tile], [b_imp], imp2, imp2, addm, ALU.add)

                def fc():
                    DVE("max", [b_imp], [b_sel], m8a, imp2)
                    DVE("match_replace", [b_imp, b_sel], [b_sel], tmpk, m8a, imp2, -3e9)
                    DVE("max", [b_sel], [b_sel], m8b, tmpk)
                    DVE("tensor_scalar", [b_imp, b_sel], [b_sel], selb, imp2, m8b[:, 7:8], None, ALU.is_ge)
                    DVE("tensor_scalar", [b_sel], [b_sel], selb, selb, 1.0, -NEGB, ALU.subtract, ALU.mult)
                return [fa, fb, fc]

            def seltr(g):
                def f():
                    DVE("tensor_copy", [b_sel], [], selb_sw[:, 0:64], selb[:, 64:128], pwrites=[b_sel])
                    DVE("tensor_copy", [b_sel], [], selb_sw[:, 64:128], selb[:, 0:64], pwrites=[b_sel])
                    TR(PB[7][:, 0:128], selb, ident_f, [b_sel, b_const], PBb[7], True)
                    TR(PB[7][:, 128:256], selb_sw, ident_f, [b_sel, b_const], PBb[7], False)
                    rows = slice(64, 128) if g == 0 else slice(0, 64)
                    for w_ in range(2):
                        nat = (w_ == 1) if g == 0 else (w_ == 0)
                        c0 = 0 if nat else 128
                        srcv = PB[7][rows, c0:c0 + 128].unsqueeze(1).broadcast_to([64, 4, 128])
                        dstv = QS[par][g][w_][rows, :].rearrange("p (b t) -> p b t", b=4)
                        DVE("tensor_copy", [PBb[7]], [], dstv, srcv, pwrites=[b_QS[par][g][w_]])
                return f

            for g in range(2):
                for r in range(4):
                    jobs.append(score(g, r))
                jobs += chain(g)
                jobs += [None] * 3
                jobs.append(seltr(g))
            return jobs

        def epilogue_jobs(i):
            par = i % 2
            tau = 4 * i + 3
            jobs = []

            def j0():
                DMA("sp", gab[par], gabscr[i], reads=[b_gab], writes=[b_gabb[par]], dbuf=b_gabb[par])
                DMA("sp", hb2[par], hscr[tau * 128:(tau + 1) * 128, :], reads=[b_hscr], writes=[b_hb2[par]], dbuf=b_hb2[par])
                DVE("tensor_copy", [b_oabf[par]], [b_oab], oab, oab_f[par])
            jobs.append(j0)
            jobs.append(lambda: transpose_to(oT, oab, b_oab, 8, 6, b_oT, evac="dve"))

            def ab(hf):
                def f():
                    for c in range(4):
                        MM(PB[6], oT[:, c, :], Wa[:, c, hf * 512:(hf + 1) * 512], c == 0, c == 3, [b_oT, b_E], PBb[6])
                    for c in range(4):
                        MM(PB[7], oT[:, 4 + c, :], Wb[:, c, hf * 512:(hf + 1) * 512], c == 0, c == 3, [b_oT, b_E], PBb[7])
                    DVE("tensor_tensor", [PBb[6], b_gabb[par]], [b_m1], m1, PB[6], gab[par][:, hf * 512:(hf + 1) * 512], ALU.mult)
                    DVE("tensor_tensor", [PBb[7], b_gabb[par]], [b_m2], m2, PB[7],
                        gab[par][:, 1024 + hf * 512:1024 + (hf + 1) * 512], ALU.mult)
                    if hf == 0:
                        DVE("tensor_tensor", [b_m1, b_m2], [b_mg], merged[:, 0:512], m1, m2, ALU.add)
                    else:
                        DVE("tensor_tensor", [b_m1, b_m2], [], merged[:, 512:1024], m1, m2, ALU.add, pwrites=[b_mg])
                return f
            jobs.append(None)
            jobs.append(ab(0))
            jobs.append(ab(1))
            jobs += [None] * 2
            jobs.append(lambda: transpose_to(mT, merged, b_mg, 8, 6, b_mT, evac="dve"))
            jobs.append(None)

            def wo(hf):
                def f():
                    bk = 6 + hf
                    for kc in range(8):
                        MM(PB[bk], mT[:, kc, :], Wo[:, kc, hf * 512:(hf + 1) * 512], kc == 0, kc == 7, [b_mT, b_E], PBb[bk])
                    DVE("tensor_tensor", [PBb[bk], b_hb2[par]], [], hb2[par][:, hf * 512:(hf + 1) * 512], PB[bk],
                        hb2[par][:, hf * 512:(hf + 1) * 512], ALU.add, pwrites=[b_hb2[par]])
                    if hf == 1:
                        DMA("pool", h2scr[i * 128:(i + 1) * 128, :], hb2[par], reads=[b_hb2[par]], pwrites=[b_h2scr],
                            dbuf=b_h2scr)
                return f
            jobs.append(wo(0))
            jobs.append(wo(1))
            return jobs

        def tile_groups(i):
            par = i % 2
            tau = 4 * i + 3
            nch = (8 * tau - 1) // 128 + 1
            of = oab_f[par]
            groups = []
            for g in range(2):
                items = [(KcT[:, ch * 128:(ch + 1) * 128], V1c[:, ch, g, :], cvalid[:, ch:ch + 1], None,
                          cmT_rep[par] if ch == nch - 1 else None) for ch in range(nch)]
                groups.append((g, 0, items, ngs[par][:, 0 * 8 + g * 4:0 * 8 + g * 4 + 4], None, of[:, 0:512], True))
            for g in range(2):
                items = [(KT_win[:, kt * 128:(kt + 1) * 128], V1_win[:, kt, g, :], kvalid[:, kt:kt + 1] if kt < 3 else None, None,
                          causal_rep if kt == tau else (strict_rep if kt == tau - 4 else None))
                         for kt in range(max(0, tau - 4), tau + 1)]
                groups.append((g, 0, items, ngs[par][:, 2 * 8 + g * 4:2 * 8 + g * 4 + 4], None, of[:, 0:512], False))
            for g in range(2):
                items = [(KT_swa[:, kt * 128:(kt + 1) * 128], V1_swa[:, kt, :], kvalid[:, kt:kt + 1] if kt < 3 else None, None,
                          causal_rep if kt == tau else strict_rep) for kt in (tau - 1, tau)]
                groups.append((g, 4, items, None, sinkexp[:, g * 4:(g + 1) * 4], of[:, 512:1024], True))
            for g in range(2):
                items = [(KE[g][:, kt * 128:(kt + 1) * 128], V1_slc[:, kt, g, :], kvalid[:, kt:kt + 1] if kt < 3 else None,
                          kt if kt < tau else None, causal_rep if kt == tau else None) for kt in range(tau + 1)]
                groups.append((g, 0, items, ngs[par][:, 1 * 8 + g * 4:1 * 8 + g * 4 + 4], None, of[:, 0:512], False))
            return groups

        p0 = prologue_jobs(0)
        p0[0]()
        DMA("pool", KE1[0:64, :], eall_d[:, :], pwrites=[b_KV], dbuf=b_KV)
        DMA("pool", KT_slc[64:128, :], eall_d[:, :], pwrites=[b_KV], dbuf=b_KV)
        DMA("pool", Wa, w_a.rearrange("(c p) n -> p c n", p=128), pwrites=[b_E], dbuf=b_E)
        DMA("pool", Wb, w_b.rearrange("(c p) n -> p c n", p=128), pwrites=[b_E], dbuf=b_E)
        DMA("pool", Wo, w_o.rearrange("(c p) n -> p c n", p=128), pwrites=[b_E], dbuf=b_E)
        g0 = tile_groups(0)
        run_groups(0, g0[:6], [j_ for j_ in p0[1:] if j_ is not None])
        side = prologue_jobs(1) if nown > 1 else []
        run_groups(0, g0[6:], side)
        for i in range(1, nown):
            side = epilogue_jobs(i - 1)
            if i + 1 < nown:
                side += prologue_jobs(i + 1)
            run_groups(i % 2, tile_groups(i), side)
        for job in epilogue_jobs(nown - 1):
            if job is not None:
                job()
        S.barrier()
        A.off = base_mark

        b_out = Buf("out")
        gfin = A.alloc([D], F32)
        b_gfin = Buf()
        DMA("sp", gfin, g_fin[0:1, :].broadcast_to([128, D]), writes=[b_gfin], dbuf=b_gfin)
        fjunk = A.alloc([D], BF16)
        fsc = A.alloc([4], F32)
        b_fj, b_fsc = Buf(), Buf()

        def fin3(t, h_sb, hbuf):
            ACT(fjunk, h_sb, AF.Square, [hbuf], [b_fj, b_fsc], accum=fsc[:, 0:1])
            DVE("tensor_scalar", [b_fsc], [b_fsc], fsc[:, 1:2], fsc[:, 0:1], 1.0 / D, EPS, ALU.mult, ALU.add)
            ACT(fsc[:, 2:3], fsc[:, 1:2], AF.Sqrt, [b_fsc], [b_fsc])
            DVE("reciprocal", [b_fsc], [b_fsc], fsc[:, 3:4], fsc[:, 2:3])
            DVE("scalar_tensor_tensor", [hbuf, b_fsc, b_gfin], [], h_sb, h_sb, fsc[:, 3:4], gfin, ALU.mult, ALU.mult,
                pwrites=[hbuf])
            DMA("pool", out_d[t * 128:(t + 1) * 128, :], h_sb, reads=[hbuf], pwrites=[b_out], dbuf=b_out)

        b_hscr = b_h2scr
        ffn_phase(nown, h2scr, g_ffn2, w2g, w2u, w2d, fin3)
        S.op("sp", lambda e: e.nop(), reads=[b_out])
        S.emit()
        return nc, S, A


def _tables(c):
    sh = 3 - c
    t = {}
    t["ident"] = np.eye(128, dtype=np.float32)
    half = 32
    invf = (np.float32(10000.0) ** (-np.arange(half, dtype=np.float32) / np.float32(half))).astype(np.float32)
    t["invf2"] = np.concatenate([invf, invf])[None, :].astype(np.float32)
    t["phase"] = np.concatenate([np.zeros(32), np.full(32, np.pi / 2)])[None, :].astype(np.float32)
    t["eall"] = np.tile(np.repeat(np.eye(64, dtype=np.float32), 64, axis=1), (1, 2))
    k = np.arange(128)[:, None]
    q = np.arange(128)[None, :]
    t["cmask"] = np.stack([(k <= q), (k > q)], axis=1).astype(np.float32)
    tau = np.arange(NT)
    t["kvalid"] = np.broadcast_to(np.where(tau - sh >= 0, 0.0, NEGB)[None, :], (128, NT)).astype(np.float32).copy()
    jr = np.arange(512)
    gj = jr - 8 * sh
    cval = (gj >= 0) & (gj < 511)
    t["cvalid"] = np.where(cval, 0.0, NEGB).reshape(4, 128).T.astype(np.float32).copy()
    cmpmT = np.zeros((NOWN, 128, 128), np.float32)
    cmpm = np.zeros((NOWN, 128, 512), np.float32)
    keep = np.zeros((NOWN, 128, 128), np.float32)
    addm = np.zeros((NOWN, 128, 128), np.float32)
    for i in range(NOWN):
        ta = 4 * i + 3
        ch = (8 * ta - 1) // 128
        jl = np.arange(128)
        jrr = ch * 128 + jl
        cmpmT[i] = (16 * jrr[:, None] + 31 <= 128 * ta + np.arange(128)[None, :]).astype(np.float32)
        vis = (16 * jr[None, :] + 31 <= 128 * ta + np.arange(128)[:, None])
        cmpm[i] = (vis & cval[None, :]).astype(np.float32)
        blk = np.arange(128)[None, :]
        tb = 2 * ta + (np.arange(128)[:, None] >= 64)
        g = blk - 2 * sh
        invalid = g < 0
        forced = (g == 0) | (blk == tb) | (blk == tb - 1)
        future = blk > tb
        kp = np.ones((128, 128), np.float32)
        ad = np.zeros((128, 128), np.float32)
        kp[np.broadcast_to(future, kp.shape)] = 0
        ad[np.broadcast_to(future, kp.shape)] = -FORCE
        kp[forced] = 0
        ad[forced] = FORCE
        kp[np.broadcast_to(invalid, kp.shape)] = 0
        ad[np.broadcast_to(invalid, kp.shape)] = -FORCE
        keep[i] = kp
        addm[i] = ad
    t["cmpmT"] = cmpmT
    t["cmpm"] = cmpm
    t["keepm"] = keep
    t["addm"] = addm
    return t


def _core_inputs(inp, core):
    b, c = core // 4, core % 4
    sh = 3 - c
    x = np.asarray(inp["x"])
    pos = np.asarray(inp["positions"])
    m = {}
    xr = np.zeros((NT * 128, D), np.float32)
    pr = np.zeros((NT * 128,), np.int32)
    if sh > 0:
        xr[sh * 128:] = x[b, :(NT - sh) * 128]
        pr[sh * 128:] = pos[b, :(NT - sh) * 128]
    else:
        xr[:] = x[b]
        pr[:] = pos[b]
    m["xrel"] = xr
    posrel = pr.reshape(NT, 128).T
    jr = np.arange(512)
    gtok = 16 * jr + 31 - sh * 128
    ok = (16 * jr - sh * 128 >= 0) & (gtok < 8192)
    pc = np.zeros(512, np.int32)
    pc[ok] = pos[b, gtok[ok]]
    m["posall"] = np.ascontiguousarray(np.concatenate([posrel, pc.reshape(4, 128).T], axis=1))
    f = lambda k: np.ascontiguousarray(np.asarray(inp[k])[0])
    m["g_ffn1"] = f("norm_ffn1")[None, :]
    m["g_mix"] = f("norm_mix")[None, :]
    m["g_ffn2"] = f("norm_ffn2")[None, :]
    m["g_fin"] = np.ascontiguousarray(np.asarray(inp["norm_final"]))[None, :]
    m["w1g"], m["w1u"], m["w1d"] = f("ffn1_gate"), f("ffn1_up"), f("ffn1_down")
    m["w2g"], m["w2u"], m["w2d"] = f("ffn2_gate"), f("ffn2_up"), f("ffn2_down")
    m["w_in"] = f("w_in")
    m["pe_k"] = np.ascontiguousarray(f("cmp_pe_k").T)
    m["pe_v"] = np.ascontiguousarray(f("cmp_pe_v").T)
    m["ck_w1"], m["ck_w2"] = f("cmp_k_w1"), f("cmp_k_w2")
    m["cv_w1"], m["cv_w2"] = f("cmp_v_w1"), f("cmp_v_w2")
    m["sinks"] = f("swa_sinks")[None, :]
    m["w_a"], m["w_b"], m["w_o"] = f("w_branch_a"), f("w_branch_b"), f("w_out")
    m.update(_tables(c))
    return m


_PROG = {}


def kernel(**inputs):
    if "nc" not in _PROG:
        _PROG["nc"] = build_program()[0]
    nc = _PROG["nc"]
    in_maps = [_core_inputs(inputs, core) for core in range(8)]
    res = run_bass_kernel_spmd(nc, in_maps, core_ids=list(range(8)))
    out = np.zeros((2, 8192, D), np.float32)
    for core in range(8):
        b, c = core // 4, core % 4
        r = np.asarray(res.results[core]["out"]).reshape(NOWN, 128, D)
        for i in range(NOWN):
            t = 4 * i + c
            out[b, t * 128:(t + 1) * 128] = r[i]
    return out
```

```python
import contextlib
import numpy as np
import concourse.bass as bass
import concourse.mybir as mybir
from concourse.bass_utils import run_bass_kernel_spmd

F32 = mybir.dt.float32
BF16 = mybir.dt.bfloat16
I32 = mybir.dt.int32
AF = mybir.ActivationFunctionType
ALU = mybir.AluOpType
AX = mybir.AxisListType

D = 1024
DFF = 2816
NF = DFF // 128
NT = 64
NOWN = 16
WIN_W = 3992
EPS = 1e-6
FORCE = 1e9
NEGB = -30000.0
SEG = 8000
ARENA_WORDS = 53200


class Buf:
    __slots__ = ("name", "writers", "readers", "dsem", "dcount", "shadow", "excl")

    def __init__(self, name="", excl=False):
        self.name = name
        self.excl = excl
        self.writers = []
        self.readers = []
        self.dsem = None
        self.dcount = 0
        self.shadow = None


class Op:
    __slots__ = ("eng", "fn", "waits", "idx", "dma", "dbuf", "dval")

    def __init__(self, eng, fn, dma):
        self.eng = eng
        self.fn = fn
        self.dma = dma
        self.waits = []
        self.idx = -1
        self.dbuf = None
        self.dval = 0


class Sched:
    ENGS = ("pe", "act", "dve", "pool", "sp")

    def __init__(self, nc):
        self.nc = nc
        self.ops = {e: [] for e in self.ENGS}
        self.seen = {e: {} for e in self.ENGS}
        self.dbufs = []
        self.cnt = {e: 0 for e in self.ENGS}
        self.last = {e: None for e in self.ENGS}

    def _need(self, op, dep):
        if dep.dma:
            b = dep.dbuf
            return (("d", id(b)), b, b.dcount * 16)
        if dep.eng == op.eng and dep.eng == "pe":
            return None
        seg = dep.idx // SEG
        return (("e", dep.eng, seg), None, dep.idx % SEG + 1)

    def op(self, eng, fn, reads=(), writes=(), pwrites=(), dma=False, dbuf=None, extra=()):
        o = Op(eng, fn, dma)
        deps = list(extra)
        for b in reads:
            deps.extend(b.writers)
            if b.excl:
                deps.extend(r for r in b.readers if r.eng != eng)
        for b in pwrites:
            deps.extend(b.readers)
            if b.writers:
                deps.append(b.writers[0])
        for b in writes:
            deps.extend(b.writers)
            deps.extend(b.readers)
        if dma:
            if eng == "pool":
                if dbuf.shadow is None:
                    dbuf.shadow = Buf(dbuf.name + "_sw")
                dbuf = dbuf.shadow
            if dbuf.dsem is None:
                dbuf.dsem = True
                self.dbufs.append(dbuf)
            dbuf.dcount += 1
            o.dbuf = dbuf
            o.dval = dbuf.dcount * 16
        else:
            o.idx = self.cnt[eng]
            self.cnt[eng] += 1
            self.last[eng] = o
        seen = self.seen[eng]
        need = {}
        for d in deps:
            if d is o:
                continue
            r = self._need(o, d)
            if r is None:
                continue
            key, b, val = r
            if dma and d.dma and d.dbuf is dbuf and val == o.dval:
                val -= 16
                if val <= 0:
                    continue
            if key not in need or need[key][1] < val:
                need[key] = (b, val)
        for key, (b, val) in need.items():
            if seen.get(key, 0) >= val:
                continue
            seen[key] = val
            o.waits.append((key, b, val))
        self.ops[eng].append(o)
        for b in reads:
            b.readers.append(o)
        for b in pwrites:
            b.writers.append(o)
        for b in writes:
            b.writers = [o]
            b.readers = []
        return o

    def barrier(self):
        lasts = [o for o in self.last.values() if o is not None]
        dmas = []
        for b in self.dbufs:
            f = Op("sp", None, True)
            f.dbuf = b
            dmas.append(f)
        for e in self.ENGS:
            self.op(e, lambda eng: eng.nop(), extra=lasts + dmas)

    def emit(self):
        nc = self.nc
        with contextlib.ExitStack() as st:
            esems = {}
            for e in self.ENGS:
                n = self.cnt[e]
                for s in range((n + SEG - 1) // SEG):
                    esems[(e, s)] = st.enter_context(nc.semaphore(f"se_{e}_{s}"))
            for i, b in enumerate(self.dbufs):
                b.dsem = st.enter_context(nc.semaphore(f"sd_{i}"))
            self.nsem = len(esems) + len(self.dbufs)
            block = st.enter_context(nc.Block())

            def run(engname, eng):
                cnt = 0
                for o in self.ops[engname]:
                    for key, b, val in o.waits:
                        if key[0] == "d":
                            eng.wait_ge(b.dsem, val)
                        else:
                            eng.wait_ge(esems[(key[1], key[2])], val)
                    ins = o.fn(eng)
                    if o.dma:
                        ins.then_inc(o.dbuf.dsem, 16)
                    else:
                        ins.then_inc(esems[(engname, cnt // SEG)], 1)
                        cnt += 1

            @block.tensor
            def _(eng):
                run("pe", eng)

            @block.scalar
            def _(eng):
                run("act", eng)

            @block.vector
            def _(eng):
                run("dve", eng)

            @block.gpsimd
            def _(eng):
                run("pool", eng)

            @block.sync
            def _(eng):
                run("sp", eng)


class Arena:
    def __init__(self, base, nwords):
        self.base = base
        self.n = nwords
        self.off = 0
        self.peak = 0

    def alloc(self, shape, dtype):
        free = int(np.prod(shape))
        words = free if dtype in (F32, I32) else (free + 1) // 2
        words = (words + 7) // 8 * 8
        assert self.off + words <= self.n, f"arena overflow {self.off}+{words}>{self.n}"
        v = self.base[:, self.off:self.off + words]
        self.off += words
        self.peak = max(self.peak, self.off)
        if dtype == BF16:
            v = v.bitcast(BF16)
        elif dtype == I32:
            v = v.bitcast(I32)
        v = v[:, 0:free]
        if len(shape) > 1:
            names = [f"a{i}" for i in range(len(shape))]
            kw = {n: int(s) for n, s in zip(names, shape)}
            v = v.rearrange(f"p ({' '.join(names)}) -> p {' '.join(names)}", **kw)
        return v


def build_program(stop_after=None, dbg=False, skip1a=False, nt1b=NT, nown=NOWN):
    nc = bass.Bass("TRN2", target_bir_lowering=False)

    def din(name, shape, dt=F32):
        return nc.dram_tensor(name, list(shape), dt, kind="ExternalInput").ap()

    xrel = din("xrel", [NT * 128, D])
    posall = din("posall", [128, NT + 4], I32)
    g_ffn1 = din("g_ffn1", [1, D])
    g_mix = din("g_mix", [1, D])
    g_ffn2 = din("g_ffn2", [1, D])
    g_fin = din("g_fin", [1, D])
    w1g = din("w1g", [D, DFF])
    w1u = din("w1u", [D, DFF])
    w1d = din("w1d", [DFF, D])
    w2g = din("w2g", [D, DFF])
    w2u = din("w2u", [D, DFF])
    w2d = din("w2d", [DFF, D])
    w_in = din("w_in", [D, WIN_W])
    pe_k = din("pe_k", [64, 32])
    pe_v = din("pe_v", [64, 32])
    ck_w1 = din("ck_w1", [2048, 256])
    ck_w2 = din("ck_w2", [256, 64])
    cv_w1 = din("cv_w1", [2048, 256])
    cv_w2 = din("cv_w2", [256, 64])
    sinks = din("sinks", [1, 8])
    w_a = din("w_a", [512, D])
    w_b = din("w_b", [512, D])
    w_o = din("w_o", [D, D])
    ident_d = din("ident", [128, 128])
    invf2_d = din("invf2", [1, 64])
    phase_d = din("phase", [1, 64])
    eall_d = din("eall", [64, NT * 128])
    cmask_d = din("cmask", [128, 2, 128])
    kvalid_d = din("kvalid", [128, NT])
    cvalid_d = din("cvalid", [128, 4])
    cmpmT_d = din("cmpmT", [NOWN, 128, 128])
    cmpm_d = din("cmpm", [NOWN, 128, 512])
    keep_d = din("keepm", [NOWN, 128, 128])
    addm_d = din("addm", [NOWN, 128, 128])

    out_d = nc.dram_tensor("out", [NOWN * 128, D], F32, kind="ExternalOutput").ap()
    hscr = nc.dram_tensor("hscr", [NT * 128, D], F32, kind="Internal").ap()
    h2scr = nc.dram_tensor("h2scr", [NOWN * 128, D], F32, kind="Internal").ap()
    qscr = nc.dram_tensor("qscr", [NOWN, 128, 2048], BF16, kind="Internal").ap()
    gabscr = nc.dram_tensor("gabscr", [NOWN, 128, 2048], BF16, kind="Internal").ap()
    ngscr = nc.dram_tensor("ngscr", [NOWN, 128, 24], F32, kind="Internal").ap()
    dbg_d = None
    if dbg:
        dbg_d = nc.dram_tensor("dbg", [128, 16384], F32, kind="ExternalOutput").ap()

    st = contextlib.ExitStack()
    with st:
        arena_t = st.enter_context(nc.sbuf_tensor("arena", [128, ARENA_WORDS], F32))
        A = Arena(arena_t[:], ARENA_WORDS)
        psum_all = st.enter_context(nc.psum_tensor("psum_all", [128, 4096], F32))[:]
        PB = [psum_all[:, i * 512:(i + 1) * 512] for i in range(8)]
        PBb = [Buf(f"bank{i}", excl=True) for i in range(8)]
        S = Sched(nc)

        def MM(out, lhsT, rhs, start, stop, reads, wb):
            if start:
                S.op("pe", lambda e: e.matmul(out, lhsT=lhsT, rhs=rhs, start=True, stop=stop),
                     reads=reads, writes=[wb])
            else:
                S.op("pe", lambda e: e.matmul(out, lhsT=lhsT, rhs=rhs, start=False, stop=stop),
                     reads=reads, pwrites=[wb])

        def MMP(out, lhsT, rhs, start, stop, reads, wb, first):
            if first:
                S.op("pe", lambda e: e.matmul(out, lhsT=lhsT, rhs=rhs, start=start, stop=stop),
                     reads=reads, writes=[wb])
            else:
                S.op("pe", lambda e: e.matmul(out, lhsT=lhsT, rhs=rhs, start=start, stop=stop),
                     reads=reads, pwrites=[wb])

        def TR(out, in_, ident, reads, wb, first):
            if first:
                S.op("pe", lambda e: e.transpose(out, in_, ident), reads=reads, writes=[wb])
            else:
                S.op("pe", lambda e: e.transpose(out, in_, ident), reads=reads, pwrites=[wb])

        def ACT(out, in_, func, reads, writes, bias=None, scale=None, accum=None, pwrites=()):
            kw = {}
            if bias is not None:
                kw["bias"] = bias
            if scale is not None:
                kw["scale"] = scale
            if accum is not None:
                kw["accum_out"] = accum
            S.op("act", lambda e: e.activation(out, in_, func, **kw), reads=reads, writes=writes,
                 pwrites=pwrites)

        def ENG(eng, name, reads, writes, *args, pwrites=(), **kw):
            S.op(eng, lambda e: getattr(e, name)(*args, **kw), reads=reads, writes=writes,
                 pwrites=pwrites)

        def DVE(name, reads, writes, *args, pwrites=(), **kw):
            ENG("dve", name, reads, writes, *args, pwrites=pwrites, **kw)

        def POOL(name, reads, writes, *args, pwrites=(), **kw):
            ENG("pool", name, reads, writes, *args, pwrites=pwrites, **kw)

        def DMA(q, out, in_, reads=(), writes=(), pwrites=(), dbuf=None):
            S.op(q, lambda e: e.dma_start(out=out, in_=in_), reads=reads, writes=writes,
                 pwrites=pwrites, dma=True, dbuf=dbuf)

        dbg_off = [0]
        dbg_buf = Buf("dbg")

        def DBG(ap_sb, rbuf, ncols, parts=128):
            if not dbg:
                return
            o = dbg_off[0]
            DMA("sp", dbg_d[0:parts, o:o + ncols], ap_sb, reads=[rbuf], pwrites=[dbg_buf], dbuf=dbg_buf)
            dbg_off[0] = o + ncols

        def early(name, tile_ap=None, rbuf=None):
            if stop_after != name:
                return False
            b_o = Buf("out")
            if tile_ap is not None:
                DMA("sp", out_d[0:128, 0:tile_ap.shape[1]], tile_ap, reads=[rbuf], pwrites=[b_o], dbuf=b_o)
                S.op("sp", lambda e: e.nop(), reads=[b_o])
            S.barrier()
            S.emit()
            return True

        ident_f = A.alloc([128], F32)
        ident_b = A.alloc([128], BF16)
        b_const = Buf("const")
        DMA("sp", ident_f, ident_d[:, :], pwrites=[b_const], dbuf=b_const)
        DMA("pool", ident_b, ident_d[:, :], pwrites=[b_const], dbuf=b_const)
        ones_f = A.alloc([128], F32)
        DVE("memset", [], [], ones_f, 1.0, pwrites=[b_const])
        stats = A.alloc([64], F32)
        base_mark = A.off

        def rmsnorm_to_bf16(x_sb, xbuf, gam, gbuf, out_bf, obuf, junk, jbuf, sc, scbuf):
            ACT(junk, x_sb, AF.Square, [xbuf], [jbuf, scbuf], accum=sc[:, 0:1])
            DVE("tensor_scalar", [scbuf], [scbuf], sc[:, 1:2], sc[:, 0:1], 1.0 / D, EPS, ALU.mult, ALU.add)
            ACT(sc[:, 2:3], sc[:, 1:2], AF.Sqrt, [scbuf], [scbuf])
            DVE("reciprocal", [scbuf], [scbuf], sc[:, 3:4], sc[:, 2:3])
            DVE("scalar_tensor_tensor", [xbuf, scbuf, gbuf], [obuf], out_bf, x_sb, sc[:, 3:4], gam,
                ALU.mult, ALU.mult)

        def transpose_to(dst3, src_bf, sbuf_, nblk, bank, dstbuf, evac="act"):
            pbv = PB[bank].bitcast(BF16)
            for b in range(nblk):
                TR(pbv[:, b * 128:(b + 1) * 128], src_bf[:, b * 128:(b + 1) * 128], ident_b,
                   [sbuf_, b_const], PBb[bank], b == 0)
            src = pbv[:, 0:nblk * 128].rearrange("p (b t) -> p b t", b=nblk)
            if evac == "act":
                ACT(dst3, src, AF.Copy, [PBb[bank]], [dstbuf])
            else:
                DVE("tensor_copy", [PBb[bank]], [dstbuf], dst3, src)

        def ffn_phase(ntiles, src_d, gam_d, wg_d, wu_d, wd_d, finish):
            m0 = A.off
            Wg = A.alloc([8, DFF], BF16)
            Wu = A.alloc([8, DFF], BF16)
            Wd = A.alloc([NF, D], BF16)
            xnT = A.alloc([8, 512], BF16)
            aT = A.alloc([NF, 512], BF16)
            xs = [A.alloc([D], F32) for _ in range(4)]
            xr = [A.alloc([D], F32) for _ in range(2)]
            sg = [A.alloc([512], BF16) for _ in range(2)]
            gam = A.alloc([D], F32)
            xn = [A.alloc([D], BF16) for _ in range(4)]
            sc = [A.alloc([4], F32) for _ in range(4)]
            b_gam = Buf()
            DMA("sp", gam, gam_d[0:1, :].broadcast_to([128, D]), writes=[b_gam], dbuf=b_gam)
            fparts = [(0, 2), (2, 6), (6, 14), (14, 22)]
            b_wg = [Buf() for _ in fparts]
            b_wu = [Buf() for _ in fparts]
            wgs = wg_d.rearrange("(kc p) n -> p kc n", p=128)
            wus = wu_d.rearrange("(kc p) n -> p kc n", p=128)
            for i, (f0, f1) in enumerate(fparts):
                DMA("pool", Wg[:, :, f0 * 128:f1 * 128], wgs[:, :, f0 * 128:f1 * 128], writes=[b_wg[i]], dbuf=b_wg[i])
                DMA("pool", Wu[:, :, f0 * 128:f1 * 128], wus[:, :, f0 * 128:f1 * 128], writes=[b_wu[i]], dbuf=b_wu[i])
            wds = wd_d.rearrange("(f p) n -> p f n", p=128)
            dparts = [(0, 6), (6, 14), (14, 22)]
            b_wd = [Buf() for _ in dparts]
            for i, (f0, f1) in enumerate(dparts):
                DMA("pool", Wd[:, f0:f1, :], wds[:, f0:f1, :], writes=[b_wd[i]], dbuf=b_wd[i])

            def fpart(f, parts):
                for i, (f0, f1) in enumerate(parts):
                    if f0 <= f < f1:
                        return i
                raise ValueError

            b_xs = [Buf() for _ in range(4)]
            b_xr = [Buf() for _ in range(2)]
            b_xn = [Buf() for _ in range(4)]
            b_sc = [Buf() for _ in range(4)]
            b_xnT = Buf()
            b_aT = [Buf() for _ in range(NF)]
            b_sg = [Buf() for _ in range(2)]
            ngroups = ntiles // 4

            def norm_part(G):
                for j in range(4):
                    t = 4 * G + j
                    DMA("sp", xs[j], src_d[t * 128:(t + 1) * 128, :], writes=[b_xs[j]], dbuf=b_xs[j])
                    rmsnorm_to_bf16(xs[j], b_xs[j], gam, b_gam, xn[j], b_xn[j], xn[j], b_xn[j], sc[j], b_sc[j])

            def tr_part(G):
                for j in range(4):
                    transpose_to(xnT[:, :, j * 128:(j + 1) * 128], xn[j], b_xn[j], 8, 4 + j, b_xnT, evac="act")

            cnt = [0]

            def gateup(G):
                for f in range(NF):
                    if f == 6 and G + 1 < ngroups:
                        norm_part(G + 1)
                    pg, pu = (0, 1) if f % 2 == 0 else (2, 3)
                    ig = fpart(f, fparts)
                    for kc in range(8):
                        MM(PB[pg], Wg[:, kc, f * 128:(f + 1) * 128], xnT[:, kc, :], kc == 0, kc == 7,
                           [b_wg[ig], b_xnT], PBb[pg])
                    for kc in range(8):
                        MM(PB[pu], Wu[:, kc, f * 128:(f + 1) * 128], xnT[:, kc, :], kc == 0, kc == 7,
                           [b_wu[ig], b_xnT], PBb[pu])
                    s = f % 2
                    ACT(sg[s], PB[pg], AF.Silu, [PBb[pg]], [b_sg[s]])
                    DVE("tensor_tensor", [b_sg[s], PBb[pu]], [b_aT[f]], aT[:, f, :], sg[s], PB[pu], ALU.mult)

            def down(G):
                for j in range(4):
                    t = 4 * G + j
                    r = t % 2
                    DMA("sp", xr[r], src_d[t * 128:(t + 1) * 128, :], writes=[b_xr[r]], dbuf=b_xr[r])
                    bk0 = 4 if j % 2 == 0 else 6
                    for f in range(NF):
                        idp = fpart(f, dparts)
                        for hf in range(2):
                            bk = bk0 + hf
                            MM(PB[bk], aT[:, f, j * 128:(j + 1) * 128], Wd[:, f, hf * 512:(hf + 1) * 512],
                               f == 0, f == NF - 1, [b_aT[f], b_wd[idp]], PBb[bk])
                    for hf in range(2):
                        bk = bk0 + hf
                        DVE("scalar_tensor_tensor", [PBb[bk], b_xr[r]], [], xr[r][:, hf * 512:(hf + 1) * 512],
                            PB[bk], 0.5, xr[r][:, hf * 512:(hf + 1) * 512], ALU.mult, ALU.add,
                            pwrites=[b_xr[r]])
                    finish(t, xr[r], b_xr[r])

            norm_part(0)
            tr_part(0)
            for G in range(ngroups):
                gateup(G)
                if G + 1 < ngroups:
                    tr_part(G + 1)
                down(G)
            S.barrier()
            A.off = m0

        b_hscr = Buf("hscr")

        def fin1(t, h_sb, hbuf):
            DMA("pool", hscr[t * 128:(t + 1) * 128, :], h_sb, reads=[hbuf], pwrites=[b_hscr], dbuf=b_hscr)

        if skip1a:
            hscr = xrel
        else:
            ffn_phase(nt1b, xrel, g_ffn1, w1g, w1u, w1d, fin1)

        if stop_after == "1a":
            tmp = A.alloc([D], F32)
            b_tmp = Buf()
            b_out = Buf("out")
            for t in range(NOWN):
                DMA("sp", tmp, hscr[t * 128:(t + 1) * 128, :], reads=[b_hscr], writes=[b_tmp], dbuf=b_tmp)
                DMA("sp", out_d[t * 128:(t + 1) * 128, :], tmp, reads=[b_tmp], pwrites=[b_out], dbuf=b_out)
            S.op("sp", lambda e: e.nop(), reads=[b_out, dbg_buf])
            S.emit()
            return nc, S, A


        KT_all = A.alloc([3, NT * 128], BF16)
        KT_slc, KT_win, KT_swa = KT_all[:, 0, :], KT_all[:, 1, :], KT_all[:, 2, :]
        V1_slc = A.alloc([NT, 2, 65], BF16)
        V1_win = A.alloc([NT, 2, 65], BF16)
        V1_swa = A.alloc([NT, 65], BF16)
        KcT = A.alloc([512], BF16)
        V1c = A.alloc([4, 2, 65], BF16)
        kvalid = A.alloc([NT], F32)
        cvalid = A.alloc([4], F32)
        b_KV = Buf("kv")
        b_tab = Buf("tab")
        DMA("sp", kvalid, kvalid_d[:, :], pwrites=[b_tab], dbuf=b_tab)
        DMA("sp", cvalid, cvalid_d[:, :], pwrites=[b_tab], dbuf=b_tab)
        POOL("memset", [], [], V1_slc[:, :, :, 64:65], 1.0, pwrites=[b_KV])
        POOL("memset", [], [], V1_win[:, :, :, 64:65], 1.0, pwrites=[b_KV])
        POOL("memset", [], [], V1_swa[:, :, 64:65], 1.0, pwrites=[b_KV])
        POOL("memset", [], [], V1c[:, :, :, 64:65], 1.0, pwrites=[b_KV])
        if early("res", kvalid, b_tab):
            return nc, S, A
        m_res = A.off
        sincos_all = A.alloc([NT + 4, 64], F32)
        sincos = sincos_all[:, 0:NT, :]
        sincos_c = sincos_all[:, NT:NT + 4, :]
        b_sc_tab = Buf("sincos")
        m_sincos = A.off

        TWO_PI = float(2 * np.pi)
        C1 = 6.28125
        C2 = float(2 * np.pi - 6.28125)

        def make_sincos(dst, pos_d, n):
            m0 = A.off
            posi = A.alloc([n], I32)
            posf = A.alloc([n], F32)
            invf2 = A.alloc([64], F32)
            ph = A.alloc([64], F32)
            ang = A.alloc([n, 64], F32)
            ki = A.alloc([n, 64], I32)
            kf = A.alloc([n, 64], F32)
            bt = Buf()
            DMA("sp", posi, pos_d[:, :], pwrites=[bt], dbuf=bt)
            DMA("sp", invf2, invf2_d[0:1, :].broadcast_to([128, 64]), pwrites=[bt], dbuf=bt)
            DMA("sp", ph, phase_d[0:1, :].broadcast_to([128, 64]), pwrites=[bt], dbuf=bt)
            bw = Buf()
            DVE("tensor_copy", [bt], [bw], posf, posi)
            DVE("tensor_tensor", [bt, bw], [bw], ang, invf2.unsqueeze(1).broadcast_to([128, n, 64]),
                posf.unsqueeze(2).broadcast_to([128, n, 64]), ALU.mult)
            DVE("tensor_tensor", [bt, bw], [bw], ang, ang, ph.unsqueeze(1).broadcast_to([128, n, 64]), ALU.add)
            DVE("tensor_scalar", [bw], [bw], ki, ang, 1.0 / TWO_PI, None, ALU.mult)
            DVE("tensor_copy", [bw], [bw], kf, ki)
            DVE("scalar_tensor_tensor", [bw], [bw], ang, kf, -C1, ang, ALU.mult, ALU.add)
            DVE("scalar_tensor_tensor", [bw], [bw], ang, kf, -C2, ang, ALU.mult, ALU.add)
            DVE("tensor_scalar", [bw], [bw], kf, ang, float(np.pi), -TWO_PI, ALU.is_gt, ALU.mult)
            DVE("tensor_tensor", [bw], [bw], ang, ang, kf, ALU.add)
            DVE("tensor_scalar", [bw], [bw], kf, ang, float(-np.pi), TWO_PI, ALU.is_lt, ALU.mult)
            DVE("tensor_tensor", [bw], [bw], ang, ang, kf, ALU.add)
            DVE("tensor_scalar", [bw], [bw], ang, ang, float(np.pi), float(-np.pi), ALU.min, ALU.max)
            ACT(dst, ang, AF.Sin, [bw], [], pwrites=[b_sc_tab])
            S.barrier()
            A.off = m0

        m_pre_wkv = A.off
        Wkv = A.alloc([8, 896], BF16)
        b_wkv = Buf()
        wins = w_in.rearrange("(kc p) n -> p kc n", p=128)
        for d0, s0, wd in ((0, 768, 128), (128, 1024, 128), (256, 1816, 64), (320, 896, 128),
                           (448, 1152, 128), (576, 1880, 64), (640, 512, 256)):
            DMA("pool", Wkv[:, :, d0:d0 + wd], wins[:, :, s0:s0 + wd], pwrites=[b_wkv], dbuf=b_wkv)
        make_sincos(sincos_all, posall, NT + 4)
        if early("sincos", sincos[:, 3, :], b_sc_tab):
            return nc, S, A

        def rope(dst, src, sc_ap, nh, rbufs, wbuf, tmp, tbuf, dst_views=None):
            sin_b = sc_ap[:, 0:32].unsqueeze(1).broadcast_to([128, nh, 32])
            cos_b = sc_ap[:, 32:64].unsqueeze(1).broadcast_to([128, nh, 32])
            x1 = src[:, :, 0:32]
            x2 = src[:, :, 32:64]
            t1 = tmp[:, 0, 0:nh, :]
            t2 = tmp[:, 1, 0:nh, :]
            d1, d2 = (dst[:, :, 0:32], dst[:, :, 32:64]) if dst_views is None else dst_views
            DVE("tensor_tensor", rbufs, [tbuf], t1, x1, cos_b, ALU.mult)
            DVE("tensor_tensor", rbufs + [tbuf], [tbuf], t2, x2, sin_b, ALU.mult)
            DVE("tensor_tensor", [tbuf], [], d1, t1, t2, ALU.subtract, pwrites=[wbuf])
            DVE("tensor_tensor", rbufs + [tbuf], [tbuf], t1, x2, cos_b, ALU.mult)
            DVE("tensor_tensor", rbufs + [tbuf], [tbuf], t2, x1, sin_b, ALU.mult)
            DVE("tensor_tensor", [tbuf], [], d2, t1, t2, ALU.add, pwrites=[wbuf])

        m1b = A.off
        kvT_raw = A.alloc([2, NT * 128 + 16], BF16)
        b_kvT = Buf("kvT_raw")
        if nt1b < NT:
            POOL("memset", [], [b_kvT], kvT_raw, 0.0)
            POOL("memset", [], [b_KV], KT_all, 0.0)
            for t_ in (V1_slc, V1_win):
                POOL("memset", [], [b_KV], t_[:, :, :, 0:64], 0.0)
            POOL("memset", [], [b_KV], V1_swa[:, :, 0:64], 0.0)
        POOL("memset", [], [b_kvT], kvT_raw[:, :, NT * 128:NT * 128 + 16], 0.0)
        m1b2 = A.off
        gmix = A.alloc([D], F32)
        b_gmix = Buf()
        DMA("sp", gmix, g_mix[0:1, :].broadcast_to([128, D]), writes=[b_gmix], dbuf=b_gmix)
        hb = [A.alloc([D], F32) for _ in range(3)]
        ub = [A.alloc([D], BF16) for _ in range(3)]
        uT = [A.alloc([8, 128], BF16) for _ in range(3)]
        junk = A.alloc([D], BF16)
        scs = [A.alloc([4], F32) for _ in range(3)]
        rk = [A.alloc([6, 64], BF16) for _ in range(3)]
        rtmp = A.alloc([2, 16, 32], F32)
        kvraw = [A.alloc([256], BF16) for _ in range(3)]
        b_hb = [Buf() for _ in range(3)]
        b_ub = [Buf() for _ in range(3)]
        b_uT = [Buf() for _ in range(3)]
        b_junk = Buf()
        b_scs = [Buf() for _ in range(3)]
        b_kin = [Buf() for _ in range(2)]
        b_rk = [Buf() for _ in range(3)]
        b_rtmp = Buf()
        b_kvraw = [Buf() for _ in range(3)]

        def load_norm(tau, k, gam, gbuf):
            DMA("sp", hb[k], hscr[tau * 128:(tau + 1) * 128, :], reads=[b_hscr], writes=[b_hb[k]], dbuf=b_hb[k])
            rmsnorm_to_bf16(hb[k], b_hb[k], gam, gbuf, ub[k], b_ub[k], junk, b_junk, scs[k], b_scs[k])

        def norm_T(k):
            transpose_to(uT[k], ub[k], b_ub[k], 8, 6, b_uT[k], evac="act")

        a_sb = [A.alloc([448], F32) for _ in range(2)]
        b_sb = [A.alloc([448], F32) for _ in range(2)]
        b_asb = [Buf() for _ in range(2)]
        b_bsb = [Buf() for _ in range(2)]

        def stage1(tau):
            k = tau % 2
            k4 = tau % 3
            pa, pb = (0, 1) if k == 0 else (2, 3)
            for kc in range(8):
                MM(PB[pa][:, 0:448], uT[k4][:, kc, :], Wkv[:, kc, 0:448], kc == 0, kc == 7, [b_uT[k4], b_wkv], PBb[pa])
            for kc in range(8):
                MM(PB[pb][:, 0:448], uT[k4][:, kc, :], Wkv[:, kc, 448:896], kc == 0, kc == 7, [b_uT[k4], b_wkv], PBb[pb])
            ACT(a_sb[k], PB[pa][:, 0:448], AF.Copy, [PBb[pa]], [b_asb[k]])
            ACT(b_sb[k], PB[pb][:, 0:448], AF.Copy, [PBb[pb]], [b_bsb[k]])
            POOL("tensor_copy", [b_asb[k]], [], V1_slc[:, tau, :, 0:64],
                 a_sb[k][:, 320:448].rearrange("p (g d) -> p g d", g=2), pwrites=[b_KV])
            POOL("tensor_copy", [b_bsb[k]], [], V1_win[:, tau, :, 0:64],
                 b_sb[k][:, 0:128].rearrange("p (g d) -> p g d", g=2), pwrites=[b_KV])
            POOL("tensor_copy", [b_bsb[k]], [], V1_swa[:, tau, 0:64], b_sb[k][:, 128:192], pwrites=[b_KV])
            ACT(kvraw[k4], b_sb[k][:, 192:448], AF.Copy, [b_bsb[k]], [b_kvraw[k4]])
            rope(rk[k4][:, 0:5, :], a_sb[k][:, 0:320].rearrange("p (h d) -> p h d", h=5), sincos[:, tau, :], 5,
                 [b_asb[k], b_sc_tab], b_rk[k4], rtmp, b_rtmp)
            DVE("tensor_copy", [b_rk[k4]], [], rk[k4][:, 5, :], rk[k4][:, 4, :], pwrites=[b_rk[k4]])

        def stage2(tau):
            k = tau % 3
            pbv = PB[7].bitcast(BF16)
            rkf = rk[k].rearrange("p h d -> p (h d)")
            for bl in range(3):
                TR(pbv[:, bl * 128:(bl + 1) * 128], rkf[:, bl * 128:(bl + 1) * 128], ident_b, [b_rk[k], b_const],
                   PBb[7], bl == 0)
            for bl in range(2):
                TR(pbv[:, (3 + bl) * 128:(4 + bl) * 128], kvraw[k][:, bl * 128:(bl + 1) * 128], ident_b,
                   [b_kvraw[k], b_const], PBb[7], False)
            ts = slice(tau * 128, (tau + 1) * 128)
            ACT(KT_all[:, :, ts], pbv[:, 0:384].rearrange("p (a t) -> p a t", a=3), AF.Copy, [PBb[7]], [], pwrites=[b_KV])
            DVE("tensor_copy", [PBb[7]], [], kvT_raw[:, :, ts], pbv[:, 384:640].rearrange("p (a t) -> p a t", a=2),
                pwrites=[b_kvT])

        for t_ in range(min(3, nt1b)):
            load_norm(t_, t_ % 3, gmix, b_gmix)
        for t_ in range(min(2, nt1b)):
            norm_T(t_ % 3)
        stage1(0)
        for tau in range(nt1b):
            if tau + 3 < nt1b:
                load_norm(tau + 3, (tau + 3) % 3, gmix, b_gmix)
            if tau + 2 < nt1b:
                norm_T((tau + 2) % 3)
            if tau + 1 < nt1b:
                stage1(tau + 1)
            if tau >= 1:
                stage2(tau - 1)
        stage2(nt1b - 1)
        S.barrier()
        A.off = m1b2

        if stop_after == "1b":
            b_out = Buf("out")
            tmpf = A.alloc([2048], F32)
            bt_ = Buf()
            DVE("tensor_copy", [b_KV], [bt_], tmpf[:, 0:128], KT_slc[:, 384:512])
            DVE("tensor_copy", [b_KV], [], tmpf[:, 128:256], KT_win[:, 384:512], pwrites=[bt_])
            DVE("tensor_copy", [b_KV], [], tmpf[:, 256:384], KT_swa[:, 384:512], pwrites=[bt_])
            DVE("tensor_copy", [b_KV], [], tmpf[:, 384:514], V1_slc[:, 3, :, :].rearrange("p g d -> p (g d)"), pwrites=[bt_])
            DVE("tensor_copy", [b_KV], [], tmpf[:, 514:644], V1_win[:, 3, :, :].rearrange("p g d -> p (g d)"), pwrites=[bt_])
            DVE("tensor_copy", [b_KV], [], tmpf[:, 644:709], V1_swa[:, 3, :], pwrites=[bt_])
            DVE("tensor_copy", [b_kvT], [], tmpf[:, 709:837], kvT_raw[:, 0, 384:512], pwrites=[bt_])
            DVE("tensor_copy", [b_kvT], [], tmpf[:, 837:965], kvT_raw[:, 1, 384:512], pwrites=[bt_])
            DVE("tensor_copy", [b_sc_tab], [], tmpf[:, 965:1029], sincos[:, 3, :], pwrites=[bt_])
            DMA("sp", out_d[0:128, :], tmpf[:, 0:1024], reads=[bt_], pwrites=[b_out], dbuf=b_out)
            DMA("sp", out_d[128:256, :], tmpf[:, 1024:2048], reads=[bt_], pwrites=[b_out], dbuf=b_out)
            S.op("sp", lambda e: e.nop(), reads=[b_out])
            S.emit()
            return nc, S, A

        m1c = A.off
        W1bd = A.alloc([32, 512], BF16)
        W2s = [A.alloc([2, 64], BF16) for _ in range(2)]
        peT = [A.alloc([32], BF16) for _ in range(2)]
        peb = A.alloc([512], BF16)
        ones_b = A.alloc([128], BF16)
        rtmp = A.alloc([2, 16, 32], F32)
        b_rtmp = Buf()
        b_w1r = Buf()
        b_w1bd = Buf()
        POOL("memset", [], [b_w1bd], W1bd[0:64, :, 256:512], 0.0)
        POOL("memset", [], [], W1bd[64:128, :, 0:256], 0.0, pwrites=[b_w1bd])
        for kv, (w2d_, ped_) in enumerate(((ck_w2, pe_k), (cv_w2, pe_v))):
            DMA("pool", W2s[kv], w2d_.rearrange("(c p) n -> p c n", p=128), pwrites=[b_w1r], dbuf=b_w1r)
            DMA("pool", peT[kv][0:64], ped_[:, :], pwrites=[b_w1r], dbuf=b_w1r)
        DVE("memset", [], [], ones_b, 1.0, pwrites=[b_w1r])
        b_peb = Buf()
        xg = A.alloc([512], F32)
        sq = A.alloc([512], F32)
        h1 = A.alloc([512], BF16)
        h1T = A.alloc([4, 128], BF16)
        kc_in = A.alloc([2, 64], F32)
        rkc = A.alloc([2, 64], BF16)
        b_xg, b_sq, b_h1, b_h1T, b_kcin, b_rkc = Buf(), Buf(), Buf(), Buf(), Buf(), Buf()
        for kv, w1d_ in enumerate((ck_w1, cv_w1)):
            src = w1d_.rearrange("(o d) n -> d o n", d=64)
            DMA("pool", W1bd[0:64, :, 0:256], src, pwrites=[b_w1bd], dbuf=b_w1bd)
            DMA("pool", W1bd[64:128, :, 256:512], src, pwrites=[b_w1bd], dbuf=b_w1bd)
            for o in range(32):
                MM(PB[2][0:1, 0:256], peT[kv][0:64, o:o + 1], W1bd[0:64, o, 0:256], o == 0, o == 31,
                   [b_w1r, b_w1bd], PBb[2])
            ACT(peb[0:1, 0:256], PB[2][0:1, 0:256], AF.Copy, [PBb[2]], [b_peb])
            ACT(peb[0:1, 256:512], PB[2][0:1, 0:256], AF.Copy, [PBb[2]], [], pwrites=[b_peb])
            def first_layer(ch):
                bk = ch % 2
                for o in range(32):
                    c0 = ch * 2048 + o
                    MM(PB[bk], kvT_raw[:, kv, c0:c0 + 2033:16], W1bd[:, o, :], o == 0, False, [b_kvT, b_w1bd], PBb[bk])
                MM(PB[bk], ones_b[0:1, :], peb[0:1, :], False, True, [b_peb, b_w1r], PBb[bk])

            def rest(ch):
                bk = ch % 2
                ACT(sq, PB[bk], AF.Square, [PBb[bk]], [b_sq])
                DVE("tensor_scalar", [b_sq], [b_sq], sq, sq, 0.044715, 1.0, ALU.mult, ALU.add)
                DVE("tensor_tensor", [b_sq, PBb[bk]], [b_xg], xg, sq, PB[bk], ALU.mult)
                ACT(xg, xg, AF.Sigmoid, [b_xg], [b_xg], scale=1.5957691216057308)
                DVE("tensor_tensor", [b_xg, PBb[bk]], [b_h1], h1, xg, PB[bk], ALU.mult)

            def second_layer(ch):
                transpose_to(h1T, h1, b_h1, 4, 7, b_h1T, evac="act")
                for g in range(2):
                    ob = 4 + g
                    for c2 in range(2):
                        MM(PB[ob][:, 0:64], h1T[:, g * 2 + c2, :], W2s[kv][:, c2, :], c2 == 0, c2 == 1,
                           [b_h1T, b_w1r], PBb[ob])
                    if kv == 0:
                        if g == 0:
                            DVE("tensor_copy", [PBb[ob]], [b_kcin], kc_in[:, 0, :], PB[ob][:, 0:64])
                        else:
                            DVE("tensor_copy", [PBb[ob]], [], kc_in[:, 1, :], PB[ob][:, 0:64], pwrites=[b_kcin])
                    else:
                        ACT(V1c[:, ch, g, 0:64], PB[ob][:, 0:64], AF.Copy, [PBb[ob]], [], pwrites=[b_KV])
                if kv == 0:
                    rope(rkc, kc_in, sincos_c[:, ch, :], 2, [b_kcin, b_sc_tab], b_rkc, rtmp, b_rtmp)
                    pbv = PB[6].bitcast(BF16)
                    TR(pbv[:, 0:128], rkc.rearrange("p h d -> p (h d)"), ident_b, [b_rkc, b_const], PBb[6], True)
                    ACT(KcT[:, ch * 128:(ch + 1) * 128], pbv[:, 0:128], AF.Copy, [PBb[6]], [], pwrites=[b_KV])

            first_layer(0)
            for ch in range(4):
                rest(ch)
                if ch + 1 < 4:
                    first_layer(ch + 1)
                second_layer(ch)
        S.barrier()
        A.off = m_pre_wkv

        if stop_after == "1c":
            b_out = Buf("out")
            tmpf = A.alloc([1024], F32)
            bt_ = Buf()
            DVE("tensor_copy", [b_KV], [bt_], tmpf[:, 0:512], KcT)
            DVE("tensor_copy", [b_KV], [], tmpf[:, 512:1024], V1c.rearrange("p c g d -> p (c g d)")[:, 0:512], pwrites=[bt_])
            DMA("sp", out_d[0:128, :], tmpf[:, 0:1024], reads=[bt_], pwrites=[b_out], dbuf=b_out)
            S.op("sp", lambda e: e.nop(), reads=[b_out])
            S.emit()
            return nc, S, A

        b_qscr, b_gab, b_ngs = Buf("qscr"), Buf("gabscr"), Buf("ngscr")
        Wq = A.alloc([8, 1048], BF16)
        Wgab = A.alloc([8, 2048], BF16)
        b_wq = Buf()
        DMA("pool", Wq[:, :, 0:512], wins[:, :, 0:512], pwrites=[b_wq], dbuf=b_wq)
        DMA("pool", Wq[:, :, 512:1024], wins[:, :, 1304:1816], pwrites=[b_wq], dbuf=b_wq)
        DMA("pool", Wq[:, :, 1024:1048], wins[:, :, 1280:1304], pwrites=[b_wq], dbuf=b_wq)
        DMA("pool", Wgab[:, :, 0:1024], wins[:, :, 1944:2968], pwrites=[b_wq], dbuf=b_wq)
        DMA("pool", Wgab[:, :, 1024:2048], wins[:, :, 2968:3992], pwrites=[b_wq], dbuf=b_wq)
        gmix = A.alloc([D], F32)
        b_gmix = Buf()
        DMA("sp", gmix, g_mix[0:1, :].broadcast_to([128, D]), writes=[b_gmix], dbuf=b_gmix)
        hb = [A.alloc([D], F32) for _ in range(2)]
        ub = [A.alloc([D], BF16) for _ in range(3)]
        uT = [A.alloc([8, 128], BF16) for _ in range(2)]
        gabs = [A.alloc([2048], BF16) for _ in range(2)]
        b_gabbs = [Buf() for _ in range(2)]
        ngss = [A.alloc([24], F32) for _ in range(2)]
        b_ngsbs = [Buf() for _ in range(2)]
        scs = [A.alloc([4], F32) for _ in range(3)]
        rtmp = A.alloc([2, 8, 32], F32)
        qin = A.alloc([16, 64], F32)
        rq = A.alloc([1024], BF16)
        qTz = A.alloc([4, 4, 128], BF16)
        b_qTz = Buf()
        POOL("memset", [], [b_qTz], qTz, 0.0)
        b_hb = [Buf() for _ in range(2)]
        b_ub = [Buf() for _ in range(3)]
        b_uT = [Buf() for _ in range(2)]
        b_junk, b_rtmp, b_qin, b_rq, b_qT, b_ngsb, b_gabb = Buf(), Buf(), Buf(), Buf(), Buf(), Buf(), Buf()
        b_scs = [Buf() for _ in range(3)]
        def stage_n1d(i):
            tau = 4 * i + 3
            kh, k3 = i % 2, i % 3
            DMA("sp", hb[kh], hscr[tau * 128:(tau + 1) * 128, :], reads=[b_hscr], writes=[b_hb[kh]], dbuf=b_hb[kh])
            rmsnorm_to_bf16(hb[kh], b_hb[kh], gmix, b_gmix, ub[k3], b_ub[k3], ub[k3], b_ub[k3], scs[k3], b_scs[k3])

        def stage_a1d(i):
            tau = 4 * i + 3
            k = i % 2
            qin = qins[k]
            b_qin = b_qins[k]
            gab, b_gabb, ngs, b_ngsb = gabs[k], b_gabbs[k], ngss[k], b_ngsbs[k]
            transpose_to(uT[k], ub[i % 3], b_ub[i % 3], 8, 6, b_uT[k], evac="act")
            for kc in range(8):
                MM(PB[5][:, 0:24], uT[k][:, kc, :], Wq[:, kc, 1024:1048], kc == 0, kc == 7, [b_uT[k], b_wq], PBb[5])
            ACT(ngs, PB[5][:, 0:24], AF.Sigmoid, [PBb[5]], [b_ngsb])
            DMA("pool", ngscr[i], ngs, reads=[b_ngsb], pwrites=[b_ngs], dbuf=b_ngs)
            for hq in range(2):
                for kc in range(8):
                    MM(PB[hq], uT[k][:, kc, :], Wq[:, kc, hq * 512:(hq + 1) * 512], kc == 0, kc == 7,
                       [b_uT[k], b_wq], PBb[hq])
            ACT(qin[:, 0:8, :], PB[0].rearrange("p (h d) -> p h d", h=8), AF.Copy, [PBb[0]], [b_qin])
            ACT(qin[:, 8:16, :], PB[1].rearrange("p (h d) -> p h d", h=8), AF.Copy, [PBb[1]], [], pwrites=[b_qin])
            for gq in range(4):
                bk = 2 + (gq % 2) if gq < 2 else 4 + (gq % 2)
                bk = [2, 3, 4, 5][gq]
                for kc in range(8):
                    MM(PB[bk], uT[k][:, kc, :], Wgab[:, kc, gq * 512:(gq + 1) * 512], kc == 0, kc == 7,
                       [b_uT[k], b_wq], PBb[bk])
                if gq == 0:
                    ACT(gab[:, 0:512], PB[bk], AF.Sigmoid, [PBb[bk]], [b_gabb])
                else:
                    ACT(gab[:, gq * 512:(gq + 1) * 512], PB[bk], AF.Sigmoid, [PBb[bk]], [], pwrites=[b_gabb])
            DMA("pool", gabscr[i], gab, reads=[b_gabb], pwrites=[b_gab], dbuf=b_gab)

        def stage_r1d(i):
            tau = 4 * i + 3
            k = i % 2
            rq = rqs[k]
            b_rq = b_rqs[k]
            qin = qins[k]
            b_qin = b_qins[k]
            for s_ in range(2):
                src = qin[:, s_ * 8:(s_ + 1) * 8, :].rearrange("p (g r) d -> p r g d", g=2)
                dstv = rq[:, s_ * 512:(s_ + 1) * 512].rearrange("p (r g d) -> p r g d", r=4, g=2)
                sc_ap = sincos[:, tau, :]
                sin_b = sc_ap[:, 0:32].unsqueeze(1).unsqueeze(1).broadcast_to([128, 4, 2, 32])
                cos_b = sc_ap[:, 32:64].unsqueeze(1).unsqueeze(1).broadcast_to([128, 4, 2, 32])
                x1, x2 = src[:, :, :, 0:32], src[:, :, :, 32:64]
                t1 = rtmp[:, 0, 0:8, :].rearrange("p (r g) d -> p r g d", g=2)
                t2 = rtmp[:, 1, 0:8, :].rearrange("p (r g) d -> p r g d", g=2)
                rb = [b_qin, b_sc_tab]
                DVE("tensor_tensor", rb, [b_rtmp], t1, x1, cos_b, ALU.mult)
                DVE("tensor_tensor", rb + [b_rtmp], [b_rtmp], t2, x2, sin_b, ALU.mult)
                if s_ == 0:
                    DVE("tensor_tensor", [b_rtmp], [b_rq], dstv[:, :, :, 0:32], t1, t2, ALU.subtract)
                else:
                    DVE("tensor_tensor", [b_rtmp], [], dstv[:, :, :, 0:32], t1, t2, ALU.subtract, pwrites=[b_rq])
                DVE("tensor_tensor", rb + [b_rtmp], [b_rtmp], t1, x2, cos_b, ALU.mult)
                DVE("tensor_tensor", rb + [b_rtmp], [b_rtmp], t2, x1, sin_b, ALU.mult)
                DVE("tensor_tensor", [b_rtmp], [], dstv[:, :, :, 32:64], t1, t2, ALU.add, pwrites=[b_rq])

        def stage_t1d(i):
            k = i % 2
            rq = rqs[k]
            b_rq = b_rqs[k]
            pbv7 = PB[7].bitcast(BF16)
            for bl in range(8):
                TR(pbv7[:, bl * 128:(bl + 1) * 128], rq[:, bl * 128:(bl + 1) * 128], ident_b, [b_rq, b_const],
                   PBb[7], bl == 0)
            for s_ in range(2):
                for g in range(2):
                    srcv = pbv7[g * 64:(g + 1) * 64, s_ * 512:(s_ + 1) * 512].rearrange("p (b t) -> p b t", b=4)
                    dstv = qTz[g * 64:(g + 1) * 64, 2 * s_ + g, :, :]
                    if g == 0:
                        ACT(dstv, srcv, AF.Copy, [PBb[7]], [], pwrites=[b_qTz])
                    else:
                        DVE("tensor_copy", [PBb[7]], [], dstv, srcv, pwrites=[b_qTz])
            DMA("pool", qscr[i], qTz.rearrange("p a b t -> p (a b t)"), reads=[b_qTz], pwrites=[b_qscr], dbuf=b_qscr)


        qins = [qin, A.alloc([16, 64], F32)]
        b_qins = [b_qin, Buf()]
        rqs = [rq, A.alloc([1024], BF16)]
        b_rqs = [b_rq, Buf()]
        for i_ in range(min(3, nown)):
            stage_n1d(i_)
        stage_a1d(0)
        if nown > 1:
            stage_a1d(1)
        stage_r1d(0)
        for i in range(nown):
            if i + 3 < nown:
                stage_n1d(i + 3)
            if i + 2 < nown:
                stage_a1d(i + 2)
            if i + 1 < nown:
                stage_r1d(i + 1)
            stage_t1d(i)
        S.barrier()
        A.off = m_res

        if stop_after == "1d":
            b_out = Buf("out")
            tmpb = A.alloc([1024], BF16)
            tmpf = A.alloc([1024], F32)
            bt_, bt2 = Buf(), Buf()
            DMA("sp", tmpb, qscr[0][:, 0:1024], reads=[b_qscr], writes=[bt_], dbuf=bt_)
            DVE("tensor_copy", [bt_], [bt2], tmpf, tmpb)
            DMA("sp", out_d[0:128, :], tmpf, reads=[bt2], pwrites=[b_out], dbuf=b_out)
            S.op("sp", lambda e: e.nop(), reads=[b_out])
            S.emit()
            return nc, S, A

        KE1 = A.alloc([NT * 128], BF16)
        KE = [KT_slc, KE1]
        DVE("tensor_copy", [b_KV], [], KE1[64:128, 0:NT * 64], KT_slc[64:128, 0:NT * 64], pwrites=[b_KV])
        ACT(KE1[64:128, NT * 64:NT * 128], KT_slc[64:128, NT * 64:NT * 128], AF.Copy, [b_KV], [], pwrites=[b_KV])
        Wa = A.alloc([4, D], BF16)
        Wb = A.alloc([4, D], BF16)
        Wo = A.alloc([8, D], BF16)
        b_E = Buf("E")
        causal_m = A.alloc([128], BF16)
        strict_m = A.alloc([128], BF16)
        causal_rep = causal_m.unsqueeze(1).broadcast_to([128, 4, 128])
        strict_rep = strict_m.unsqueeze(1).broadcast_to([128, 4, 128])
        sinkexp = A.alloc([8], F32)
        b_masks = Buf("masks")
        b_mrep = Buf("mrep")
        DMA("pool", causal_m, cmask_d[:, 0, :], pwrites=[b_mrep], dbuf=b_mrep)
        DMA("pool", strict_m, cmask_d[:, 1, :], pwrites=[b_mrep], dbuf=b_mrep)
        DMA("sp", sinkexp, sinks[0:1, :].broadcast_to([128, 8]), writes=[b_masks], dbuf=b_masks)
        ACT(sinkexp, sinkexp, AF.Exp, [b_masks], [], pwrites=[b_mrep])
        qT = [A.alloc([4, 512], BF16) for _ in range(2)]
        ngs = [A.alloc([24], F32) for _ in range(2)]
        gab0 = A.alloc([2048], BF16)
        hb20 = A.alloc([D], F32)
        gab = [gab0, gab0]
        hb2 = [hb20, hb20]
        cmT_m = [A.alloc([128], BF16) for _ in range(2)]
        cmT_rep = [m_.unsqueeze(1).broadcast_to([128, 4, 128]) for m_ in cmT_m]
        QS = [[[A.alloc([512], BF16) for _ in range(2)] for _ in range(2)] for _ in range(2)]
        b_QS = [[[Buf() for _ in range(2)] for _ in range(2)] for _ in range(2)]
        selb_sw = A.alloc([128], F32)
        oab_f = [A.alloc([1024], F32) for _ in range(2)]
        b_qT = [Buf() for _ in range(2)]
        b_ngsb = [Buf() for _ in range(2)]
        b_gabb0, b_hb20 = Buf(), Buf()
        b_gabb = [b_gabb0, b_gabb0]
        b_hb2 = [b_hb20, b_hb20]
        b_cmT = [Buf() for _ in range(2)]
        b_oabf = [Buf() for _ in range(2)]
        cmpm = A.alloc([512], F32)
        keepm = A.alloc([128], F32)
        addm = A.alloc([128], F32)
        e_sb = A.alloc([512], F32)
        em = [A.alloc([512], F32) for _ in range(4)]
        P4 = A.alloc([512], F32)
        imp = A.alloc([128], F32)
        imp2 = A.alloc([128], F32)
        tmpk = A.alloc([128], F32)
        m8a = A.alloc([8], F32)
        m8b = A.alloc([8], F32)
        rs = A.alloc([4], F32)
        rinv = A.alloc([4], F32)
        selb = A.alloc([128], F32)
        otmp = A.alloc([4, 64], F32)
        fac4 = A.alloc([4], F32)
        b_tile, b_e, b_P4, b_imp, b_sel, b_rs = Buf(), Buf(), Buf(), Buf(), Buf(), Buf()
        b_em = [Buf() for _ in range(4)]
        b_otmp, b_fac = Buf(), Buf()
        m1, m2 = em[0], em[1]
        b_m1, b_m2 = b_em[0], b_em[1]
        oT = em[2].bitcast(BF16).rearrange("p (b t) -> p b t", b=8)
        mT = em[3].bitcast(BF16).rearrange("p (b t) -> p b t", b=8)
        b_oT, b_mT = b_em[2], b_em[3]
        oab = e_sb.bitcast(BF16)
        merged = P4.bitcast(BF16)
        b_oab, b_mg = b_e, b_P4
        b_h2scr = Buf("h2scr")
        acc_cnt = [0]
        oT_cnt = [0]
        oTs = [A.alloc([512], F32) for _ in range(2)]
        b_oTs = [Buf() for _ in range(2)]
        pT2 = [A.alloc([1024], BF16) for _ in range(2)]
        b_pT2 = [[Buf(), Buf()] for _ in range(2)]

        def next_ob():
            ob = 4 + acc_cnt[0] % 2
            acc_cnt[0] += 1
            return ob

        def combine(par, ob, g, gate_ap, sink_ap, dst, first):
            v = PB[ob][:, 0:260].rearrange("p (r e) -> p r e", e=65)
            den = v[:, :, 64]
            num = v[:, :, 0:64]
            if sink_ap is not None:
                DVE("tensor_tensor", [PBb[ob], b_mrep], [b_fac], fac4, den, sink_ap, ALU.add)
            else:
                DVE("tensor_scalar", [PBb[ob]], [b_fac], fac4, den, 1e-30, None, ALU.max)
            DVE("reciprocal", [b_fac], [b_fac], fac4, fac4)
            if gate_ap is not None:
                DVE("tensor_tensor", [b_fac, b_ngsb[par]], [b_fac], fac4, fac4, gate_ap, ALU.mult)
            tgt = dst[:, g * 256:(g + 1) * 256].rearrange("p (r d) -> p r d", r=4)
            fbc = fac4.unsqueeze(2).broadcast_to([128, 4, 64])
            if first:
                DVE("tensor_tensor", [PBb[ob], b_fac], [], tgt, num, fbc, ALU.mult, pwrites=[b_oabf[par]])
            else:
                DVE("tensor_tensor", [PBb[ob], b_fac], [b_otmp], otmp, num, fbc, ALU.mult)
                DVE("tensor_tensor", [b_otmp, b_oabf[par]], [], tgt, tgt, otmp, ALU.add, pwrites=[b_oabf[par]])

        def run_groups(par, groups, side):
            units = []
            for (g, qblk0, items, gate_ap, sink_ap, dst, first) in groups:
                ob = next_ob()
                n = len(items)
                for idx, it in enumerate(items):
                    units.append((g, qblk0, ob, idx, n, it, (gate_ap, sink_ap, dst, first)))
            pending = []
            LOOK = 3

            def stage_a(j, u):
                g, qblk0, ob, idx, n, (KT_ap, V_ap, bias_ap, sel, mask), _ = u
                bk = j % 4
                pv = pT2[bk // 2][:, (bk % 2) * 512:(bk % 2 + 1) * 512]
                pb_ = b_pT2[bk // 2][bk % 2]
                if sel is None:
                    rhs = qT[par][:, (qblk0 // 4) * 2 + g, :]
                    MM(PB[bk], KT_ap, rhs, True, True, [b_KV, b_qT[par]], PBb[bk])
                else:
                    w_ = sel // 32
                    MM(PB[bk], KT_ap, QS[par][g][w_], True, True, [b_KV, b_QS[par][g][w_]], PBb[bk])
                if bias_ap is None:
                    ACT(pv, PB[bk], AF.Exp, [PBb[bk]], [pb_], scale=0.125)
                else:
                    ACT(pv, PB[bk], AF.Exp, [PBb[bk], b_tab], [pb_], bias=bias_ap, scale=0.125)
                if mask is not None:
                    pv3 = pv.rearrange("p (b t) -> p b t", b=4)
                    DVE("tensor_tensor", [pb_, b_mrep, b_cmT[par]], [pb_], pv3, pv3, mask, ALU.mult)

            def stage_b(j, u):
                g, qblk0, ob, idx, n, (KT_ap, V_ap, bias_ap, sel, mask), (gate_ap, sink_ap, dst, first) = u
                bk = j % 4
                pv = pT2[bk // 2][:, (bk % 2) * 512:(bk % 2 + 1) * 512]
                pb_ = b_pT2[bk // 2][bk % 2]
                MM(PB[ob][0:65, :], V_ap, pv, idx == 0, idx == n - 1, [pb_, b_KV], PBb[ob])
                if idx == n - 1:
                    k2 = oT_cnt[0] % 2
                    oT_cnt[0] += 1
                    DVE("tensor_copy", [PBb[ob]], [b_oTs[k2]], oTs[k2][0:65, :], PB[ob][0:65, :])
                    pending.append((j + 3, k2, g, gate_ap, sink_ap, dst, first))

            def flush(j, force=False):
                while pending and (force or pending[0][0] <= j):
                    _, k2, g, gate_ap, sink_ap, dst, first = pending.pop(0)
                    for r in range(4):
                        TR(PB[7][:, r * 65:(r + 1) * 65], oTs[k2][0:65, r * 128:(r + 1) * 128], ident_f[0:65, 0:65],
                           [b_oTs[k2], b_const], PBb[7], r == 0)
                    combine(par, 7, g, gate_ap, sink_ap, dst, first)

            nu = len(units)
            njobs = len(side)
            stride = max(1, (nu + LOOK) // max(1, njobs))
            burst = max(1, -(-njobs // (nu + LOOK)))
            for j in range(nu + LOOK):
                if j < nu:
                    stage_a(j, units[j])
                if j - LOOK >= 0:
                    stage_b(j - LOOK, units[j - LOOK])
                    flush(j - LOOK)
                if j % stride == stride - 1:
                    for _ in range(burst):
                        if side:
                            job = side.pop(0)
                            if job is not None:
                                job()
            flush(0, force=True)
            while side:
                job = side.pop(0)
                if job is not None:
                    job()

        def prologue_jobs(i):
            par = i % 2
            tau = 4 * i + 3
            ncol = 32 * i + 32
            nblk = ncol // 4
            jobs = []

            def loads():
                DMA("sp", qT[par].rearrange("p a t -> p (a t)"), qscr[i], reads=[b_qscr], writes=[b_qT[par]], dbuf=b_qT[par])
                DMA("sp", ngs[par], ngscr[i], reads=[b_ngs], writes=[b_ngsb[par]], dbuf=b_ngsb[par])
                DMA("pool", cmT_m[par], cmpmT_d[i], writes=[b_cmT[par]], dbuf=b_cmT[par])
                for w_ in range(2):
                    DMA("sp", QS[par][0][w_][0:64, :], qscr[i][0:64, 0:512], reads=[b_qscr], pwrites=[b_QS[par][0][w_]],
                        dbuf=b_QS[par][0][w_])
                    DMA("sp", QS[par][1][w_][64:128, :], qscr[i][64:128, 512:1024], reads=[b_qscr],
                        pwrites=[b_QS[par][1][w_]], dbuf=b_QS[par][1][w_])
                DMA("sp", cmpm, cmpm_d[i], writes=[b_tile], dbuf=b_tile)
                DMA("sp", keepm, keep_d[i], pwrites=[b_tile], dbuf=b_tile)
                DMA("sp", addm, addm_d[i], pwrites=[b_tile], dbuf=b_tile)
            jobs.append(loads)

            def score(g, r):
                def f():
                    bk = 6 if r % 2 == 0 else 7
                    MM(PB[bk][:, 0:ncol], qT[par][:, g, r * 128:(r + 1) * 128], KcT[:, 0:ncol], True, True,
                       [b_qT[par], b_KV], PBb[bk])
                    ACT(e_sb[:, 0:ncol], PB[bk][:, 0:ncol], AF.Exp, [PBb[bk]], [b_e], scale=0.125)
                    DVE("scalar_tensor_tensor", [b_e, b_tile], [b_em[r]], em[r][:, 0:ncol], e_sb[:, 0:ncol], 1.0,
                        cmpm[:, 0:ncol], ALU.mult, ALU.mult, pwrites=[b_rs], accum_out=rs[:, r:r + 1])
                return f

            def chain(g):
                P4v = P4[:, 0:ncol].rearrange("p (b f) -> p b f", f=4)

                def fa():
                    DVE("tensor_scalar", [b_rs], [b_rs], rs, rs, 1e-30, None, ALU.max)
                    DVE("reciprocal", [b_rs], [b_rs], rinv, rs)
                    DVE("tensor_scalar", [b_em[0], b_rs], [b_P4], P4[:, 0:ncol], em[0][:, 0:ncol], rinv[:, 0:1], None, ALU.mult)
                    for r in range(1, 4):
                        DVE("scalar_tensor_tensor", [b_em[r], b_rs, b_P4], [b_P4], P4[:, 0:ncol], em[r][:, 0:ncol],
                            rinv[:, r:r + 1], P4[:, 0:ncol], ALU.mult, ALU.add)

                def fb():
                    DVE("memset", [], [b_imp], imp, 0.0)
                    DVE("tensor_reduce", [b_P4, b_imp], [b_imp], imp[:, 0:nblk], P4v, AX.X, ALU.add)
                    DVE("scalar_tensor_tensor", [b_P4, b_imp], [b_imp], imp[:, 0:nblk], P4v[:, :, 3], -0.5, imp[:, 0:nblk],
                        ALU.mult, ALU.add)
                    DVE("scalar_tensor_tensor", [b_P4, b_imp], [b_imp], imp[:, 1:nblk], P4v[:, 0:nblk - 1, 3], 0.5,
                        imp[:, 1:nblk], ALU.mult, ALU.add)
                    DVE("tensor_tensor", [b_imp, b_tile], [b_imp], imp2, imp, keepm, ALU.mult)
                    DVE("tensor_tensor", [b_imp, b_tile], [b_imp], imp2, imp2, addm, ALU.add)

                def fc():
                    DVE("max", [b_imp], [b_sel], m8a, imp2)
                    DVE("match_replace", [b_imp, b_sel], [b_sel], tmpk, m8a, imp2, -3e9)
                    DVE("max", [b_sel], [b_sel], m8b, tmpk)
                    DVE("tensor_scalar", [b_imp, b_sel], [b_sel], selb, imp2, m8b[:, 7:8], None, ALU.is_ge)
                    DVE("tensor_scalar", [b_sel], [b_sel], selb, selb, 1.0, -NEGB, ALU.subtract, ALU.mult)
                return [fa, fb, fc]

            def seltr(g):
                def f():
                    DVE("tensor_copy", [b_sel], [], selb_sw[:, 0:64], selb[:, 64:128], pwrites=[b_sel])
                    DVE("tensor_copy", [b_sel], [], selb_sw[:, 64:128], selb[:, 0:64], pwrites=[b_sel])
                    TR(PB[7][:, 0:128], selb, ident_f, [b_sel, b_const], PBb[7], True)
                    TR(PB[7][:, 128:256], selb_sw, ident_f, [b_sel, b_const], PBb[7], False)
                    rows = slice(64, 128) if g == 0 else slice(0, 64)
                    for w_ in range(2):
                        nat = (w_ == 1) if g == 0 else (w_ == 0)
                        c0 = 0 if nat else 128
                        srcv = PB[7][rows, c0:c0 + 128].unsqueeze(1).broadcast_to([64, 4, 128])
                        dstv = QS[par][g][w_][rows, :].rearrange("p (b t) -> p b t", b=4)
                        if w_ == 0:
                            ACT(dstv, srcv, AF.Copy, [PBb[7]], [], pwrites=[b_QS[par][g][w_]])
                        else:
                            DVE("tensor_copy", [PBb[7]], [], dstv, srcv, pwrites=[b_QS[par][g][w_]])
                return f

            for g in range(2):
                for r in range(4):
                    jobs.append(score(g, r))
                jobs += chain(g)
                jobs += [None] * 3
                jobs.append(seltr(g))
            return jobs

        def epilogue_jobs(i):
            par = i % 2
            tau = 4 * i + 3
            jobs = []

            def j0():
                DMA("sp", gab[par], gabscr[i], reads=[b_gab], writes=[b_gabb[par]], dbuf=b_gabb[par])
                DMA("sp", hb2[par], hscr[tau * 128:(tau + 1) * 128, :], reads=[b_hscr], writes=[b_hb2[par]], dbuf=b_hb2[par])
                DVE("tensor_copy", [b_oabf[par]], [b_oab], oab, oab_f[par])
            jobs.append(j0)
            jobs.append(lambda: transpose_to(oT, oab, b_oab, 8, 6, b_oT, evac="dve"))

            def ab(hf):
                def f():
                    for c in range(4):
                        MM(PB[6], oT[:, c, :], Wa[:, c, hf * 512:(hf + 1) * 512], c == 0, c == 3, [b_oT, b_E], PBb[6])
                    for c in range(4):
                        MM(PB[7], oT[:, 4 + c, :], Wb[:, c, hf * 512:(hf + 1) * 512], c == 0, c == 3, [b_oT, b_E], PBb[7])
                    DVE("tensor_tensor", [PBb[6], b_gabb[par]], [b_m1], m1, PB[6], gab[par][:, hf * 512:(hf + 1) * 512], ALU.mult)
                    DVE("tensor_tensor", [PBb[7], b_gabb[par]], [b_m2], m2, PB[7],
                        gab[par][:, 1024 + hf * 512:1024 + (hf + 1) * 512], ALU.mult)
                    if hf == 0:
                        DVE("tensor_tensor", [b_m1, b_m2], [b_mg], merged[:, 0:512], m1, m2, ALU.add)
                    else:
                        DVE("tensor_tensor", [b_m1, b_m2], [], merged[:, 512:1024], m1, m2, ALU.add, pwrites=[b_mg])
                return f
            jobs.append(None)
            jobs.append(ab(0))
            jobs.append(ab(1))
            jobs += [None] * 2
            jobs.append(lambda: transpose_to(mT, merged, b_mg, 8, 6, b_mT, evac="dve"))
            jobs.append(None)

            def wo(hf):
                def f():
                    bk = 6 + hf
                    for kc in range(8):
                        MM(PB[bk], mT[:, kc, :], Wo[:, kc, hf * 512:(hf + 1) * 512], kc == 0, kc == 7, [b_mT, b_E], PBb[bk])
                    DVE("tensor_tensor", [PBb[bk], b_hb2[par]], [], hb2[par][:, hf * 512:(hf + 1) * 512], PB[bk],
                        hb2[par][:, hf * 512:(hf + 1) * 512], ALU.add, pwrites=[b_hb2[par]])
                    if hf == 1:
                        DMA("pool", h2scr[i * 128:(i + 1) * 128, :], hb2[par], reads=[b_hb2[par]], pwrites=[b_h2scr],
                            dbuf=b_h2scr)
                return f
            jobs.append(wo(0))
            jobs.append(wo(1))
            return jobs

        def tile_groups(i):
            par = i % 2
            tau = 4 * i + 3
            nch = (8 * tau - 1) // 128 + 1
            of = oab_f[par]
            groups = []
            for g in range(2):
                items = [(KcT[:, ch * 128:(ch + 1) * 128], V1c[:, ch, g, :], cvalid[:, ch:ch + 1], None,
                          cmT_rep[par] if ch == nch - 1 else None) for ch in range(nch)]
                groups.append((g, 0, items, ngs[par][:, 0 * 8 + g * 4:0 * 8 + g * 4 + 4], None, of[:, 0:512], True))
            for g in range(2):
                items = [(KT_win[:, kt * 128:(kt + 1) * 128], V1_win[:, kt, g, :], kvalid[:, kt:kt + 1] if kt < 3 else None, None,
                          causal_rep if kt == tau else (strict_rep if kt == tau - 4 else None))
                         for kt in range(max(0, tau - 4), tau + 1)]
                groups.append((g, 0, items, ngs[par][:, 2 * 8 + g * 4:2 * 8 + g * 4 + 4], None, of[:, 0:512], False))
            for g in range(2):
                items = [(KT_swa[:, kt * 128:(kt + 1) * 128], V1_swa[:, kt, :], kvalid[:, kt:kt + 1] if kt < 3 else None, None,
                          causal_rep if kt == tau else strict_rep) for kt in (tau - 1, tau)]
                groups.append((g, 4, items, None, sinkexp[:, g * 4:(g + 1) * 4], of[:, 512:1024], True))
            for g in range(2):
                items = [(KE[g][:, kt * 128:(kt + 1) * 128], V1_slc[:, kt, g, :], kvalid[:, kt:kt + 1] if kt < 3 else None,
                          kt if kt < tau else None, causal_rep if kt == tau else None) for kt in range(tau + 1)]
                groups.append((g, 0, items, ngs[par][:, 1 * 8 + g * 4:1 * 8 + g * 4 + 4], None, of[:, 0:512], False))
            return groups

        p0 = prologue_jobs(0)
        p0[0]()
        DMA("pool", KE1[0:64, :], eall_d[:, :], pwrites=[b_KV], dbuf=b_KV)
        DMA("pool", KT_slc[64:128, :], eall_d[:, :], pwrites=[b_KV], dbuf=b_KV)
        DMA("pool", Wa, w_a.rearrange("(c p) n -> p c n", p=128), pwrites=[b_E], dbuf=b_E)
        DMA("pool", Wb, w_b.rearrange("(c p) n -> p c n", p=128), pwrites=[b_E], dbuf=b_E)
        DMA("pool", Wo, w_o.rearrange("(c p) n -> p c n", p=128), pwrites=[b_E], dbuf=b_E)
        g0 = tile_groups(0)
        run_groups(0, g0[:6], [j_ for j_ in p0[1:] if j_ is not None])
        side = prologue_jobs(1) if nown > 1 else []
        run_groups(0, g0[6:], side)
        for i in range(1, nown):
            side = epilogue_jobs(i - 1)
            if i + 1 < nown:
                side += prologue_jobs(i + 1)
            run_groups(i % 2, tile_groups(i), side)
        for job in epilogue_jobs(nown - 1):
            if job is not None:
                job()
        S.barrier()
        A.off = base_mark

        b_out = Buf("out")
        gfin = A.alloc([D], F32)
        b_gfin = Buf()
        DMA("sp", gfin, g_fin[0:1, :].broadcast_to([128, D]), writes=[b_gfin], dbuf=b_gfin)
        fjunk = A.alloc([D], BF16)
        fsc = A.alloc([4], F32)
        b_fj, b_fsc = Buf(), Buf()

        def fin3(t, h_sb, hbuf):
            ACT(fjunk, h_sb, AF.Square, [hbuf], [b_fj, b_fsc], accum=fsc[:, 0:1])
            DVE("tensor_scalar", [b_fsc], [b_fsc], fsc[:, 1:2], fsc[:, 0:1], 1.0 / D, EPS, ALU.mult, ALU.add)
            ACT(fsc[:, 2:3], fsc[:, 1:2], AF.Sqrt, [b_fsc], [b_fsc])
            DVE("reciprocal", [b_fsc], [b_fsc], fsc[:, 3:4], fsc[:, 2:3])
            DVE("scalar_tensor_tensor", [hbuf, b_fsc, b_gfin], [], h_sb, h_sb, fsc[:, 3:4], gfin, ALU.mult, ALU.mult,
                pwrites=[hbuf])
            DMA("pool", out_d[t * 128:(t + 1) * 128, :], h_sb, reads=[hbuf], pwrites=[b_out], dbuf=b_out)

        b_hscr = b_h2scr
        ffn_phase(nown, h2scr, g_ffn2, w2g, w2u, w2d, fin3)
        S.op("sp", lambda e: e.nop(), reads=[b_out])
        S.emit()
        return nc, S, A


def _tables(c):
    sh = 3 - c
    t = {}
    t["ident"] = np.eye(128, dtype=np.float32)
    half = 32
    invf = (np.float32(10000.0) ** (-np.arange(half, dtype=np.float32) / np.float32(half))).astype(np.float32)
    t["invf2"] = np.concatenate([invf, invf])[None, :].astype(np.float32)
    t["phase"] = np.concatenate([np.zeros(32), np.full(32, np.pi / 2)])[None, :].astype(np.float32)
    t["eall"] = np.tile(np.repeat(np.eye(64, dtype=np.float32), 64, axis=1), (1, 2))
    k = np.arange(128)[:, None]
    q = np.arange(128)[None, :]
    t["cmask"] = np.stack([(k <= q), (k > q)], axis=1).astype(np.float32)
    tau = np.arange(NT)
    t["kvalid"] = np.broadcast_to(np.where(tau - sh >= 0, 0.0, NEGB)[None, :], (128, NT)).astype(np.float32).copy()
    jr = np.arange(512)
    gj = jr - 8 * sh
    cval = (gj >= 0) & (gj < 511)
    t["cvalid"] = np.where(cval, 0.0, NEGB).reshape(4, 128).T.astype(np.float32).copy()
    cmpmT = np.zeros((NOWN, 128, 128), np.float32)
    cmpm = np.zeros((NOWN, 128, 512), np.float32)
    keep = np.zeros((NOWN, 128, 128), np.float32)
    addm = np.zeros((NOWN, 128, 128), np.float32)
    for i in range(NOWN):
        ta = 4 * i + 3
        ch = (8 * ta - 1) // 128
        jl = np.arange(128)
        jrr = ch * 128 + jl
        cmpmT[i] = (16 * jrr[:, None] + 31 <= 128 * ta + np.arange(128)[None, :]).astype(np.float32)
        vis = (16 * jr[None, :] + 31 <= 128 * ta + np.arange(128)[:, None])
        cmpm[i] = (vis & cval[None, :]).astype(np.float32)
        blk = np.arange(128)[None, :]
        tb = 2 * ta + (np.arange(128)[:, None] >= 64)
        g = blk - 2 * sh
        invalid = g < 0
        forced = (g == 0) | (blk == tb) | (blk == tb - 1)
        future = blk > tb
        kp = np.ones((128, 128), np.float32)
        ad = np.zeros((128, 128), np.float32)
        kp[np.broadcast_to(future, kp.shape)] = 0
        ad[np.broadcast_to(future, kp.shape)] = -FORCE
        kp[forced] = 0
        ad[forced] = FORCE
        kp[np.broadcast_to(invalid, kp.shape)] = 0
        ad[np.broadcast_to(invalid, kp.shape)] = -FORCE
        keep[i] = kp
        addm[i] = ad
    t["cmpmT"] = cmpmT
    t["cmpm"] = cmpm
    t["keepm"] = keep
    t["addm"] = addm
    return t


def _core_inputs(inp, core):
    b, c = core // 4, core % 4
    sh = 3 - c
    x = np.asarray(inp["x"])
    pos = np.asarray(inp["positions"])
    m = {}
    xr = np.zeros((NT * 128, D), np.float32)
    pr = np.zeros((NT * 128,), np.int32)
    if sh > 0:
        xr[sh * 128:] = x[b, :(NT - sh) * 128]
        pr[sh * 128:] = pos[b, :(NT - sh) * 128]
    else:
        xr[:] = x[b]
        pr[:] = pos[b]
    m["xrel"] = xr
    posrel = pr.reshape(NT, 128).T
    jr = np.arange(512)
    gtok = 16 * jr + 31 - sh * 128
    ok = (16 * jr - sh * 128 >= 0) & (gtok < 8192)
    pc = np.zeros(512, np.int32)
    pc[ok] = pos[b, gtok[ok]]
    m["posall"] = np.ascontiguousarray(np.concatenate([posrel, pc.reshape(4, 128).T], axis=1))
    f = lambda k: np.ascontiguousarray(np.asarray(inp[k])[0])
    m["g_ffn1"] = f("norm_ffn1")[None, :]
    m["g_mix"] = f("norm_mix")[None, :]
    m["g_ffn2"] = f("norm_ffn2")[None, :]
    m["g_fin"] = np.ascontiguousarray(np.asarray(inp["norm_final"]))[None, :]
    m["w1g"], m["w1u"], m["w1d"] = f("ffn1_gate"), f("ffn1_up"), f("ffn1_down")
    m["w2g"], m["w2u"], m["w2d"] = f("ffn2_gate"), f("ffn2_up"), f("ffn2_down")
    m["w_in"] = f("w_in")
    m["pe_k"] = np.ascontiguousarray(f("cmp_pe_k").T)
    m["pe_v"] = np.ascontiguousarray(f("cmp_pe_v").T)
    m["ck_w1"], m["ck_w2"] = f("cmp_k_w1"), f("cmp_k_w2")
    m["cv_w1"], m["cv_w2"] = f("cmp_v_w1"), f("cmp_v_w2")
    m["sinks"] = f("swa_sinks")[None, :]
    m["w_a"], m["w_b"], m["w_o"] = f("w_branch_a"), f("w_branch_b"), f("w_out")
    m.update(_tables(c))
    return m


_PROG = {}


def kernel(**inputs):
    if "nc" not in _PROG:
        _PROG["nc"] = build_program()[0]
    nc = _PROG["nc"]
    in_maps = [_core_inputs(inputs, core) for core in range(8)]
    res = run_bass_kernel_spmd(nc, in_maps, core_ids=list(range(8)))
    out = np.zeros((2, 8192, D), np.float32)
    for core in range(8):
        b, c = core // 4, core % 4
        r = np.asarray(res.results[core]["out"]).reshape(NOWN, 128, D)
        for i in range(NOWN):
            t = 4 * i + c
            out[b, t * 128:(t + 1) * 128] = r[i]
    return out
```

```python
import contextlib
import numpy as np
import concourse.bass as bass
import concourse.mybir as mybir
from concourse.bass_utils import run_bass_kernel_spmd

F32 = mybir.dt.float32
BF16 = mybir.dt.bfloat16
I32 = mybir.dt.int32
AF = mybir.ActivationFunctionType
ALU = mybir.AluOpType
AX = mybir.AxisListType

D = 1024
DFF = 2816
NF = DFF // 128
NT = 64
NOWN = 16
WIN_W = 3992
EPS = 1e-6
FORCE = 1e9
NEGB = -30000.0
SEG = 8000
ARENA_WORDS = 53200


class Buf:
    __slots__ = ("name", "writers", "readers", "dsem", "dcount", "shadow", "excl")

    def __init__(self, name="", excl=False):
        self.name = name
        self.excl = excl
        self.writers = []
        self.readers = []
        self.dsem = None
        self.dcount = 0
        self.shadow = None


class Op:
    __slots__ = ("eng", "fn", "waits", "idx", "dma", "dbuf", "dval")

    def __init__(self, eng, fn, dma):
        self.eng = eng
        self.fn = fn
        self.dma = dma
        self.waits = []
        self.idx = -1
        self.dbuf = None
        self.dval = 0


class Sched:
    ENGS = ("pe", "act", "dve", "pool", "sp")

    def __init__(self, nc):
        self.nc = nc
        self.ops = {e: [] for e in self.ENGS}
        self.seen = {e: {} for e in self.ENGS}
        self.dbufs = []
        self.cnt = {e: 0 for e in self.ENGS}
        self.last = {e: None for e in self.ENGS}

    def _need(self, op, dep):
        if dep.dma:
            b = dep.dbuf
            return (("d", id(b)), b, b.dcount * 16)
        if dep.eng == op.eng and dep.eng == "pe":
            return None
        seg = dep.idx // SEG
        return (("e", dep.eng, seg), None, dep.idx % SEG + 1)

    def op(self, eng, fn, reads=(), writes=(), pwrites=(), dma=False, dbuf=None, extra=()):
        o = Op(eng, fn, dma)
        deps = list(extra)
        for b in reads:
            deps.extend(b.writers)
            if b.excl:
                deps.extend(r for r in b.readers if r.eng != eng)
        for b in pwrites:
            deps.extend(b.readers)
            if b.writers:
                deps.append(b.writers[0])
        for b in writes:
            deps.extend(b.writers)
            deps.extend(b.readers)
        if dma:
            if eng == "pool":
                if dbuf.shadow is None:
                    dbuf.shadow = Buf(dbuf.name + "_sw")
                dbuf = dbuf.shadow
            if dbuf.dsem is None:
                dbuf.dsem = True
                self.dbufs.append(dbuf)
            dbuf.dcount += 1
            o.dbuf = dbuf
            o.dval = dbuf.dcount * 16
        else:
            o.idx = self.cnt[eng]
            self.cnt[eng] += 1
            self.last[eng] = o
        seen = self.seen[eng]
        need = {}
        for d in deps:
            if d is o:
                continue
            r = self._need(o, d)
            if r is None:
                continue
            key, b, val = r
            if dma and d.dma and d.dbuf is dbuf and val == o.dval:
                val -= 16
                if val <= 0:
                    continue
            if key not in need or need[key][1] < val:
                need[key] = (b, val)
        for key, (b, val) in need.items():
            if seen.get(key, 0) >= val:
                continue
            seen[key] = val
            o.waits.append((key, b, val))
        self.ops[eng].append(o)
        for b in reads:
            b.readers.append(o)
        for b in pwrites:
            b.writers.append(o)
        for b in writes:
            b.writers = [o]
            b.readers = []
        return o

    def barrier(self):
        lasts = [o for o in self.last.values() if o is not None]
        dmas = []
        for b in self.dbufs:
            f = Op("sp", None, True)
            f.dbuf = b
            dmas.append(f)
        for e in self.ENGS:
            self.op(e, lambda eng: eng.nop(), extra=lasts + dmas)

    def emit(self):
        nc = self.nc
        with contextlib.ExitStack() as st:
            esems = {}
            for e in self.ENGS:
                n = self.cnt[e]
                for s in range((n + SEG - 1) // SEG):
                    esems[(e, s)] = st.enter_context(nc.semaphore(f"se_{e}_{s}"))
            for i, b in enumerate(self.dbufs):
                b.dsem = st.enter_context(nc.semaphore(f"sd_{i}"))
            self.nsem = len(esems) + len(self.dbufs)
            block = st.enter_context(nc.Block())

            def run(engname, eng):
                cnt = 0
                for o in self.ops[engname]:
                    for key, b, val in o.waits:
                        if key[0] == "d":
                            eng.wait_ge(b.dsem, val)
                        else:
                            eng.wait_ge(esems[(key[1], key[2])], val)
                    ins = o.fn(eng)
                    if o.dma:
                        ins.then_inc(o.dbuf.dsem, 16)
                    else:
                        ins.then_inc(esems[(engname, cnt // SEG)], 1)
                        cnt += 1

            @block.tensor
            def _(eng):
                run("pe", eng)

            @block.scalar
            def _(eng):
                run("act", eng)

            @block.vector
            def _(eng):
                run("dve", eng)

            @block.gpsimd
            def _(eng):
                run("pool", eng)

            @block.sync
            def _(eng):
                run("sp", eng)


class Arena:
    def __init__(self, base, nwords):
        self.base = base
        self.n = nwords
        self.off = 0
        self.peak = 0

    def alloc(self, shape, dtype):
        free = int(np.prod(shape))
        words = free if dtype in (F32, I32) else (free + 1) // 2
        words = (words + 7) // 8 * 8
        assert self.off + words <= self.n, f"arena overflow {self.off}+{words}>{self.n}"
        v = self.base[:, self.off:self.off + words]
        self.off += words
        self.peak = max(self.peak, self.off)
        if dtype == BF16:
            v = v.bitcast(BF16)
        elif dtype == I32:
            v = v.bitcast(I32)
        v = v[:, 0:free]
        if len(shape) > 1:
            names = [f"a{i}" for i in range(len(shape))]
            kw = {n: int(s) for n, s in zip(names, shape)}
            v = v.rearrange(f"p ({' '.join(names)}) -> p {' '.join(names)}", **kw)
        return v


def build_program(stop_after=None, dbg=False, skip1a=False, nt1b=NT, nown=NOWN):
    nc = bass.Bass("TRN2", target_bir_lowering=False)

    def din(name, shape, dt=F32):
        return nc.dram_tensor(name, list(shape), dt, kind="ExternalInput").ap()

    xrel = din("xrel", [NT * 128, D])
    posall = din("posall", [128, NT + 4], I32)
    g_ffn1 = din("g_ffn1", [1, D])
    g_mix = din("g_mix", [1, D])
    g_ffn2 = din("g_ffn2", [1, D])
    g_fin = din("g_fin", [1, D])
    w1g = din("w1g", [D, DFF])
    w1u = din("w1u", [D, DFF])
    w1d = din("w1d", [DFF, D])
    w2g = din("w2g", [D, DFF])
    w2u = din("w2u", [D, DFF])
    w2d = din("w2d", [DFF, D])
    w_in = din("w_in", [D, WIN_W])
    pe_k = din("pe_k", [64, 32])
    pe_v = din("pe_v", [64, 32])
    ck_w1 = din("ck_w1", [2048, 256])
    ck_w2 = din("ck_w2", [256, 64])
    cv_w1 = din("cv_w1", [2048, 256])
    cv_w2 = din("cv_w2", [256, 64])
    sinks = din("sinks", [1, 8])
    w_a = din("w_a", [512, D])
    w_b = din("w_b", [512, D])
    w_o = din("w_o", [D, D])
    ident_d = din("ident", [128, 128])
    invf2_d = din("invf2", [1, 64])
    phase_d = din("phase", [1, 64])
    eall_d = din("eall", [64, NT * 128])
    cmask_d = din("cmask", [128, 2, 128])
    kvalid_d = din("kvalid", [128, NT])
    cvalid_d = din("cvalid", [128, 4])
    cmpmT_d = din("cmpmT", [NOWN, 128, 128])
    cmpm_d = din("cmpm", [NOWN, 128, 512])
    keep_d = din("keepm", [NOWN, 128, 128])
    addm_d = din("addm", [NOWN, 128, 128])

    out_d = nc.dram_tensor("out", [NOWN * 128, D], F32, kind="ExternalOutput").ap()
    hscr = nc.dram_tensor("hscr", [NT * 128, D], F32, kind="Internal").ap()
    h2scr = nc.dram_tensor("h2scr", [NOWN * 128, D], F32, kind="Internal").ap()
    qscr = nc.dram_tensor("qscr", [NOWN, 128, 2048], BF16, kind="Internal").ap()
    gabscr = nc.dram_tensor("gabscr", [NOWN, 128, 2048], BF16, kind="Internal").ap()
    ngscr = nc.dram_tensor("ngscr", [NOWN, 128, 24], F32, kind="Internal").ap()
    dbg_d = None
    if dbg:
        dbg_d = nc.dram_tensor("dbg", [128, 16384], F32, kind="ExternalOutput").ap()

    st = contextlib.ExitStack()
    with st:
        arena_t = st.enter_context(nc.sbuf_tensor("arena", [128, ARENA_WORDS], F32))
        A = Arena(arena_t[:], ARENA_WORDS)
        psum_all = st.enter_context(nc.psum_tensor("psum_all", [128, 4096], F32))[:]
        PB = [psum_all[:, i * 512:(i + 1) * 512] for i in range(8)]
        PBb = [Buf(f"bank{i}", excl=True) for i in range(8)]
        S = Sched(nc)

        def MM(out, lhsT, rhs, start, stop, reads, wb):
            if start:
                S.op("pe", lambda e: e.matmul(out, lhsT=lhsT, rhs=rhs, start=True, stop=stop),
                     reads=reads, writes=[wb])
            else:
                S.op("pe", lambda e: e.matmul(out, lhsT=lhsT, rhs=rhs, start=False, stop=stop),
                     reads=reads, pwrites=[wb])

        def MMP(out, lhsT, rhs, start, stop, reads, wb, first):
            if first:
                S.op("pe", lambda e: e.matmul(out, lhsT=lhsT, rhs=rhs, start=start, stop=stop),
                     reads=reads, writes=[wb])
            else:
                S.op("pe", lambda e: e.matmul(out, lhsT=lhsT, rhs=rhs, start=start, stop=stop),
                     reads=reads, pwrites=[wb])

        def TR(out, in_, ident, reads, wb, first):
            if first:
                S.op("pe", lambda e: e.transpose(out, in_, ident), reads=reads, writes=[wb])
            else:
                S.op("pe", lambda e: e.transpose(out, in_, ident), reads=reads, pwrites=[wb])

        def ACT(out, in_, func, reads, writes, bias=None, scale=None, accum=None, pwrites=()):
            kw = {}
            if bias is not None:
                kw["bias"] = bias
            if scale is not None:
                kw["scale"] = scale
            if accum is not None:
                kw["accum_out"] = accum
            S.op("act", lambda e: e.activation(out, in_, func, **kw), reads=reads, writes=writes,
                 pwrites=pwrites)

        def ENG(eng, name, reads, writes, *args, pwrites=(), **kw):
            S.op(eng, lambda e: getattr(e, name)(*args, **kw), reads=reads, writes=writes,
                 pwrites=pwrites)

        def DVE(name, reads, writes, *args, pwrites=(), **kw):
            ENG("dve", name, reads, writes, *args, pwrites=pwrites, **kw)

        def POOL(name, reads, writes, *args, pwrites=(), **kw):
            ENG("pool", name, reads, writes, *args, pwrites=pwrites, **kw)

        def DMA(q, out, in_, reads=(), writes=(), pwrites=(), dbuf=None):
            S.op(q, lambda e: e.dma_start(out=out, in_=in_), reads=reads, writes=writes,
                 pwrites=pwrites, dma=True, dbuf=dbuf)

        dbg_off = [0]
        dbg_buf = Buf("dbg")

        def DBG(ap_sb, rbuf, ncols, parts=128):
            if not dbg:
                return
            o = dbg_off[0]
            DMA("sp", dbg_d[0:parts, o:o + ncols], ap_sb, reads=[rbuf], pwrites=[dbg_buf], dbuf=dbg_buf)
            dbg_off[0] = o + ncols

        def early(name, tile_ap=None, rbuf=None):
            if stop_after != name:
                return False
            b_o = Buf("out")
            if tile_ap is not None:
                DMA("sp", out_d[0:128, 0:tile_ap.shape[1]], tile_ap, reads=[rbuf], pwrites=[b_o], dbuf=b_o)
                S.op("sp", lambda e: e.nop(), reads=[b_o])
            S.barrier()
            S.emit()
            return True

        ident_f = A.alloc([128], F32)
        ident_b = A.alloc([128], BF16)
        b_const = Buf("const")
        DMA("sp", ident_f, ident_d[:, :], pwrites=[b_const], dbuf=b_const)
        DMA("pool", ident_b, ident_d[:, :], pwrites=[b_const], dbuf=b_const)
        ones_f = A.alloc([128], F32)
        DVE("memset", [], [], ones_f, 1.0, pwrites=[b_const])
        stats = A.alloc([64], F32)
        base_mark = A.off

        def rmsnorm_to_bf16(x_sb, xbuf, gam, gbuf, out_bf, obuf, junk, jbuf, sc, scbuf):
            ACT(junk, x_sb, AF.Square, [xbuf], [jbuf, scbuf], accum=sc[:, 0:1])
            DVE("tensor_scalar", [scbuf], [scbuf], sc[:, 1:2], sc[:, 0:1], 1.0 / D, EPS, ALU.mult, ALU.add)
            ACT(sc[:, 2:3], sc[:, 1:2], AF.Sqrt, [scbuf], [scbuf])
            DVE("reciprocal", [scbuf], [scbuf], sc[:, 3:4], sc[:, 2:3])
            DVE("scalar_tensor_tensor", [xbuf, scbuf, gbuf], [obuf], out_bf, x_sb, sc[:, 3:4], gam,
                ALU.mult, ALU.mult)

        def transpose_to(dst3, src_bf, sbuf_, nblk, bank, dstbuf, evac="act"):
            pbv = PB[bank].bitcast(BF16)
            for b in range(nblk):
                TR(pbv[:, b * 128:(b + 1) * 128], src_bf[:, b * 128:(b + 1) * 128], ident_b,
                   [sbuf_, b_const], PBb[bank], b == 0)
            src = pbv[:, 0:nblk * 128].rearrange("p (b t) -> p b t", b=nblk)
            if evac == "act":
                ACT(dst3, src, AF.Copy, [PBb[bank]], [dstbuf])
            else:
                DVE("tensor_copy", [PBb[bank]], [dstbuf], dst3, src)

        def ffn_phase(ntiles, src_d, gam_d, wg_d, wu_d, wd_d, finish):
            m0 = A.off
            Wg = A.alloc([8, DFF], BF16)
            Wu = A.alloc([8, DFF], BF16)
            Wd = A.alloc([NF, D], BF16)
            xnT = A.alloc([8, 512], BF16)
            aT = A.alloc([NF, 512], BF16)
            xs = [A.alloc([D], F32) for _ in range(4)]
            xr = [A.alloc([D], F32) for _ in range(2)]
            sg = [A.alloc([512], BF16) for _ in range(2)]
            gam = A.alloc([D], F32)
            xn = [A.alloc([D], BF16) for _ in range(4)]
            sc = [A.alloc([4], F32) for _ in range(4)]
            b_gam = Buf()
            DMA("sp", gam, gam_d[0:1, :].broadcast_to([128, D]), writes=[b_gam], dbuf=b_gam)
            fparts = [(0, 2), (2, 6), (6, 14), (14, 22)]
            b_wg = [Buf() for _ in fparts]
            b_wu = [Buf() for _ in fparts]
            wgs = wg_d.rearrange("(kc p) n -> p kc n", p=128)
            wus = wu_d.rearrange("(kc p) n -> p kc n", p=128)
            for i, (f0, f1) in enumerate(fparts):
                DMA("pool", Wg[:, :, f0 * 128:f1 * 128], wgs[:, :, f0 * 128:f1 * 128], writes=[b_wg[i]], dbuf=b_wg[i])
                DMA("pool", Wu[:, :, f0 * 128:f1 * 128], wus[:, :, f0 * 128:f1 * 128], writes=[b_wu[i]], dbuf=b_wu[i])
            wds = wd_d.rearrange("(f p) n -> p f n", p=128)
            dparts = [(0, 6), (6, 14), (14, 22)]
            b_wd = [Buf() for _ in dparts]
            for i, (f0, f1) in enumerate(dparts):
                DMA("pool", Wd[:, f0:f1, :], wds[:, f0:f1, :], writes=[b_wd[i]], dbuf=b_wd[i])

            def fpart(f, parts):
                for i, (f0, f1) in enumerate(parts):
                    if f0 <= f < f1:
                        return i
                raise ValueError

            b_xs = [Buf() for _ in range(4)]
            b_xr = [Buf() for _ in range(2)]
            b_xn = [Buf() for _ in range(4)]
            b_sc = [Buf() for _ in range(4)]
            b_xnT = Buf()
            b_aT = [Buf() for _ in range(NF)]
            b_sg = [Buf() for _ in range(2)]
            ngroups = ntiles // 4

            def norm_part(G):
                for j in range(4):
                    t = 4 * G + j
                    DMA("sp", xs[j], src_d[t * 128:(t + 1) * 128, :], writes=[b_xs[j]], dbuf=b_xs[j])
                    rmsnorm_to_bf16(xs[j], b_xs[j], gam, b_gam, xn[j], b_xn[j], xn[j], b_xn[j], sc[j], b_sc[j])

            def tr_part(G):
                for j in range(4):
                    transpose_to(xnT[:, :, j * 128:(j + 1) * 128], xn[j], b_xn[j], 8, 4 + j, b_xnT, evac="act")

            cnt = [0]

            def gateup(G):
                for f in range(NF):
                    if f == 6 and G + 1 < ngroups:
                        norm_part(G + 1)
                    pg, pu = (0, 1) if f % 2 == 0 else (2, 3)
                    ig = fpart(f, fparts)
                    for kc in range(8):
                        MM(PB[pg], Wg[:, kc, f * 128:(f + 1) * 128], xnT[:, kc, :], kc == 0, kc == 7,
                           [b_wg[ig], b_xnT], PBb[pg])
                    for kc in range(8):
                        MM(PB[pu], Wu[:, kc, f * 128:(f + 1) * 128], xnT[:, kc, :], kc == 0, kc == 7,
                           [b_wu[ig], b_xnT], PBb[pu])
                    s = f % 2
                    ACT(sg[s], PB[pg], AF.Silu, [PBb[pg]], [b_sg[s]])
                    DVE("tensor_tensor", [b_sg[s], PBb[pu]], [b_aT[f]], aT[:, f, :], sg[s], PB[pu], ALU.mult)

            def down(G):
                for j in range(4):
                    t = 4 * G + j
                    r = t % 2
                    DMA("sp", xr[r], src_d[t * 128:(t + 1) * 128, :], writes=[b_xr[r]], dbuf=b_xr[r])
                    bk0 = 4 if j % 2 == 0 else 6
                    for f in range(NF):
                        idp = fpart(f, dparts)
                        for hf in range(2):
                            bk = bk0 + hf
                            MM(PB[bk], aT[:, f, j * 128:(j + 1) * 128], Wd[:, f, hf * 512:(hf + 1) * 512],
                               f == 0, f == NF - 1, [b_aT[f], b_wd[idp]], PBb[bk])
                    for hf in range(2):
                        bk = bk0 + hf
                        DVE("scalar_tensor_tensor", [PBb[bk], b_xr[r]], [], xr[r][:, hf * 512:(hf + 1) * 512],
                            PB[bk], 0.5, xr[r][:, hf * 512:(hf + 1) * 512], ALU.mult, ALU.add,
                            pwrites=[b_xr[r]])
                    finish(t, xr[r], b_xr[r])

            norm_part(0)
            tr_part(0)
            for G in range(ngroups):
                gateup(G)
                if G + 1 < ngroups:
                    tr_part(G + 1)
                down(G)
            S.barrier()
            A.off = m0

        b_hscr = Buf("hscr")

        def fin1(t, h_sb, hbuf):
            DMA("pool", hscr[t * 128:(t + 1) * 128, :], h_sb, reads=[hbuf], pwrites=[b_hscr], dbuf=b_hscr)

        if skip1a:
            hscr = xrel
        else:
            ffn_phase(nt1b, xrel, g_ffn1, w1g, w1u, w1d, fin1)

        if stop_after == "1a":
            tmp = A.alloc([D], F32)
            b_tmp = Buf()
            b_out = Buf("out")
            for t in range(NOWN):
                DMA("sp", tmp, hscr[t * 128:(t + 1) * 128, :], reads=[b_hscr], writes=[b_tmp], dbuf=b_tmp)
                DMA("sp", out_d[t * 128:(t + 1) * 128, :], tmp, reads=[b_tmp], pwrites=[b_out], dbuf=b_out)
            S.op("sp", lambda e: e.nop(), reads=[b_out, dbg_buf])
            S.emit()
            return nc, S, A


        KT_all = A.alloc([3, NT * 128], BF16)
        KT_slc, KT_win, KT_swa = KT_all[:, 0, :], KT_all[:, 1, :], KT_all[:, 2, :]
        V1_slc = A.alloc([NT, 2, 65], BF16)
        V1_win = A.alloc([NT, 2, 65], BF16)
        V1_swa = A.alloc([NT, 65], BF16)
        KcT = A.alloc([512], BF16)
        V1c = A.alloc([4, 2, 65], BF16)
        kvalid = A.alloc([NT], F32)
        cvalid = A.alloc([4], F32)
        b_KV = Buf("kv")
        b_tab = Buf("tab")
        DMA("sp", kvalid, kvalid_d[:, :], pwrites=[b_tab], dbuf=b_tab)
        DMA("sp", cvalid, cvalid_d[:, :], pwrites=[b_tab], dbuf=b_tab)
        POOL("memset", [], [], V1_slc[:, :, :, 64:65], 1.0, pwrites=[b_KV])
        POOL("memset", [], [], V1_win[:, :, :, 64:65], 1.0, pwrites=[b_KV])
        POOL("memset", [], [], V1_swa[:, :, 64:65], 1.0, pwrites=[b_KV])
        POOL("memset", [], [], V1c[:, :, :, 64:65], 1.0, pwrites=[b_KV])
        if early("res", kvalid, b_tab):
            return nc, S, A
        m_res = A.off
        sincos_all = A.alloc([NT + 4, 64], F32)
        sincos = sincos_all[:, 0:NT, :]
        sincos_c = sincos_all[:, NT:NT + 4, :]
        b_sc_tab = Buf("sincos")
        m_sincos = A.off

        TWO_PI = float(2 * np.pi)
        C1 = 6.28125
        C2 = float(2 * np.pi - 6.28125)

        def make_sincos(dst, pos_d, n):
            m0 = A.off
            posi = A.alloc([n], I32)
            posf = A.alloc([n], F32)
            invf2 = A.alloc([64], F32)
            ph = A.alloc([64], F32)
            ang = A.alloc([n, 64], F32)
            ki = A.alloc([n, 64], I32)
            kf = A.alloc([n, 64], F32)
            bt = Buf()
            DMA("sp", posi, pos_d[:, :], pwrites=[bt], dbuf=bt)
            DMA("sp", invf2, invf2_d[0:1, :].broadcast_to([128, 64]), pwrites=[bt], dbuf=bt)
            DMA("sp", ph, phase_d[0:1, :].broadcast_to([128, 64]), pwrites=[bt], dbuf=bt)
            bw = Buf()
            DVE("tensor_copy", [bt], [bw], posf, posi)
            DVE("tensor_tensor", [bt, bw], [bw], ang, invf2.unsqueeze(1).broadcast_to([128, n, 64]),
                posf.unsqueeze(2).broadcast_to([128, n, 64]), ALU.mult)
            DVE("tensor_tensor", [bt, bw], [bw], ang, ang, ph.unsqueeze(1).broadcast_to([128, n, 64]), ALU.add)
            DVE("tensor_scalar", [bw], [bw], ki, ang, 1.0 / TWO_PI, None, ALU.mult)
            DVE("tensor_copy", [bw], [bw], kf, ki)
            DVE("scalar_tensor_tensor", [bw], [bw], ang, kf, -C1, ang, ALU.mult, ALU.add)
            DVE("scalar_tensor_tensor", [bw], [bw], ang, kf, -C2, ang, ALU.mult, ALU.add)
            DVE("tensor_scalar", [bw], [bw], kf, ang, float(np.pi), -TWO_PI, ALU.is_gt, ALU.mult)
            DVE("tensor_tensor", [bw], [bw], ang, ang, kf, ALU.add)
            DVE("tensor_scalar", [bw], [bw], kf, ang, float(-np.pi), TWO_PI, ALU.is_lt, ALU.mult)
            DVE("tensor_tensor", [bw], [bw], ang, ang, kf, ALU.add)
            DVE("tensor_scalar", [bw], [bw], ang, ang, float(np.pi), float(-np.pi), ALU.min, ALU.max)
            ACT(dst, ang, AF.Sin, [bw], [], pwrites=[b_sc_tab])
            S.barrier()
            A.off = m0

        m_pre_wkv = A.off
        Wkv = A.alloc([8, 896], BF16)
        b_wkv = Buf()
        wins = w_in.rearrange("(kc p) n -> p kc n", p=128)
        for d0, s0, wd in ((0, 768, 128), (128, 1024, 128), (256, 1816, 64), (320, 896, 128),
                           (448, 1152, 128), (576, 1880, 64), (640, 512, 256)):
            DMA("pool", Wkv[:, :, d0:d0 + wd], wins[:, :, s0:s0 + wd], pwrites=[b_wkv], dbuf=b_wkv)
        make_sincos(sincos_all, posall, NT + 4)
        if early("sincos", sincos[:, 3, :], b_sc_tab):
            return nc, S, A

        def rope(dst, src, sc_ap, nh, rbufs, wbuf, tmp, tbuf, dst_views=None):
            sin_b = sc_ap[:, 0:32].unsqueeze(1).broadcast_to([128, nh, 32])
            cos_b = sc_ap[:, 32:64].unsqueeze(1).broadcast_to([128, nh, 32])
            x1 = src[:, :, 0:32]
            x2 = src[:, :, 32:64]
            t1 = tmp[:, 0, 0:nh, :]
            t2 = tmp[:, 1, 0:nh, :]
            d1, d2 = (dst[:, :, 0:32], dst[:, :, 32:64]) if dst_views is None else dst_views
            DVE("tensor_tensor", rbufs, [tbuf], t1, x1, cos_b, ALU.mult)
            DVE("tensor_tensor", rbufs + [tbuf], [tbuf], t2, x2, sin_b, ALU.mult)
            DVE("tensor_tensor", [tbuf], [], d1, t1, t2, ALU.subtract, pwrites=[wbuf])
            DVE("tensor_tensor", rbufs + [tbuf], [tbuf], t1, x2, cos_b, ALU.mult)
            DVE("tensor_tensor", rbufs + [tbuf], [tbuf], t2, x1, sin_b, ALU.mult)
            DVE("tensor_tensor", [tbuf], [], d2, t1, t2, ALU.add, pwrites=[wbuf])

        m1b = A.off
        kvT_raw = A.alloc([2, NT * 128 + 16], BF16)
        b_kvT = Buf("kvT_raw")
        if nt1b < NT:
            POOL("memset", [], [b_kvT], kvT_raw, 0.0)
            POOL("memset", [], [b_KV], KT_all, 0.0)
            for t_ in (V1_slc, V1_win):
                POOL("memset", [], [b_KV], t_[:, :, :, 0:64], 0.0)
            POOL("memset", [], [b_KV], V1_swa[:, :, 0:64], 0.0)
        POOL("memset", [], [b_kvT], kvT_raw[:, :, NT * 128:NT * 128 + 16], 0.0)
        m1b2 = A.off
        gmix = A.alloc([D], F32)
        b_gmix = Buf()
        DMA("sp", gmix, g_mix[0:1, :].broadcast_to([128, D]), writes=[b_gmix], dbuf=b_gmix)
        hb = [A.alloc([D], F32) for _ in range(3)]
        ub = [A.alloc([D], BF16) for _ in range(3)]
        uT = [A.alloc([8, 128], BF16) for _ in range(3)]
        junk = A.alloc([D], BF16)
        scs = [A.alloc([4], F32) for _ in range(3)]
        rk = [A.alloc([6, 64], BF16) for _ in range(3)]
        rtmp = A.alloc([2, 16, 32], F32)
        kvraw = [A.alloc([256], BF16) for _ in range(3)]
        b_hb = [Buf() for _ in range(3)]
        b_ub = [Buf() for _ in range(3)]
        b_uT = [Buf() for _ in range(3)]
        b_junk = Buf()
        b_scs = [Buf() for _ in range(3)]
        b_kin = [Buf() for _ in range(2)]
        b_rk = [Buf() for _ in range(3)]
        b_rtmp = Buf()
        b_kvraw = [Buf() for _ in range(3)]

        def load_norm(tau, k, gam, gbuf):
            DMA("sp", hb[k], hscr[tau * 128:(tau + 1) * 128, :], reads=[b_hscr], writes=[b_hb[k]], dbuf=b_hb[k])
            rmsnorm_to_bf16(hb[k], b_hb[k], gam, gbuf, ub[k], b_ub[k], junk, b_junk, scs[k], b_scs[k])

        def norm_T(k):
            transpose_to(uT[k], ub[k], b_ub[k], 8, 4 + k, b_uT[k], evac="act")

        a_sb = [A.alloc([448], F32) for _ in range(2)]
        b_sb = [A.alloc([448], F32) for _ in range(2)]
        b_asb = [Buf() for _ in range(2)]
        b_bsb = [Buf() for _ in range(2)]

        def stage1(tau):
            k = tau % 2
            k4 = tau % 3
            pa, pb = (0, 1) if k == 0 else (2, 3)
            for kc in range(8):
                MM(PB[pa][:, 0:448], uT[k4][:, kc, :], Wkv[:, kc, 0:448], kc == 0, kc == 7, [b_uT[k4], b_wkv], PBb[pa])
            for kc in range(8):
                MM(PB[pb][:, 0:448], uT[k4][:, kc, :], Wkv[:, kc, 448:896], kc == 0, kc == 7, [b_uT[k4], b_wkv], PBb[pb])
            ACT(a_sb[k], PB[pa][:, 0:448], AF.Copy, [PBb[pa]], [b_asb[k]])
            ACT(b_sb[k], PB[pb][:, 0:448], AF.Copy, [PBb[pb]], [b_bsb[k]])
            POOL("tensor_copy", [b_asb[k]], [], V1_slc[:, tau, :, 0:64],
                 a_sb[k][:, 320:448].rearrange("p (g d) -> p g d", g=2), pwrites=[b_KV])
            POOL("tensor_copy", [b_bsb[k]], [], V1_win[:, tau, :, 0:64],
                 b_sb[k][:, 0:128].rearrange("p (g d) -> p g d", g=2), pwrites=[b_KV])
            POOL("tensor_copy", [b_bsb[k]], [], V1_swa[:, tau, 0:64], b_sb[k][:, 128:192], pwrites=[b_KV])
            ACT(kvraw[k4], b_sb[k][:, 192:448], AF.Copy, [b_bsb[k]], [b_kvraw[k4]])
            rope(rk[k4][:, 0:5, :], a_sb[k][:, 0:320].rearrange("p (h d) -> p h d", h=5), sincos[:, tau, :], 5,
                 [b_asb[k], b_sc_tab], b_rk[k4], rtmp, b_rtmp)
            DVE("tensor_copy", [b_rk[k4]], [], rk[k4][:, 5, :], rk[k4][:, 4, :], pwrites=[b_rk[k4]])

        def stage2(tau):
            k = tau % 3
            pbv = PB[7].bitcast(BF16)
            rkf = rk[k].rearrange("p h d -> p (h d)")
            for bl in range(3):
                TR(pbv[:, bl * 128:(bl + 1) * 128], rkf[:, bl * 128:(bl + 1) * 128], ident_b, [b_rk[k], b_const],
                   PBb[7], bl == 0)
            for bl in range(2):
                TR(pbv[:, (3 + bl) * 128:(4 + bl) * 128], kvraw[k][:, bl * 128:(bl + 1) * 128], ident_b,
                   [b_kvraw[k], b_const], PBb[7], False)
            ts = slice(tau * 128, (tau + 1) * 128)
            ACT(KT_all[:, :, ts], pbv[:, 0:384].rearrange("p (a t) -> p a t", a=3), AF.Copy, [PBb[7]], [], pwrites=[b_KV])
            DVE("tensor_copy", [PBb[7]], [], kvT_raw[:, :, ts], pbv[:, 384:640].rearrange("p (a t) -> p a t", a=2),
                pwrites=[b_kvT])

        for t_ in range(min(3, nt1b)):
            load_norm(t_, t_ % 3, gmix, b_gmix)
        for t_ in range(min(2, nt1b)):
            norm_T(t_ % 3)
        stage1(0)
        for tau in range(nt1b):
            if tau + 3 < nt1b:
                load_norm(tau + 3, (tau + 3) % 3, gmix, b_gmix)
            if tau + 2 < nt1b:
                norm_T((tau + 2) % 3)
            if tau + 1 < nt1b:
                stage1(tau + 1)
            if tau >= 1:
                stage2(tau - 1)
        stage2(nt1b - 1)
        S.barrier()
        A.off = m1b2

        if stop_after == "1b":
            b_out = Buf("out")
            tmpf = A.alloc([2048], F32)
            bt_ = Buf()
            DVE("tensor_copy", [b_KV], [bt_], tmpf[:, 0:128], KT_slc[:, 384:512])
            DVE("tensor_copy", [b_KV], [], tmpf[:, 128:256], KT_win[:, 384:512], pwrites=[bt_])
            DVE("tensor_copy", [b_KV], [], tmpf[:, 256:384], KT_swa[:, 384:512], pwrites=[bt_])
            DVE("tensor_copy", [b_KV], [], tmpf[:, 384:514], V1_slc[:, 3, :, :].rearrange("p g d -> p (g d)"), pwrites=[bt_])
            DVE("tensor_copy", [b_KV], [], tmpf[:, 514:644], V1_win[:, 3, :, :].rearrange("p g d -> p (g d)"), pwrites=[bt_])
            DVE("tensor_copy", [b_KV], [], tmpf[:, 644:709], V1_swa[:, 3, :], pwrites=[bt_])
            DVE("tensor_copy", [b_kvT], [], tmpf[:, 709:837], kvT_raw[:, 0, 384:512], pwrites=[bt_])
            DVE("tensor_copy", [b_kvT], [], tmpf[:, 837:965], kvT_raw[:, 1, 384:512], pwrites=[bt_])
            DVE("tensor_copy", [b_sc_tab], [], tmpf[:, 965:1029], sincos[:, 3, :], pwrites=[bt_])
            DMA("sp", out_d[0:128, :], tmpf[:, 0:1024], reads=[bt_], pwrites=[b_out], dbuf=b_out)
            DMA("sp", out_d[128:256, :], tmpf[:, 1024:2048], reads=[bt_], pwrites=[b_out], dbuf=b_out)
            S.op("sp", lambda e: e.nop(), reads=[b_out])
            S.emit()
            return nc, S, A

        m1c = A.off
        W1bd = A.alloc([32, 512], BF16)
        W2s = [A.alloc([2, 64], BF16) for _ in range(2)]
        peT = [A.alloc([32], BF16) for _ in range(2)]
        peb = A.alloc([512], BF16)
        ones_b = A.alloc([128], BF16)
        rtmp = A.alloc([2, 16, 32], F32)
        b_rtmp = Buf()
        b_w1r = Buf()
        b_w1bd = Buf()
        POOL("memset", [], [b_w1bd], W1bd[0:64, :, 256:512], 0.0)
        POOL("memset", [], [], W1bd[64:128, :, 0:256], 0.0, pwrites=[b_w1bd])
        for kv, (w2d_, ped_) in enumerate(((ck_w2, pe_k), (cv_w2, pe_v))):
            DMA("pool", W2s[kv], w2d_.rearrange("(c p) n -> p c n", p=128), pwrites=[b_w1r], dbuf=b_w1r)
            DMA("pool", peT[kv][0:64], ped_[:, :], pwrites=[b_w1r], dbuf=b_w1r)
        DVE("memset", [], [], ones_b, 1.0, pwrites=[b_w1r])
        b_peb = Buf()
        xg = A.alloc([512], F32)
        sq = A.alloc([512], F32)
        h1 = A.alloc([512], BF16)
        h1T = A.alloc([4, 128], BF16)
        kc_in = A.alloc([2, 64], F32)
        rkc = A.alloc([2, 64], BF16)
        b_xg, b_sq, b_h1, b_h1T, b_kcin, b_rkc = Buf(), Buf(), Buf(), Buf(), Buf(), Buf()
        for kv, w1d_ in enumerate((ck_w1, cv_w1)):
            src = w1d_.rearrange("(o d) n -> d o n", d=64)
            DMA("pool", W1bd[0:64, :, 0:256], src, pwrites=[b_w1bd], dbuf=b_w1bd)
            DMA("pool", W1bd[64:128, :, 256:512], src, pwrites=[b_w1bd], dbuf=b_w1bd)
            for o in range(32):
                MM(PB[2][0:1, 0:256], peT[kv][0:64, o:o + 1], W1bd[0:64, o, 0:256], o == 0, o == 31,
                   [b_w1r, b_w1bd], PBb[2])
            ACT(peb[0:1, 0:256], PB[2][0:1, 0:256], AF.Copy, [PBb[2]], [b_peb])
            ACT(peb[0:1, 256:512], PB[2][0:1, 0:256], AF.Copy, [PBb[2]], [], pwrites=[b_peb])
            def first_layer(ch):
                bk = ch % 2
                for o in range(32):
                    c0 = ch * 2048 + o
                    MM(PB[bk], kvT_raw[:, kv, c0:c0 + 2033:16], W1bd[:, o, :], o == 0, False, [b_kvT, b_w1bd], PBb[bk])
                MM(PB[bk], ones_b[0:1, :], peb[0:1, :], False, True, [b_peb, b_w1r], PBb[bk])

            def rest(ch):
                bk = ch % 2
                ACT(sq, PB[bk], AF.Square, [PBb[bk]], [b_sq])
                DVE("tensor_scalar", [b_sq], [b_sq], sq, sq, 0.044715, 1.0, ALU.mult, ALU.add)
                DVE("tensor_tensor", [b_sq, PBb[bk]], [b_xg], xg, sq, PB[bk], ALU.mult)
                ACT(xg, xg, AF.Sigmoid, [b_xg], [b_xg], scale=1.5957691216057308)
                DVE("tensor_tensor", [b_xg, PBb[bk]], [b_h1], h1, xg, PB[bk], ALU.mult)

            def second_layer(ch):
                transpose_to(h1T, h1, b_h1, 4, 7, b_h1T, evac="act")
                for g in range(2):
                    ob = 4 + g
                    for c2 in range(2):
                        MM(PB[ob][:, 0:64], h1T[:, g * 2 + c2, :], W2s[kv][:, c2, :], c2 == 0, c2 == 1,
                           [b_h1T, b_w1r], PBb[ob])
                    if kv == 0:
                        if g == 0:
                            DVE("tensor_copy", [PBb[ob]], [b_kcin], kc_in[:, 0, :], PB[ob][:, 0:64])
                        else:
                            DVE("tensor_copy", [PBb[ob]], [], kc_in[:, 1, :], PB[ob][:, 0:64], pwrites=[b_kcin])
                    else:
                        ACT(V1c[:, ch, g, 0:64], PB[ob][:, 0:64], AF.Copy, [PBb[ob]], [], pwrites=[b_KV])
                if kv == 0:
                    rope(rkc, kc_in, sincos_c[:, ch, :], 2, [b_kcin, b_sc_tab], b_rkc, rtmp, b_rtmp)
                    pbv = PB[6].bitcast(BF16)
                    TR(pbv[:, 0:128], rkc.rearrange("p h d -> p (h d)"), ident_b, [b_rkc, b_const], PBb[6], True)
                    ACT(KcT[:, ch * 128:(ch + 1) * 128], pbv[:, 0:128], AF.Copy, [PBb[6]], [], pwrites=[b_KV])

            first_layer(0)
            for ch in range(4):
                rest(ch)
                if ch + 1 < 4:
                    first_layer(ch + 1)
                second_layer(ch)
        S.barrier()
        A.off = m_pre_wkv

        if stop_after == "1c":
            b_out = Buf("out")
            tmpf = A.alloc([1024], F32)
            bt_ = Buf()
            DVE("tensor_copy", [b_KV], [bt_], tmpf[:, 0:512], KcT)
            DVE("tensor_copy", [b_KV], [], tmpf[:, 512:1024], V1c.rearrange("p c g d -> p (c g d)")[:, 0:512], pwrites=[bt_])
            DMA("sp", out_d[0:128, :], tmpf[:, 0:1024], reads=[bt_], pwrites=[b_out], dbuf=b_out)
            S.op("sp", lambda e: e.nop(), reads=[b_out])
            S.emit()
            return nc, S, A

        b_qscr, b_gab, b_ngs = Buf("qscr"), Buf("gabscr"), Buf("ngscr")
        Wq = A.alloc([8, 1048], BF16)
        Wgab = A.alloc([8, 2048], BF16)
        b_wq = Buf()
        DMA("pool", Wq[:, :, 0:512], wins[:, :, 0:512], pwrites=[b_wq], dbuf=b_wq)
        DMA("pool", Wq[:, :, 512:1024], wins[:, :, 1304:1816], pwrites=[b_wq], dbuf=b_wq)
        DMA("pool", Wq[:, :, 1024:1048], wins[:, :, 1280:1304], pwrites=[b_wq], dbuf=b_wq)
        DMA("pool", Wgab[:, :, 0:1024], wins[:, :, 1944:2968], pwrites=[b_wq], dbuf=b_wq)
        DMA("pool", Wgab[:, :, 1024:2048], wins[:, :, 2968:3992], pwrites=[b_wq], dbuf=b_wq)
        gmix = A.alloc([D], F32)
        b_gmix = Buf()
        DMA("sp", gmix, g_mix[0:1, :].broadcast_to([128, D]), writes=[b_gmix], dbuf=b_gmix)
        hb = [A.alloc([D], F32) for _ in range(2)]
        ub = [A.alloc([D], BF16) for _ in range(3)]
        uT = [A.alloc([8, 128], BF16) for _ in range(2)]
        gabs = [A.alloc([2048], BF16) for _ in range(2)]
        b_gabbs = [Buf() for _ in range(2)]
        ngss = [A.alloc([24], F32) for _ in range(2)]
        b_ngsbs = [Buf() for _ in range(2)]
        scs = [A.alloc([4], F32) for _ in range(3)]
        rtmp = A.alloc([2, 8, 32], F32)
        qin = A.alloc([16, 64], F32)
        rq = A.alloc([1024], BF16)
        qTz = A.alloc([4, 4, 128], BF16)
        b_qTz = Buf()
        POOL("memset", [], [b_qTz], qTz, 0.0)
        b_hb = [Buf() for _ in range(2)]
        b_ub = [Buf() for _ in range(3)]
        b_uT = [Buf() for _ in range(2)]
        b_junk, b_rtmp, b_qin, b_rq, b_qT, b_ngsb, b_gabb = Buf(), Buf(), Buf(), Buf(), Buf(), Buf(), Buf()
        b_scs = [Buf() for _ in range(3)]
        def stage_n1d(i):
            tau = 4 * i + 3
            kh, k3 = i % 2, i % 3
            DMA("sp", hb[kh], hscr[tau * 128:(tau + 1) * 128, :], reads=[b_hscr], writes=[b_hb[kh]], dbuf=b_hb[kh])
            rmsnorm_to_bf16(hb[kh], b_hb[kh], gmix, b_gmix, ub[k3], b_ub[k3], ub[k3], b_ub[k3], scs[k3], b_scs[k3])

        def stage_a1d(i):
            tau = 4 * i + 3
            k = i % 2
            qin = qins[k]
            b_qin = b_qins[k]
            gab, b_gabb, ngs, b_ngsb = gabs[k], b_gabbs[k], ngss[k], b_ngsbs[k]
            transpose_to(uT[k], ub[i % 3], b_ub[i % 3], 8, 6, b_uT[k], evac="act")
            for kc in range(8):
                MM(PB[5][:, 0:24], uT[k][:, kc, :], Wq[:, kc, 1024:1048], kc == 0, kc == 7, [b_uT[k], b_wq], PBb[5])
            ACT(ngs, PB[5][:, 0:24], AF.Sigmoid, [PBb[5]], [b_ngsb])
            DMA("pool", ngscr[i], ngs, reads=[b_ngsb], pwrites=[b_ngs], dbuf=b_ngs)
            for hq in range(2):
                for kc in range(8):
                    MM(PB[hq], uT[k][:, kc, :], Wq[:, kc, hq * 512:(hq + 1) * 512], kc == 0, kc == 7,
                       [b_uT[k], b_wq], PBb[hq])
            ACT(qin[:, 0:8, :], PB[0].rearrange("p (h d) -> p h d", h=8), AF.Copy, [PBb[0]], [b_qin])
            ACT(qin[:, 8:16, :], PB[1].rearrange("p (h d) -> p h d", h=8), AF.Copy, [PBb[1]], [], pwrites=[b_qin])
            for gq in range(4):
                bk = 2 + (gq % 2) if gq < 2 else 4 + (gq % 2)
                bk = [2, 3, 4, 5][gq]
                for kc in range(8):
                    MM(PB[bk], uT[k][:, kc, :], Wgab[:, kc, gq * 512:(gq + 1) * 512], kc == 0, kc == 7,
                       [b_uT[k], b_wq], PBb[bk])
                if gq == 0:
                    ACT(gab[:, 0:512], PB[bk], AF.Sigmoid, [PBb[bk]], [b_gabb])
                else:
                    ACT(gab[:, gq * 512:(gq + 1) * 512], PB[bk], AF.Sigmoid, [PBb[bk]], [], pwrites=[b_gabb])
            DMA("pool", gabscr[i], gab, reads=[b_gabb], pwrites=[b_gab], dbuf=b_gab)

        def stage_r1d(i):
            tau = 4 * i + 3
            k = i % 2
            rq = rqs[k]
            b_rq = b_rqs[k]
            qin = qins[k]
            b_qin = b_qins[k]
            for s_ in range(2):
                src = qin[:, s_ * 8:(s_ + 1) * 8, :].rearrange("p (g r) d -> p r g d", g=2)
                dstv = rq[:, s_ * 512:(s_ + 1) * 512].rearrange("p (r g d) -> p r g d", r=4, g=2)
                sc_ap = sincos[:, tau, :]
                sin_b = sc_ap[:, 0:32].unsqueeze(1).unsqueeze(1).broadcast_to([128, 4, 2, 32])
                cos_b = sc_ap[:, 32:64].unsqueeze(1).unsqueeze(1).broadcast_to([128, 4, 2, 32])
                x1, x2 = src[:, :, :, 0:32], src[:, :, :, 32:64]
                t1 = rtmp[:, 0, 0:8, :].rearrange("p (r g) d -> p r g d", g=2)
                t2 = rtmp[:, 1, 0:8, :].rearrange("p (r g) d -> p r g d", g=2)
                rb = [b_qin, b_sc_tab]
                DVE("tensor_tensor", rb, [b_rtmp], t1, x1, cos_b, ALU.mult)
                DVE("tensor_tensor", rb + [b_rtmp], [b_rtmp], t2, x2, sin_b, ALU.mult)
                if s_ == 0:
                    DVE("tensor_tensor", [b_rtmp], [b_rq], dstv[:, :, :, 0:32], t1, t2, ALU.subtract)
                else:
                    DVE("tensor_tensor", [b_rtmp], [], dstv[:, :, :, 0:32], t1, t2, ALU.subtract, pwrites=[b_rq])
                DVE("tensor_tensor", rb + [b_rtmp], [b_rtmp], t1, x2, cos_b, ALU.mult)
                DVE("tensor_tensor", rb + [b_rtmp], [b_rtmp], t2, x1, sin_b, ALU.mult)
                DVE("tensor_tensor", [b_rtmp], [], dstv[:, :, :, 32:64], t1, t2, ALU.add, pwrites=[b_rq])

        def stage_t1d(i):
            k = i % 2
            rq = rqs[k]
            b_rq = b_rqs[k]
            pbv7 = PB[7].bitcast(BF16)
            for bl in range(8):
                TR(pbv7[:, bl * 128:(bl + 1) * 128], rq[:, bl * 128:(bl + 1) * 128], ident_b, [b_rq, b_const],
                   PBb[7], bl == 0)
            for s_ in range(2):
                for g in range(2):
                    srcv = pbv7[g * 64:(g + 1) * 64, s_ * 512:(s_ + 1) * 512].rearrange("p (b t) -> p b t", b=4)
                    dstv = qTz[g * 64:(g + 1) * 64, 2 * s_ + g, :, :]
                    if g == 0:
                        ACT(dstv, srcv, AF.Copy, [PBb[7]], [], pwrites=[b_qTz])
                    else:
                        DVE("tensor_copy", [PBb[7]], [], dstv, srcv, pwrites=[b_qTz])
            DMA("pool", qscr[i], qTz.rearrange("p a b t -> p (a b t)"), reads=[b_qTz], pwrites=[b_qscr], dbuf=b_qscr)


        qins = [qin, A.alloc([16, 64], F32)]
        b_qins = [b_qin, Buf()]
        rqs = [rq, A.alloc([1024], BF16)]
        b_rqs = [b_rq, Buf()]
        for i_ in range(min(3, nown)):
            stage_n1d(i_)
        stage_a1d(0)
        if nown > 1:
            stage_a1d(1)
        stage_r1d(0)
        for i in range(nown):
            if i + 3 < nown:
                stage_n1d(i + 3)
            if i + 2 < nown:
                stage_a1d(i + 2)
            if i + 1 < nown:
                stage_r1d(i + 1)
            stage_t1d(i)
        S.barrier()
        A.off = m_res

        if stop_after == "1d":
            b_out = Buf("out")
            tmpb = A.alloc([1024], BF16)
            tmpf = A.alloc([1024], F32)
            bt_, bt2 = Buf(), Buf()
            DMA("sp", tmpb, qscr[0][:, 0:1024], reads=[b_qscr], writes=[bt_], dbuf=bt_)
            DVE("tensor_copy", [bt_], [bt2], tmpf, tmpb)
            DMA("sp", out_d[0:128, :], tmpf, reads=[bt2], pwrites=[b_out], dbuf=b_out)
            S.op("sp", lambda e: e.nop(), reads=[b_out])
            S.emit()
            return nc, S, A

        KE1 = A.alloc([NT * 128], BF16)
        KE = [KT_slc, KE1]
        DVE("tensor_copy", [b_KV], [], KE1[64:128, 0:NT * 64], KT_slc[64:128, 0:NT * 64], pwrites=[b_KV])
        ACT(KE1[64:128, NT * 64:NT * 128], KT_slc[64:128, NT * 64:NT * 128], AF.Copy, [b_KV], [], pwrites=[b_KV])
        Wa = A.alloc([4, D], BF16)
        Wb = A.alloc([4, D], BF16)
        Wo = A.alloc([8, D], BF16)
        b_E = Buf("E")
        causal_m = A.alloc([128], BF16)
        strict_m = A.alloc([128], BF16)
        causal_rep = causal_m.unsqueeze(1).broadcast_to([128, 4, 128])
        strict_rep = strict_m.unsqueeze(1).broadcast_to([128, 4, 128])
        sinkexp = A.alloc([8], F32)
        b_masks = Buf("masks")
        b_mrep = Buf("mrep")
        DMA("pool", causal_m, cmask_d[:, 0, :], pwrites=[b_mrep], dbuf=b_mrep)
        DMA("pool", strict_m, cmask_d[:, 1, :], pwrites=[b_mrep], dbuf=b_mrep)
        DMA("sp", sinkexp, sinks[0:1, :].broadcast_to([128, 8]), writes=[b_masks], dbuf=b_masks)
        ACT(sinkexp, sinkexp, AF.Exp, [b_masks], [], pwrites=[b_mrep])
        qT = [A.alloc([4, 512], BF16) for _ in range(2)]
        ngs = [A.alloc([24], F32) for _ in range(2)]
        gab0 = A.alloc([2048], BF16)
        hb20 = A.alloc([D], F32)
        gab = [gab0, gab0]
        hb2 = [hb20, hb20]
        cmT_m = [A.alloc([128], BF16) for _ in range(2)]
        cmT_rep = [m_.unsqueeze(1).broadcast_to([128, 4, 128]) for m_ in cmT_m]
        QS = [[[A.alloc([512], BF16) for _ in range(2)] for _ in range(2)] for _ in range(2)]
        b_QS = [[[Buf() for _ in range(2)] for _ in range(2)] for _ in range(2)]
        selb_sw = A.alloc([128], F32)
        oab_f = [A.alloc([1024], F32) for _ in range(2)]
        b_qT = [Buf() for _ in range(2)]
        b_ngsb = [Buf() for _ in range(2)]
        b_gabb0, b_hb20 = Buf(), Buf()
        b_gabb = [b_gabb0, b_gabb0]
        b_hb2 = [b_hb20, b_hb20]
        b_cmT = [Buf() for _ in range(2)]
        b_oabf = [Buf() for _ in range(2)]
        cmpm = A.alloc([512], F32)
        keepm = A.alloc([128], F32)
        addm = A.alloc([128], F32)
        e_sb = A.alloc([512], F32)
        em = [A.alloc([512], F32) for _ in range(4)]
        P4 = A.alloc([512], F32)
        imp = A.alloc([128], F32)
        imp2 = A.alloc([128], F32)
        tmpk = A.alloc([128], F32)
        m8a = A.alloc([8], F32)
        m8b = A.alloc([8], F32)
        rs = A.alloc([4], F32)
        rinv = A.alloc([4], F32)
        selb = A.alloc([128], F32)
        otmp = A.alloc([4, 64], F32)
        fac4 = A.alloc([4], F32)
        b_tile, b_e, b_P4, b_imp, b_sel, b_rs = Buf(), Buf(), Buf(), Buf(), Buf(), Buf()
        b_em = [Buf() for _ in range(4)]
        b_otmp, b_fac = Buf(), Buf()
        m1, m2 = em[0], em[1]
        b_m1, b_m2 = b_em[0], b_em[1]
        oT = em[2].bitcast(BF16).rearrange("p (b t) -> p b t", b=8)
        mT = em[3].bitcast(BF16).rearrange("p (b t) -> p b t", b=8)
        b_oT, b_mT = b_em[2], b_em[3]
        oab = e_sb.bitcast(BF16)
        merged = P4.bitcast(BF16)
        b_oab, b_mg = b_e, b_P4
        b_h2scr = Buf("h2scr")
        acc_cnt = [0]
        oT_cnt = [0]
        oTs = [A.alloc([512], F32) for _ in range(2)]
        b_oTs = [Buf() for _ in range(2)]
        pT2 = [A.alloc([1024], BF16) for _ in range(2)]
        b_pT2 = [[Buf(), Buf()] for _ in range(2)]

        def next_ob():
            ob = 4 + acc_cnt[0] % 2
            acc_cnt[0] += 1
            return ob

        def combine(par, ob, g, gate_ap, sink_ap, dst, first):
            v = PB[ob][:, 0:260].rearrange("p (r e) -> p r e", e=65)
            den = v[:, :, 64]
            num = v[:, :, 0:64]
            if sink_ap is not None:
                DVE("tensor_tensor", [PBb[ob], b_mrep], [b_fac], fac4, den, sink_ap, ALU.add)
            else:
                DVE("tensor_scalar", [PBb[ob]], [b_fac], fac4, den, 1e-30, None, ALU.max)
            DVE("reciprocal", [b_fac], [b_fac], fac4, fac4)
            if gate_ap is not None:
                DVE("tensor_tensor", [b_fac, b_ngsb[par]], [b_fac], fac4, fac4, gate_ap, ALU.mult)
            tgt = dst[:, g * 256:(g + 1) * 256].rearrange("p (r d) -> p r d", r=4)
            fbc = fac4.unsqueeze(2).broadcast_to([128, 4, 64])
            if first:
                DVE("tensor_tensor", [PBb[ob], b_fac], [], tgt, num, fbc, ALU.mult, pwrites=[b_oabf[par]])
            else:
                DVE("tensor_tensor", [PBb[ob], b_fac], [b_otmp], otmp, num, fbc, ALU.mult)
                DVE("tensor_tensor", [b_otmp, b_oabf[par]], [], tgt, tgt, otmp, ALU.add, pwrites=[b_oabf[par]])

        def run_groups(par, groups, side):
            units = []
            for (g, qblk0, items, gate_ap, sink_ap, dst, first) in groups:
                ob = next_ob()
                n = len(items)
                for idx, it in enumerate(items):
                    units.append((g, qblk0, ob, idx, n, it, (gate_ap, sink_ap, dst, first)))
            pending = []
            LOOK = 3

            def stage_a(j, u):
                g, qblk0, ob, idx, n, (KT_ap, V_ap, bias_ap, sel, mask), _ = u
                bk = j % 4
                pv = pT2[bk // 2][:, (bk % 2) * 512:(bk % 2 + 1) * 512]
                pb_ = b_pT2[bk // 2][bk % 2]
                if sel is None:
                    rhs = qT[par][:, (qblk0 // 4) * 2 + g, :]
                    MM(PB[bk], KT_ap, rhs, True, True, [b_KV, b_qT[par]], PBb[bk])
                else:
                    w_ = sel // 32
                    MM(PB[bk], KT_ap, QS[par][g][w_], True, True, [b_KV, b_QS[par][g][w_]], PBb[bk])
                if bias_ap is None:
                    ACT(pv, PB[bk], AF.Exp, [PBb[bk]], [pb_], scale=0.125)
                else:
                    ACT(pv, PB[bk], AF.Exp, [PBb[bk], b_tab], [pb_], bias=bias_ap, scale=0.125)
                if mask is not None:
                    pv3 = pv.rearrange("p (b t) -> p b t", b=4)
                    DVE("tensor_tensor", [pb_, b_mrep, b_cmT[par]], [pb_], pv3, pv3, mask, ALU.mult)

            def stage_b(j, u):
                g, qblk0, ob, idx, n, (KT_ap, V_ap, bias_ap, sel, mask), (gate_ap, sink_ap, dst, first) = u
                bk = j % 4
                pv = pT2[bk // 2][:, (bk % 2) * 512:(bk % 2 + 1) * 512]
                pb_ = b_pT2[bk // 2][bk % 2]
                MM(PB[ob][0:65, :], V_ap, pv, idx == 0, idx == n - 1, [pb_, b_KV], PBb[ob])
                if idx == n - 1:
                    k2 = oT_cnt[0] % 2
                    oT_cnt[0] += 1
                    DVE("tensor_copy", [PBb[ob]], [b_oTs[k2]], oTs[k2][0:65, :], PB[ob][0:65, :])
                    pending.append((j + 3, k2, g, gate_ap, sink_ap, dst, first))

            def flush(j, force=False):
                while pending and (force or pending[0][0] <= j):
                    _, k2, g, gate_ap, sink_ap, dst, first = pending.pop(0)
                    for r in range(4):
                        TR(PB[7][:, r * 65:(r + 1) * 65], oTs[k2][0:65, r * 128:(r + 1) * 128], ident_f[0:65, 0:65],
                           [b_oTs[k2], b_const], PBb[7], r == 0)
                    combine(par, 7, g, gate_ap, sink_ap, dst, first)

            nu = len(units)
            njobs = len(side)
            stride = max(1, (nu + LOOK) // max(1, njobs))
            burst = max(1, -(-njobs // (nu + LOOK)))
            for j in range(nu + LOOK):
                if j < nu:
                    stage_a(j, units[j])
                if j - LOOK >= 0:
                    stage_b(j - LOOK, units[j - LOOK])
                    flush(j - LOOK)
                if j % stride == stride - 1:
                    for _ in range(burst):
                        if side:
                            job = side.pop(0)
                            if job is not None:
                                job()
            flush(0, force=True)
            while side:
                job = side.pop(0)
                if job is not None:
                    job()

        def prologue_jobs(i):
            par = i % 2
            tau = 4 * i + 3
            ncol = 32 * i + 32
            nblk = ncol // 4
            jobs = []

            def loads():
                DMA("sp", qT[par].rearrange("p a t -> p (a t)"), qscr[i], reads=[b_qscr], writes=[b_qT[par]], dbuf=b_qT[par])
                DMA("sp", ngs[par], ngscr[i], reads=[b_ngs], writes=[b_ngsb[par]], dbuf=b_ngsb[par])
                DMA("pool", cmT_m[par], cmpmT_d[i], writes=[b_cmT[par]], dbuf=b_cmT[par])
                for w_ in range(2):
                    DMA("sp", QS[par][0][w_][0:64, :], qscr[i][0:64, 0:512], reads=[b_qscr], pwrites=[b_QS[par][0][w_]],
                        dbuf=b_QS[par][0][w_])
                    DMA("sp", QS[par][1][w_][64:128, :], qscr[i][64:128, 512:1024], reads=[b_qscr],
                        pwrites=[b_QS[par][1][w_]], dbuf=b_QS[par][1][w_])
                DMA("sp", cmpm, cmpm_d[i], writes=[b_tile], dbuf=b_tile)
                DMA("sp", keepm, keep_d[i], pwrites=[b_tile], dbuf=b_tile)
                DMA("sp", addm, addm_d[i], pwrites=[b_tile], dbuf=b_tile)
            jobs.append(loads)

            def score(g, r):
                def f():
                    bk = 6 if r % 2 == 0 else 7
                    MM(PB[bk][:, 0:ncol], qT[par][:, g, r * 128:(r + 1) * 128], KcT[:, 0:ncol], True, True,
                       [b_qT[par], b_KV], PBb[bk])
                    ACT(e_sb[:, 0:ncol], PB[bk][:, 0:ncol], AF.Exp, [PBb[bk]], [b_e], scale=0.125)
                    DVE("scalar_tensor_tensor", [b_e, b_tile], [b_em[r]], em[r][:, 0:ncol], e_sb[:, 0:ncol], 1.0,
                        cmpm[:, 0:ncol], ALU.mult, ALU.mult, pwrites=[b_rs], accum_out=rs[:, r:r + 1])
                return f

            def chain(g):
                P4v = P4[:, 0:ncol].rearrange("p (b f) -> p b f", f=4)

                def fa():
                    DVE("tensor_scalar", [b_rs], [b_rs], rs, rs, 1e-30, None, ALU.max)
                    DVE("reciprocal", [b_rs], [b_rs], rinv, rs)
                    DVE("tensor_scalar", [b_em[0], b_rs], [b_P4], P4[:, 0:ncol], em[0][:, 0:ncol], rinv[:, 0:1], None, ALU.mult)
                    for r in range(1, 4):
                        DVE("scalar_tensor_tensor", [b_em[r], b_rs, b_P4], [b_P4], P4[:, 0:ncol], em[r][:, 0:ncol],
                            rinv[:, r:r + 1], P4[:, 0:ncol], ALU.mult, ALU.add)

                def fb():
                    DVE("memset", [], [b_imp], imp, 0.0)
                    DVE("tensor_reduce", [b_P4, b_imp], [b_imp], imp[:, 0:nblk], P4v, AX.X, ALU.add)
                    DVE("scalar_tensor_tensor", [b_P4, b_imp], [b_imp], imp[:, 0:nblk], P4v[:, :, 3], -0.5, imp[:, 0:nblk],
                        ALU.mult, ALU.add)
                    DVE("scalar_tensor_tensor", [b_P4, b_imp], [b_imp], imp[:, 1:nblk], P4v[:, 0:nblk - 1, 3], 0.5,
                        imp[:, 1:nblk], ALU.mult, ALU.add)
                    DVE("tensor_tensor", [b_imp, b_tile], [b_imp], imp2, imp, keepm, ALU.mult)
                    DVE("tensor_tensor", [b_imp, b_tile], [b_imp], imp2, imp2, addm, ALU.add)

                def fc():
                    DVE("max", [b_imp], [b_sel], m8a, imp2)
                    DVE("match_replace", [b_imp, b_sel], [b_sel], tmpk, m8a, imp2, -3e9)
                    DVE("max", [b_sel], [b_sel], m8b, tmpk)
                    DVE("tensor_scalar", [b_imp, b_sel], [b_sel], selb, imp2, m8b[:, 7:8], None, ALU.is_ge)
                    DVE("tensor_scalar", [b_sel], [b_sel], selb, selb, 1.0, -NEGB, ALU.subtract, ALU.mult)
                return [fa, fb, fc]

            def seltr(g):
                def f():
                    DVE("tensor_copy", [b_sel], [], selb_sw[:, 0:64], selb[:, 64:128], pwrites=[b_sel])
                    DVE("tensor_copy", [b_sel], [], selb_sw[:, 64:128], selb[:, 0:64], pwrites=[b_sel])
                    TR(PB[7][:, 0:128], selb, ident_f, [b_sel, b_const], PBb[7], True)
                    TR(PB[7][:, 128:256], selb_sw, ident_f, [b_sel, b_const], PBb[7], False)
                    rows = slice(64, 128) if g == 0 else slice(0, 64)
                    for w_ in range(2):
                        nat = (w_ == 1) if g == 0 else (w_ == 0)
                        c0 = 0 if nat else 128
                        srcv = PB[7][rows, c0:c0 + 128].unsqueeze(1).broadcast_to([64, 4, 128])
                        dstv = QS[par][g][w_][rows, :].rearrange("p (b t) -> p b t", b=4)
                        if w_ == 0:
                            ACT(dstv, srcv, AF.Copy, [PBb[7]], [], pwrites=[b_QS[par][g][w_]])
                        else:
                            DVE("tensor_copy", [PBb[7]], [], dstv, srcv, pwrites=[b_QS[par][g][w_]])
                return f

            for g in range(2):
                for r in range(4):
                    jobs.append(score(g, r))
                jobs += chain(g)
                jobs += [None] * 3
                jobs.append(seltr(g))
            return jobs

        def epilogue_jobs(i):
            par = i % 2
            tau = 4 * i + 3
            jobs = []

            def j0():
                DMA("sp", gab[par], gabscr[i], reads=[b_gab], writes=[b_gabb[par]], dbuf=b_gabb[par])
                DMA("sp", hb2[par], hscr[tau * 128:(tau + 1) * 128, :], reads=[b_hscr], writes=[b_hb2[par]], dbuf=b_hb2[par])
                DVE("tensor_copy", [b_oabf[par]], [b_oab], oab, oab_f[par])
            jobs.append(j0)
            jobs.append(lambda: transpose_to(oT, oab, b_oab, 8, 6, b_oT, evac="dve"))

            def ab(hf):
                def f():
                    for c in range(4):
                        MM(PB[6], oT[:, c, :], Wa[:, c, hf * 512:(hf + 1) * 512], c == 0, c == 3, [b_oT, b_E], PBb[6])
                    for c in range(4):
                        MM(PB[7], oT[:, 4 + c, :], Wb[:, c, hf * 512:(hf + 1) * 512], c == 0, c == 3, [b_oT, b_E], PBb[7])
                    DVE("tensor_tensor", [PBb[6], b_gabb[par]], [b_m1], m1, PB[6], gab[par][:, hf * 512:(hf + 1) * 512], ALU.mult)
                    DVE("tensor_tensor", [PBb[7], b_gabb[par]], [b_m2], m2, PB[7],
                        gab[par][:, 1024 + hf * 512:1024 + (hf + 1) * 512], ALU.mult)
                    if hf == 0:
                        DVE("tensor_tensor", [b_m1, b_m2], [b_mg], merged[:, 0:512], m1, m2, ALU.add)
                    else:
                        DVE("tensor_tensor", [b_m1, b_m2], [], merged[:, 512:1024], m1, m2, ALU.add, pwrites=[b_mg])
                return f
            jobs.append(None)
            jobs.append(ab(0))
            jobs.append(ab(1))
            jobs += [None] * 2
            jobs.append(lambda: transpose_to(mT, merged, b_mg, 8, 6, b_mT, evac="dve"))
            jobs.append(None)

            def wo(hf):
                def f():
                    bk = 6 + hf
                    for kc in range(8):
                        MM(PB[bk], mT[:, kc, :], Wo[:, kc, hf * 512:(hf + 1) * 512], kc == 0, kc == 7, [b_mT, b_E], PBb[bk])
                    DVE("tensor_tensor", [PBb[bk], b_hb2[par]], [], hb2[par][:, hf * 512:(hf + 1) * 512], PB[bk],
                        hb2[par][:, hf * 512:(hf + 1) * 512], ALU.add, pwrites=[b_hb2[par]])
                    if hf == 1:
                        DMA("pool", h2scr[i * 128:(i + 1) * 128, :], hb2[par], reads=[b_hb2[par]], pwrites=[b_h2scr],
                            dbuf=b_h2scr)
                return f
            jobs.append(wo(0))
            jobs.append(wo(1))
            return jobs

        def tile_groups(i):
            par = i % 2
            tau = 4 * i + 3
            nch = (8 * tau - 1) // 128 + 1
            of = oab_f[par]
            groups = []
            for g in range(2):
                items = [(KcT[:, ch * 128:(ch + 1) * 128], V1c[:, ch, g, :], cvalid[:, ch:ch + 1], None,
                          cmT_rep[par] if ch == nch - 1 else None) for ch in range(nch)]
                groups.append((g, 0, items, ngs[par][:, 0 * 8 + g * 4:0 * 8 + g * 4 + 4], None, of[:, 0:512], True))
            for g in range(2):
                items = [(KT_win[:, kt * 128:(kt + 1) * 128], V1_win[:, kt, g, :], kvalid[:, kt:kt + 1] if kt < 3 else None, None,
                          causal_rep if kt == tau else (strict_rep if kt == tau - 4 else None))
                         for kt in range(max(0, tau - 4), tau + 1)]
                groups.append((g, 0, items, ngs[par][:, 2 * 8 + g * 4:2 * 8 + g * 4 + 4], None, of[:, 0:512], False))
            for g in range(2):
                items = [(KT_swa[:, kt * 128:(kt + 1) * 128], V1_swa[:, kt, :], kvalid[:, kt:kt + 1] if kt < 3 else None, None,
                          causal_rep if kt == tau else strict_rep) for kt in (tau - 1, tau)]
                groups.append((g, 4, items, None, sinkexp[:, g * 4:(g + 1) * 4], of[:, 512:1024], True))
            for g in range(2):
                items = [(KE[g][:, kt * 128:(kt + 1) * 128], V1_slc[:, kt, g, :], kvalid[:, kt:kt + 1] if kt < 3 else None,
                          kt if kt < tau else None, causal_rep if kt == tau else None) for kt in range(tau + 1)]
                groups.append((g, 0, items, ngs[par][:, 1 * 8 + g * 4:1 * 8 + g * 4 + 4], None, of[:, 0:512], False))
            return groups

        p0 = prologue_jobs(0)
        p0[0]()
        DMA("pool", KE1[0:64, :], eall_d[:, :], pwrites=[b_KV], dbuf=b_KV)
        DMA("pool", KT_slc[64:128, :], eall_d[:, :], pwrites=[b_KV], dbuf=b_KV)
        DMA("pool", Wa, w_a.rearrange("(c p) n -> p c n", p=128), pwrites=[b_E], dbuf=b_E)
        DMA("pool", Wb, w_b.rearrange("(c p) n -> p c n", p=128), pwrites=[b_E], dbuf=b_E)
        DMA("pool", Wo, w_o.rearrange("(c p) n -> p c n", p=128), pwrites=[b_E], dbuf=b_E)
        g0 = tile_groups(0)
        run_groups(0, g0[:6], [j_ for j_ in p0[1:] if j_ is not None])
        side = prologue_jobs(1) if nown > 1 else []
        run_groups(0, g0[6:], side)
        for i in range(1, nown):
            side = epilogue_jobs(i - 1)
            if i + 1 < nown:
                side += prologue_jobs(i + 1)
            run_groups(i % 2, tile_groups(i), side)
        for job in epilogue_jobs(nown - 1):
            if job is not None:
                job()
        S.barrier()
        A.off = base_mark

        b_out = Buf("out")
        gfin = A.alloc([D], F32)
        b_gfin = Buf()
        DMA("sp", gfin, g_fin[0:1, :].broadcast_to([128, D]), writes=[b_gfin], dbuf=b_gfin)
        fjunk = A.alloc([D], BF16)
        fsc = A.alloc([4], F32)
        b_fj, b_fsc = Buf(), Buf()

        def fin3(t, h_sb, hbuf):
            ACT(fjunk, h_sb, AF.Square, [hbuf], [b_fj, b_fsc], accum=fsc[:, 0:1])
            DVE("tensor_scalar", [b_fsc], [b_fsc], fsc[:, 1:2], fsc[:, 0:1], 1.0 / D, EPS, ALU.mult, ALU.add)
            ACT(fsc[:, 2:3], fsc[:, 1:2], AF.Sqrt, [b_fsc], [b_fsc])
            DVE("reciprocal", [b_fsc], [b_fsc], fsc[:, 3:4], fsc[:, 2:3])
            DVE("scalar_tensor_tensor", [hbuf, b_fsc, b_gfin], [], h_sb, h_sb, fsc[:, 3:4], gfin, ALU.mult, ALU.mult,
                pwrites=[hbuf])
            DMA("pool", out_d[t * 128:(t + 1) * 128, :], h_sb, reads=[hbuf], pwrites=[b_out], dbuf=b_out)

        b_hscr = b_h2scr
        ffn_phase(nown, h2scr, g_ffn2, w2g, w2u, w2d, fin3)
        S.op("sp", lambda e: e.nop(), reads=[b_out])
        S.emit()
        return nc, S, A


def _tables(c):
    sh = 3 - c
    t = {}
    t["ident"] = np.eye(128, dtype=np.float32)
    half = 32
    invf = (np.float32(10000.0) ** (-np.arange(half, dtype=np.float32) / np.float32(half))).astype(np.float32)
    t["invf2"] = np.concatenate([invf, invf])[None, :].astype(np.float32)
    t["phase"] = np.concatenate([np.zeros(32), np.full(32, np.pi / 2)])[None, :].astype(np.float32)
    t["eall"] = np.tile(np.repeat(np.eye(64, dtype=np.float32), 64, axis=1), (1, 2))
    k = np.arange(128)[:, None]
    q = np.arange(128)[None, :]
    t["cmask"] = np.stack([(k <= q), (k > q)], axis=1).astype(np.float32)
    tau = np.arange(NT)
    t["kvalid"] = np.broadcast_to(np.where(tau - sh >= 0, 0.0, NEGB)[None, :], (128, NT)).astype(np.float32).copy()
    jr = np.arange(512)
    gj = jr - 8 * sh
    cval = (gj >= 0) & (gj < 511)
    t["cvalid"] = np.where(cval, 0.0, NEGB).reshape(4, 128).T.astype(np.float32).copy()
    cmpmT = np.zeros((NOWN, 128, 128), np.float32)
    cmpm = np.zeros((NOWN, 128, 512), np.float32)
    keep = np.zeros((NOWN, 128, 128), np.float32)
    addm = np.zeros((NOWN, 128, 128), np.float32)
    for i in range(NOWN):
        ta = 4 * i + 3
        ch = (8 * ta - 1) // 128
        jl = np.arange(128)
        jrr = ch * 128 + jl
        cmpmT[i] = (16 * jrr[:, None] + 31 <= 128 * ta + np.arange(128)[None, :]).astype(np.float32)
        vis = (16 * jr[None, :] + 31 <= 128 * ta + np.arange(128)[:, None])
        cmpm[i] = (vis & cval[None, :]).astype(np.float32)
        blk = np.arange(128)[None, :]
        tb = 2 * ta + (np.arange(128)[:, None] >= 64)
        g = blk - 2 * sh
        invalid = g < 0
        forced = (g == 0) | (blk == tb) | (blk == tb - 1)
        future = blk > tb
        kp = np.ones((128, 128), np.float32)
        ad = np.zeros((128, 128), np.float32)
        kp[np.broadcast_to(future, kp.shape)] = 0
        ad[np.broadcast_to(future, kp.shape)] = -FORCE
        kp[forced] = 0
        ad[forced] = FORCE
        kp[np.broadcast_to(invalid, kp.shape)] = 0
        ad[np.broadcast_to(invalid, kp.shape)] = -FORCE
        keep[i] = kp
        addm[i] = ad
    t["cmpmT"] = cmpmT
    t["cmpm"] = cmpm
    t["keepm"] = keep
    t["addm"] = addm
    return t


def _core_inputs(inp, core):
    b, c = core // 4, core % 4
    sh = 3 - c
    x = np.asarray(inp["x"])
    pos = np.asarray(inp["positions"])
    m = {}
    xr = np.zeros((NT * 128, D), np.float32)
    pr = np.zeros((NT * 128,), np.int32)
    if sh > 0:
        xr[sh * 128:] = x[b, :(NT - sh) * 128]
        pr[sh * 128:] = pos[b, :(NT - sh) * 128]
    else:
        xr[:] = x[b]
        pr[:] = pos[b]
    m["xrel"] = xr
    posrel = pr.reshape(NT, 128).T
    jr = np.arange(512)
    gtok = 16 * jr + 31 - sh * 128
    ok = (16 * jr - sh * 128 >= 0) & (gtok < 8192)
    pc = np.zeros(512, np.int32)
    pc[ok] = pos[b, gtok[ok]]
    m["posall"] = np.ascontiguousarray(np.concatenate([posrel, pc.reshape(4, 128).T], axis=1))
    f = lambda k: np.ascontiguousarray(np.asarray(inp[k])[0])
    m["g_ffn1"] = f("norm_ffn1")[None, :]
    m["g_mix"] = f("norm_mix")[None, :]
    m["g_ffn2"] = f("norm_ffn2")[None, :]
    m["g_fin"] = np.ascontiguousarray(np.asarray(inp["norm_final"]))[None, :]
    m["w1g"], m["w1u"], m["w1d"] = f("ffn1_gate"), f("ffn1_up"), f("ffn1_down")
    m["w2g"], m["w2u"], m["w2d"] = f("ffn2_gate"), f("ffn2_up"), f("ffn2_down")
    m["w_in"] = f("w_in")
    m["pe_k"] = np.ascontiguousarray(f("cmp_pe_k").T)
    m["pe_v"] = np.ascontiguousarray(f("cmp_pe_v").T)
    m["ck_w1"], m["ck_w2"] = f("cmp_k_w1"), f("cmp_k_w2")
    m["cv_w1"], m["cv_w2"] = f("cmp_v_w1"), f("cmp_v_w2")
    m["sinks"] = f("swa_sinks")[None, :]
    m["w_a"], m["w_b"], m["w_o"] = f("w_branch_a"), f("w_branch_b"), f("w_out")
    m.update(_tables(c))
    return m


_PROG = {}


def kernel(**inputs):
    if "nc" not in _PROG:
        _PROG["nc"] = build_program()[0]
    nc = _PROG["nc"]
    in_maps = [_core_inputs(inputs, core) for core in range(8)]
    res = run_bass_kernel_spmd(nc, in_maps, core_ids=list(range(8)))
    out = np.zeros((2, 8192, D), np.float32)
    for core in range(8):
        b, c = core // 4, core % 4
        r = np.asarray(res.results[core]["out"]).reshape(NOWN, 128, D)
        for i in range(NOWN):
            t = 4 * i + c
            out[b, t * 128:(t + 1) * 128] = r[i]
    return out
```
